# Optimizing a Trainium2 kernel written in Bass

```python
import math
import jax
import jax.numpy as jnp
from jax import lax
import numpy as np

D_MODEL = 1024
BATCH = 4
SEQ = 4096
DEPTH = 4
DEC_BATCH = 32
DEC_SEQ = 64
PAST_LEN = 2048

CHUNK = 64
Q_BLOCK = 128
H_A = 8
DN = 64
DR = 32
DV = 64
Q_RANK = 256
KV_RANK = 128
ROPE_THETA = 10000.0
MLA_SCALE = 1.0 / math.sqrt(DN + DR)
H_B = 8
N_B = 64
D_B = H_B * N_B
W_RANK = 64
A_RANK = 64
G_RANK = 128
D_MIX = H_A * DV + D_B
D_SHIFT = 3 * D_B + W_RANK + A_RANK + G_RANK
D_IN = Q_RANK + KV_RANK + DR + D_SHIFT
N_MEM = 256
MEM_HEADS = 4
MEM_HD = D_MODEL // MEM_HEADS
D_FF = 2816
EPS = 1e-6
GN_EPS = 64e-5
NEG_INF = -1e30

kernel_name = 'hybrid_mla_rwkv7_macaron_stream_step'


def rmsnorm(x, g):
    xf = x.astype(jnp.float32)
    y = xf * lax.rsqrt(jnp.mean(xf * xf, axis=-1, keepdims=True) + EPS)
    return (y * g.astype(jnp.float32)).astype(x.dtype)


def swiglu_half(x, g, w_gate, w_up, w_down):
    h = rmsnorm(x, g)
    return x + 0.5 * ((jax.nn.silu(h @ w_gate) * (h @ w_up)) @ w_down)


def rope(x, pos):
    half = DR // 2
    inv = ROPE_THETA ** (-jnp.arange(half, dtype=jnp.float32) / half)
    ang = pos.astype(jnp.float32)[:, None] * inv[None, :]
    shp = (1, pos.shape[0]) + (1,) * (x.ndim - 3) + (half,)
    cos = jnp.cos(ang).reshape(shp)
    sin = jnp.sin(ang).reshape(shp)
    xf = x.astype(jnp.float32)
    x1, x2 = xf[..., :half], xf[..., half:]
    return jnp.concatenate([x1 * cos - x2 * sin, x1 * sin + x2 * cos], axis=-1).astype(x.dtype)


def mla_block(q_lat, q_rope, q_pos, c_kv, k_rope, k_pos):
    s = (jnp.einsum('bqhc,bkc->bhqk', q_lat, c_kv)
         + jnp.einsum('bqhr,bkr->bhqk', q_rope, k_rope)).astype(jnp.float32) * MLA_SCALE
    limit = (q_pos // CHUNK + 1) * CHUNK
    mask = k_pos[None, :] < limit[:, None]
    s = jnp.where(mask[None, None], s, NEG_INF)
    p = jax.nn.softmax(s, axis=-1).astype(c_kv.dtype)
    return jnp.einsum('bhqk,bkc->bqhc', p, c_kv)


def wkv_scan(r, w, k, v, kk, a, s0):
    def step(S, inp):
        r_t, w_t, k_t, v_t, kk_t, a_t = inp
        sa = jnp.einsum('bhvk,bhk->bhv', S, -kk_t)
        S = (S * w_t[:, :, None, :] + sa[..., None] * (kk_t * a_t)[:, :, None, :]
             + v_t[..., None] * k_t[:, :, None, :])
        return S, jnp.einsum('bhvk,bhk->bhv', S, r_t)
    tm = lambda z: jnp.moveaxis(z, 1, 0)
    S, y = lax.scan(step, s0.astype(jnp.float32), (tm(r), tm(w), tm(k), tm(v), tm(kk), tm(a)))
    return S, jnp.moveaxis(y, 0, 1)


def token_mixer(h, pos, k_pos, ckv_past, krope_past, wkv0, shift0, lp):
    f32 = jnp.float32
    b, t, _ = h.shape
    proj = h @ lp['w_in']
    c_q, c_kv, k_r, p_b = jnp.split(proj, [Q_RANK, Q_RANK + KV_RANK, Q_RANK + KV_RANK + DR], axis=-1)
    q = (rmsnorm(c_q, lp['q_norm']) @ lp['w_uq']).reshape(b, t, H_A, DN + DR)
    q_rope = rope(q[..., DN:], pos)
    q_lat = jnp.einsum('bthd,hcd->bthc', q[..., :DN], lp['w_uk'])
    c_kv = rmsnorm(c_kv, lp['kv_norm'])
    k_r = rope(k_r, pos)
    if ckv_past is None:
        ckv_all, kr_all = c_kv, k_r
    else:
        ckv_all = jnp.concatenate([ckv_past, c_kv], axis=1)
        kr_all = jnp.concatenate([krope_past, k_r], axis=1)
    if t > Q_BLOCK:
        nb = t // Q_BLOCK
        to_blocks = lambda z: jnp.moveaxis(z.reshape((b, nb, Q_BLOCK) + z.shape[2:]), 1, 0)
        o = lax.map(lambda args: mla_block(args[0], args[1], args[2], ckv_all, kr_all, k_pos),
                    (to_blocks(q_lat), to_blocks(q_rope), pos.reshape(nb, Q_BLOCK)))
        o_lat = jnp.moveaxis(o, 0, 1).reshape(b, t, H_A, KV_RANK)
    else:
        o_lat = mla_block(q_lat, q_rope, pos, ckv_all, kr_all, k_pos)
    y_a = jnp.einsum('bthc,hcv->bthv', o_lat, lp['w_uv']).reshape(b, t, H_A * DV)
    prev = jnp.concatenate([shift0, p_b[:, :-1]], axis=1)
    xs = p_b + lp['shift_mu'] * (prev - p_b)
    r, k, v, xw, xa, xg = jnp.split(
        xs, [D_B, 2 * D_B, 3 * D_B, 3 * D_B + W_RANK, 3 * D_B + W_RANK + A_RANK], axis=-1)
    z = (lp['w0'] + jnp.tanh(xw) @ lp['w_up']).astype(f32)
    decay = jnp.exp(-jnp.exp(-jax.nn.softplus(-z) - 0.5))
    a = jax.nn.sigmoid((lp['a0'] + xa @ lp['a_up']).astype(f32))
    g = jax.nn.sigmoid(xg) @ lp['g_up']
    heads = lambda u: u.astype(f32).reshape(b, t, H_B, N_B)
    r_h, k_h, v_h, a_h, w_h = heads(r), heads(k), heads(v), heads(a), heads(decay)
    kk = k_h * lp['k_k'].astype(f32).reshape(H_B, N_B)
    kk = kk * lax.rsqrt(jnp.sum(kk * kk, axis=-1, keepdims=True) + 1e-12)
    k_h = k_h * (1.0 + (a_h - 1.0) * lp['k_a'].astype(f32).reshape(H_B, N_B))
    wkv, y = wkv_scan(r_h, w_h, k_h, v_h, kk, a_h, wkv0)
    mu = jnp.mean(y, axis=-1, keepdims=True)
    var = jnp.mean(jnp.square(y - mu), axis=-1, keepdims=True)
    y = ((y - mu) * lax.rsqrt(var + GN_EPS) * lp['gn_gain'].astype(f32).reshape(H_B, N_B)
         + lp['gn_bias'].astype(f32).reshape(H_B, N_B))
    y = y + jnp.sum(r_h * k_h * lp['r_k'].astype(f32), axis=-1, keepdims=True) * v_h
    y_b = y.reshape(b, t, D_B).astype(h.dtype) * g
    out = jnp.concatenate([y_a, y_b], axis=-1) @ lp['w_out']
    return out, c_kv, k_r, wkv.astype(wkv0.dtype), p_b[:, -1:]


def mem_kv(mem, lp):
    b = mem.shape[0]
    m = rmsnorm(mem, lp['mem_kv_norm'])
    mk = (m @ lp['w_mk']).reshape(b, N_MEM, MEM_HEADS, MEM_HD)
    mv = (m @ lp['w_mv']).reshape(b, N_MEM, MEM_HEADS, MEM_HD)
    return mk, mv


def mem_attend(x, mk, mv, lp):
    b, t, _ = x.shape
    h = rmsnorm(x, lp['xattn_norm'])
    q = (h @ lp['w_mq']).reshape(b, t, MEM_HEADS, MEM_HD)
    s = jnp.einsum('bqhd,bmhd->bhqm', q, mk).astype(jnp.float32) / math.sqrt(MEM_HD)
    p = jax.nn.softmax(s, axis=-1).astype(x.dtype)
    o = jnp.einsum('bhqm,bmhd->bqhd', p, mv).reshape(b, t, D_MODEL)
    return x + o @ lp['w_mo']


def layer(x, pos, k_pos, ckv_past, krope_past, wkv0, shift0, mk, mv, lp):
    x = swiglu_half(x, lp['ffn1_norm'], lp['ffn1_w_gate'], lp['ffn1_w_up'], lp['ffn1_w_down'])
    y, c_kv, k_r, wkv, shift = token_mixer(rmsnorm(x, lp['mix_norm']), pos, k_pos,
                                           ckv_past, krope_past, wkv0, shift0, lp)
    x = x + y
    x = mem_attend(x, mk, mv, lp)
    x = swiglu_half(x, lp['ffn2_norm'], lp['ffn2_w_gate'], lp['ffn2_w_up'], lp['ffn2_w_down'])
    return x, c_kv, k_r, wkv, shift


def setup_inputs(seed: int = 0) -> dict:
    key = jax.random.key(seed)
    keys = jax.random.split(key, 64)
    ctr = [0]

    def nrm(shape, scale):
        kk = keys[ctr[0]]
        ctr[0] += 1
        return scale * jax.random.normal(kk, shape, jnp.float32)

    def gain(shape):
        return 1.0 + nrm(shape, 0.05)

    L, D = DEPTH, D_MODEL
    return {
        'x_prompt': nrm((BATCH, SEQ, D), 1.0),
        'x_sample': nrm((DEC_BATCH, DEC_SEQ, D), 1.0),
        'mem_prompt': nrm((BATCH, N_MEM, D), 1.0),
        'cache_ckv': nrm((L, DEC_BATCH, PAST_LEN, KV_RANK), 1.0),
        'cache_krope': nrm((L, DEC_BATCH, PAST_LEN, DR), 1.0),
        'cache_mem_k': nrm((L, DEC_BATCH, N_MEM, MEM_HEADS, MEM_HD), 1.0),
        'cache_mem_v': nrm((L, DEC_BATCH, N_MEM, MEM_HEADS, MEM_HD), 1.0),
        'state_wkv': nrm((L, DEC_BATCH, H_B, N_B, N_B), 0.5),
        'state_shift': nrm((L, DEC_BATCH, 1, D_SHIFT), 1.0),
        'ffn1_norm': gain((L, D)),
        'ffn1_w_gate': nrm((L, D, D_FF), D ** -0.5),
        'ffn1_w_up': nrm((L, D, D_FF), D ** -0.5),
        'ffn1_w_down': nrm((L, D_FF, D), D_FF ** -0.5),
        'mix_norm': gain((L, D)),
        'w_in': nrm((L, D, D_IN), D ** -0.5),
        'q_norm': gain((L, Q_RANK)),
        'w_uq': nrm((L, Q_RANK, H_A * (DN + DR)), Q_RANK ** -0.5),
        'kv_norm': gain((L, KV_RANK)),
        'w_uk': nrm((L, H_A, KV_RANK, DN), KV_RANK ** -0.5),
        'w_uv': nrm((L, H_A, KV_RANK, DV), KV_RANK ** -0.5),
        'shift_mu': jax.nn.sigmoid(nrm((L, D_SHIFT), 1.0)),
        'w0': -2.0 + nrm((L, D_B), 0.5),
        'w_up': nrm((L, W_RANK, D_B), 0.5 * W_RANK ** -0.5),
        'a0': nrm((L, D_B), 0.5),
        'a_up': nrm((L, A_RANK, D_B), A_RANK ** -0.5),
        'g_up': nrm((L, G_RANK, D_B), G_RANK ** -0.5),
        'k_k': 0.85 + nrm((L, D_B), 0.05),
        'k_a': 1.0 + nrm((L, D_B), 0.05),
        'r_k': nrm((L, H_B, N_B), 0.1),
        'gn_gain': gain((L, D_B)),
        'gn_bias': nrm((L, D_B), 0.02),
        'w_out': nrm((L, D_MIX, D), D_MIX ** -0.5),
        'xattn_norm': gain((L, D)),
        'mem_kv_norm': gain((L, D)),
        'w_mq': nrm((L, D, D), D ** -0.5),
        'w_mk': nrm((L, D, D), D ** -0.5),
        'w_mv': nrm((L, D, D), D ** -0.5),
        'w_mo': nrm((L, D, D), D ** -0.5),
        'ffn2_norm': gain((L, D)),
        'ffn2_w_gate': nrm((L, D, D_FF), D ** -0.5),
        'ffn2_w_up': nrm((L, D, D_FF), D ** -0.5),
        'ffn2_w_down': nrm((L, D_FF, D), D_FF ** -0.5),
        'final_norm': gain((D,)),
    }


def reference(x_prompt, x_sample, mem_prompt, cache_ckv, cache_krope, cache_mem_k, cache_mem_v,
              state_wkv, state_shift, ffn1_norm, ffn1_w_gate, ffn1_w_up, ffn1_w_down, mix_norm, w_in,
              q_norm, w_uq, kv_norm, w_uk, w_uv, shift_mu, w0, w_up, a0, a_up, g_up, k_k, k_a, r_k,
              gn_gain, gn_bias, w_out, xattn_norm, mem_kv_norm, w_mq, w_mk, w_mv, w_mo,
              ffn2_norm, ffn2_w_gate, ffn2_w_up, ffn2_w_down, final_norm):
    b_p = x_prompt.shape[0]
    t_s = x_sample.shape[1]
    pos_p = jnp.arange(x_prompt.shape[1], dtype=jnp.int32)
    pos_s = PAST_LEN + jnp.arange(t_s, dtype=jnp.int32)
    kpos_s = jnp.arange(PAST_LEN + t_s, dtype=jnp.int32)
    wkv_zero = jnp.zeros((b_p, H_B, N_B, N_B), x_prompt.dtype)
    shift_zero = jnp.zeros((b_p, 1, D_SHIFT), x_prompt.dtype)
    xp, xs = x_prompt, x_sample
    ckv_p, kr_p, mk_p, mv_p, wkv_p, sh_p = [], [], [], [], [], []
    ckv_s, kr_s, wkv_s, sh_s = [], [], [], []
    for l in range(DEPTH):
        lp = {
            'ffn1_norm': ffn1_norm[l], 'ffn1_w_gate': ffn1_w_gate[l], 'ffn1_w_up': ffn1_w_up[l],
            'ffn1_w_down': ffn1_w_down[l], 'mix_norm': mix_norm[l], 'w_in': w_in[l],
            'q_norm': q_norm[l], 'w_uq': w_uq[l], 'kv_norm': kv_norm[l], 'w_uk': w_uk[l],
            'w_uv': w_uv[l], 'shift_mu': shift_mu[l], 'w0': w0[l], 'w_up': w_up[l], 'a0': a0[l],
            'a_up': a_up[l], 'g_up': g_up[l], 'k_k': k_k[l], 'k_a': k_a[l], 'r_k': r_k[l],
            'gn_gain': gn_gain[l], 'gn_bias': gn_bias[l], 'w_out': w_out[l],
            'xattn_norm': xattn_norm[l], 'mem_kv_norm': mem_kv_norm[l], 'w_mq': w_mq[l],
            'w_mk': w_mk[l], 'w_mv': w_mv[l], 'w_mo': w_mo[l], 'ffn2_norm': ffn2_norm[l],
            'ffn2_w_gate': ffn2_w_gate[l], 'ffn2_w_up': ffn2_w_up[l], 'ffn2_w_down': ffn2_w_down[l],
        }
        mk, mv = mem_kv(mem_prompt, lp)
        xp, c1, k1, s1, h1 = layer(xp, pos_p, pos_p, None, None, wkv_zero, shift_zero, mk, mv, lp)
        ckv_p.append(c1); kr_p.append(k1); mk_p.append(mk); mv_p.append(mv)
        wkv_p.append(s1); sh_p.append(h1)
        xs, c2, k2, s2, h2 = layer(xs, pos_s, kpos_s, cache_ckv[l], cache_krope[l], state_wkv[l],
                                   state_shift[l], cache_mem_k[l], cache_mem_v[l], lp)
        ckv_s.append(c2); kr_s.append(k2); wkv_s.append(s2); sh_s.append(h2)
    y_prompt = rmsnorm(xp, final_norm)
    y_sample = rmsnorm(xs, final_norm)
    return (y_prompt, y_sample,
            jnp.stack(ckv_p), jnp.stack(kr_p), jnp.stack(mk_p), jnp.stack(mv_p),
            jnp.stack(wkv_p), jnp.stack(sh_p),
            jnp.stack(ckv_s), jnp.stack(kr_s), jnp.stack(wkv_s), jnp.stack(sh_s))
```

```python
import math
from contextlib import ExitStack
import numpy as np
import concourse.bass as bass
import concourse.mybir as mybir
from concourse.bass_utils import run_bass_kernel_spmd

F32 = mybir.dt.float32
BF16 = mybir.dt.bfloat16
AF = mybir.ActivationFunctionType
ALU = mybir.AluOpType
AX = mybir.AxisListType

D = 1024
DFF = 2816
H_A = 8
DN = 64
DR = 32
DV = 64
Q_RANK = 256
KV_RANK = 128
H_B = 8
N_B = 64
D_B = 512
D_SHIFT = 1792
N_MEM = 256
MEM_HEADS = 4
MEM_HD = 256
CH = 64
EPS = 1e-6
GN_EPS = 64e-5
MLA_SCALE = 1.0 / math.sqrt(DN + DR)
DECAY_C = math.exp(-0.5)


class Cfg:
    def __init__(self, depth=4, seq=4096, nss=4, past=2048, debug=False):
        self.depth, self.seq, self.nss, self.past = depth, seq, nss, past
        self.debug = debug
        self.zero_yb = False
        self.skip = set()
        self.xstop = 9
        self.rstop = 9
        self.ts = 64
        self.tt = seq + nss * 64


class Trk:
    __slots__ = ("w", "rs", "excl")

    def __init__(self, excl=False):
        self.w = None
        self.rs = []
        self.excl = excl


class Buf:
    def __init__(self, t, n=0, excl=False):
        self.t = t
        self.trk = Trk(excl)
        self.subs = [Trk(excl) for _ in range(n)]
        self.dsem = None
        self.dcnt = 0

    def __getitem__(self, k):
        return self.t[k]


class KB:
    ENGS = ("pe", "act", "dve", "pool", "sp")

    def __init__(self, nc, es):
        self.nc, self.es = nc, es
        self.eng = {"pe": nc.tensor, "act": nc.scalar, "dve": nc.vector, "pool": nc.gpsimd, "sp": nc.sync}
        self.sem = {e: es.enter_context(nc.semaphore("sem_" + e)) for e in self.ENGS}
        self.cnt = {e: 0 for e in self.ENGS}
        self.seen = {e: {} for e in self.ENGS}
        self.semname = {}
        self.dma_pending = {}
        self.free_dsems = {}
        self.nbuf = 0

    def sb(self, shape, dt, n=0, es=None):
        self.nbuf += 1
        t = (es or self.es).enter_context(self.nc.sbuf_tensor("sb%d" % self.nbuf, list(shape), dt))
        return Buf(t, n)

    def psb(self, shape, dt, n=0, es=None):
        self.nbuf += 1
        t = (es or self.es).enter_context(self.nc.psum_tensor("ps%d" % self.nbuf, list(shape), dt))
        return Buf(t, n, excl=True)

    def _dsem(self, buf, kind):
        if buf.dsem is None:
            buf.dsem = {}
            buf.dcnt = {}
        if kind not in buf.dsem:
            fl = self.free_dsems.setdefault(kind, [])
            if fl:
                buf.dsem[kind], buf.dcnt[kind] = fl.pop()
            else:
                self.nbuf += 1
                buf.dsem[kind] = self.es.enter_context(self.nc.semaphore("dsem%d" % self.nbuf))
                buf.dcnt[kind] = 0
        return buf.dsem[kind]

    def release_dsem(self, buf):
        if buf.dsem is not None:
            for kind in buf.dsem:
                self.free_dsems.setdefault(kind, []).append((buf.dsem[kind], buf.dcnt[kind]))
            buf.dsem = None

    def _wait(self, e, ev):
        if ev is None:
            return
        sem, val = ev
        k = id(sem)
        if self.seen[e].get(k, 0) >= val:
            return
        self.seen[e][k] = val
        self.eng[e].wait_ge(sem, val)

    def _deps(self, e, reads, writes):
        need = {}

        def add(ev):
            if ev is None or (e == "pe" and ev[0] is self.sem["pe"]):
                return
            k_ = id(ev[0])
            if k_ not in need or need[k_][1] < ev[1]:
                need[k_] = ev
        for tr in reads:
            add(tr.w)
        for tr in writes:
            add(tr.w)
            for r in tr.rs:
                add(r)
        for ev in need.values():
            self._wait(e, ev)

    def _commit(self, ev, reads, writes):
        for tr in reads:
            tr.rs.append(ev)
            if len(tr.rs) > 24:
                best = {}
                for s, v in tr.rs:
                    if id(s) not in best or best[id(s)][1] < v:
                        best[id(s)] = (s, v)
                tr.rs = list(best.values())
        for tr in writes:
            tr.w = ev
            tr.rs = []

    @staticmethod
    def _trks(lst):
        out = []
        for b in lst:
            out.append(b.trk if isinstance(b, Buf) else b)
        return out

    def op(self, e, fn, reads=(), writes=()):
        reads, writes = self._trks(reads), self._trks(writes)
        writes = writes + [t for t in reads if t.excl]
        reads = [t for t in reads if not t.excl]
        self._deps(e, reads, writes)
        inst = fn(self.eng[e])
        self.cnt[e] += 1
        inst.then_inc(self.sem[e], 1)
        self._commit((self.sem[e], self.cnt[e]), reads, writes)
        return inst

    def dma(self, e, out, in_, sbuf_buf, reads=(), writes=(), **kw):
        reads, writes = self._trks(reads), self._trks(writes)
        self._deps(e, reads, writes)
        kind = "sw" if e == "pool" else "hw"
        sem = self._dsem(sbuf_buf, kind)
        inst = self.eng[e].dma_start(out=out, in_=in_, **kw)
        sbuf_buf.dcnt[kind] += 16
        inst.then_inc(sem, 16)
        ev = (sem, sbuf_buf.dcnt[kind])
        self.dma_pending[id(sem)] = ev
        self._commit(ev, reads, writes)

    def barrier(self):
        sp = "sp"
        for ev in self.dma_pending.values():
            self._wait(sp, ev)
        self.dma_pending = {}
        for e in self.ENGS:
            if e != sp and self.cnt[e] > 0:
                self._wait(sp, (self.sem[e], self.cnt[e]))
        self.eng[sp].sem_inc(self.sem[sp], 1)
        self.cnt[sp] += 1
        for e in self.ENGS:
            if e != sp:
                self._wait(e, (self.sem[sp], self.cnt[sp]))


def col_groups(w, g=512):
    return [(s, min(g, w - s)) for s in range(0, w, g)]


class Prog:
    def __init__(self, cfg):
        self.cfg = cfg
        self.nc = bass.Bass("TRN2", target_bir_lowering=False)
        self.inputs = {}
        self.outputs = {}

    def din(self, name, shape, dt=F32):
        ap = self.nc.dram_tensor(name, list(shape), dt, kind="ExternalInput").ap()
        self.inputs[name] = ap
        return ap

    def dout(self, name, shape, dt=F32):
        ap = self.nc.dram_tensor(name, list(shape), dt, kind="ExternalOutput").ap()
        self.outputs[name] = ap
        return ap

    def dscr(self, name, shape, dt):
        return self.nc.dram_tensor(name, list(shape), dt, kind="Internal").ap()

    def build(self):
        cfg = self.cfg
        nc = self.nc
        L, SEQ, NSS, PAST, TT = cfg.depth, cfg.seq, cfg.nss, cfg.past, cfg.tt
        I = self.din
        self.x_p = I("x_p", [SEQ, D])
        self.x_s = I("x_s", [NSS * 64, D])
        self.consts = I("consts", [128, 512])
        self.pvec = I("pvec", [L, 128, 128])
        self.fin_g = I("fin_g", [128, 8])
        self.w_gate = [I("w_gate%d" % f, [L, D, DFF]) for f in (1, 2)]
        self.w_up = [I("w_up%d" % f, [L, D, DFF]) for f in (1, 2)]
        self.w_down = [I("w_down%d" % f, [L, DFF, D]) for f in (1, 2)]
        self.w_cq = I("w_cq", [L, D, 256])
        self.w_ckv = I("w_ckv", [L, D, 128])
        self.w_kr4 = I("w_kr4", [L, D, 128])
        self.w_pb = I("w_pb", [L, D, D_SHIFT])
        self.w_uq = I("w_uq", [L, 256, 768])
        self.w_ukT = I("w_ukT", [L, 64, 8, 128])
        self.w_uv = I("w_uv", [L, 128, 512])
        self.rope_cs = I("rope_cs", [2, 128, TT])
        self.w_out = I("w_out", [L, D, D])
        self.w_mq = I("w_mq", [L, D, D])
        self.w_mk = I("w_mk", [L, D, D])
        self.w_mv = I("w_mv", [L, D, D])
        self.w_mo = I("w_mo", [L, D, D])
        self.w_wup = I("w_wup", [L, 64, 512])
        self.w_aup = I("w_aup", [L, 64, 512])
        self.w_gup = I("w_gup", [L, 128, 512])
        self.st_wkv = I("st_wkv", [L, NSS, 8, 64, 64])
        self.st_sh = I("st_sh", [L, NSS, 128, 14])
        self.mem_p = I("mem_p", [N_MEM, D])
        self.c_mk = I("c_mk", [L, NSS, N_MEM, D])
        self.c_mv = I("c_mv", [L, NSS, N_MEM, D])
        self.c_ckv = I("c_ckv", [L, NSS, PAST, 128])
        self.c_kr = I("c_kr", [L, NSS, PAST, 32])
        O = self.dout
        self.y_p = O("y_p", [SEQ, D])
        self.y_s = O("y_s", [NSS * 64, D])
        self.o_ckv_p = O("o_ckv_p", [L, SEQ, 128])
        self.o_kr_p = O("o_kr_p", [L, SEQ, 32])
        self.o_mk = O("o_mk", [L, N_MEM, D])
        self.o_mv = O("o_mv", [L, N_MEM, D])
        self.o_wkv_p = O("o_wkv_p", [L, 8, 64, 64])
        self.o_sh_p = O("o_sh_p", [L, 1, D_SHIFT])
        self.o_wkv_s = O("o_wkv_s", [L, NSS, 8, 64, 64])
        self.o_sh_s = O("o_sh_s", [L, NSS, 1, D_SHIFT])
        self.o_ckv_s = O("o_ckv_s", [L, NSS * 64, 128])
        self.o_kr_s = O("o_kr_s", [L, NSS * 64, 32])
        self.xT = self.dscr("xT", [D, TT], F32)
        self.qlatT = self.dscr("qlatT", [8 * 128, TT], BF16)
        self.qropeT = self.dscr("qropeT", [2, 128, TT], BF16)
        self.ckvT = self.dscr("ckvT", [128, TT], BF16)
        self.kropeT = self.dscr("kropeT", [128, TT], BF16)
        self.pbT = self.dscr("pbT", [D_SHIFT, TT], F32)
        if cfg.debug:
            self.ymixT = O("ymixT", [D, TT], BF16)
        else:
            self.ymixT = self.dscr("ymixT", [D, TT], BF16)

        with ExitStack() as es:
            k = self.k = KB(nc, es)
            self.ident = k.sb([128, 128], F32)
            self.onesb = k.sb([128, 128], BF16)
            self.ones1 = k.sb([128, 128], F32)
            k.dma("sp", self.ident[:], self.consts[:, 0:128], self.ident, writes=[self.ident])
            k.op("dve", lambda e: e.memset(self.onesb[:], 1.0 / 1024), writes=[self.onesb])
            k.op("dve", lambda e: e.memset(self.ones1[:], 1.0), writes=[self.ones1])
            self.cmask = k.sb([128, 384], F32)
            k.dma("sp", self.cmask[:], self.consts[:, 128:512], self.cmask, writes=[self.cmask])
            self.fing = k.sb([128, 8], F32)
            k.dma("sp", self.fing[:], self.fin_g, self.fing, writes=[self.fing])
            self.pv = k.sb([128, L, 128], F32)
            k.dma("sp", self.pv[:], self.pvec.rearrange("l p c -> p l c"), self.pv, writes=[self.pv])
            self.ps = [k.psb([128, 512], F32) for _ in range(8)]
            k.barrier()

            self.phase_in()
            if cfg.zero_yb:
                with ExitStack() as es0:
                    z = k.sb([128, 4, TT], BF16, es=es0)
                    k.op("dve", lambda e: e.memset(z[:], 0.0), writes=[z])
                    k.dma("sp", self.ymixT[512:1024, :].rearrange("(c p) t -> p c t", p=128), z[:], z, reads=[z])
                    k.barrier()
                    k.release_dsem(z)
            for l in range(L):
                self.phase_ffn(l, 0)
                self.phase_proj(l)
                self.phase_mla(l)
                if "rwkv" not in cfg.skip:
                    self.phase_rwkv(l)
                if "xattn" not in cfg.skip:
                    self.phase_xattn(l)
                self.phase_ffn(l, 1)
            self.phase_out()
            k.barrier()
        return nc

    def warm(self, buf, lhsT, rhs, n=28, bank=7):
        k = self.k
        pb = self.ps[bank]
        for i in range(n):
            k.op("pe", lambda e: e.matmul(pb[:, :], lhsT, rhs, start=True, stop=True), reads=[buf], writes=[pb])

    def tiles(self, w=1024):
        cfg = self.cfg
        out = []
        for s in range(0, cfg.seq, w):
            out.append((s, min(w, cfg.seq - s)))
        out.append((cfg.seq, cfg.nss * 64))
        return out

    def phase_in(self):
        k, cfg = self.k, self.cfg
        with ExitStack() as es:
            NB = 2
            tin = [k.sb([128, D], F32, es=es) for _ in range(NB)]
            tout = [k.sb([128, 8, 128], F32, es=es) for _ in range(NB)]
            blocks = [(self.x_p, i * 128, i * 128) for i in range(cfg.seq // 128)]
            blocks += [(self.x_s, i * 128, cfg.seq + i * 128) for i in range(cfg.nss * 64 // 128)]
            for bi, (src, r0, c0) in enumerate(blocks):
                a, b = tin[bi % NB], tout[bi % NB]
                k.dma("sp", a[:], src[r0:r0 + 128, :], a, writes=[a])
                for g in range(2):
                    pb = self.ps[(bi * 2 + g) % 8]
                    for c in range(4):
                        cc = g * 4 + c
                        k.op("pe", lambda e: e.transpose(pb[:, c * 128:(c + 1) * 128], a[:, cc * 128:(cc + 1) * 128], self.ident[:]),
                             reads=[a, self.ident], writes=[pb])
                    eng = "act" if g == 0 else "dve"
                    if eng == "act":
                        k.op("act", lambda e: e.copy(out=b[:, g * 4:(g + 1) * 4, :], in_=pb[:].rearrange("p (c t) -> p c t", c=4)), reads=[pb], writes=[b])
                    else:
                        k.op("dve", lambda e: e.tensor_copy(out=b[:, g * 4:(g + 1) * 4, :], in_=pb[:].rearrange("p (c t) -> p c t", c=4)), reads=[pb], writes=[b])
                k.dma("sp", self.xT.rearrange("(c p) t -> p c t", p=128)[:, :, c0:c0 + 128], b[:], b, reads=[b])
            k.barrier()
            for b_ in tin + tout:
                k.release_dsem(b_)

    def rmsnorm_fm(self, xt, ht, W, gcol, sq, rstd, gbuf=None):
        k = self.k
        gbuf = gbuf or self.pv
        for c in range(8):
            k.op("act", lambda e: e.activation(out=sq[:, c, :W], in_=xt[:, c, :W], func=AF.Square), reads=[xt], writes=[sq.subs[c]])
        for gi, (s0, n) in enumerate(col_groups(W)):
            pb = self.ps[gi % 2]
            for c in range(8):
                k.op("pe", lambda e: e.matmul(pb[:, :n], self.onesb[:], sq[:, c, s0:s0 + n], start=(c == 0), stop=(c == 7)),
                     reads=[self.onesb, sq.subs[c]], writes=[pb])
            k.op("act", lambda e: e.activation(out=rstd[:, s0:s0 + n], in_=pb[:, :n], func=AF.Sqrt, bias=self.epsb[:, 0:1], scale=1.0), reads=[pb, self.epsb], writes=[rstd])
        k.op("dve", lambda e: e.reciprocal(out=rstd[:, :W], in_=rstd[:, :W]), reads=[rstd], writes=[rstd])
        for c in range(8):
            k.op("dve", lambda e: e.scalar_tensor_tensor(out=ht[:, c, :W], in0=xt[:, c, :W], scalar=gcol(c), in1=rstd[:, :W], op0=ALU.mult, op1=ALU.mult),
                 reads=[xt, rstd, gbuf], writes=[ht.subs[c]])

    def phase_ffn(self, l, f):
        k, cfg = self.k, self.cfg
        WMAX = 1024
        xTv = self.xT.rearrange("(c p) t -> p c t", p=128)
        with ExitStack() as es:
            self.epsb = k.sb([128, 1], F32, es=es)
            k.op("dve", lambda e: e.memset(self.epsb[:], EPS), writes=[self.epsb])
            wd = k.sb([128, 22, D], BF16, es=es)
            wdv = self.w_down[f][l].rearrange("(c p) n -> p c n", p=128)
            for c0 in range(0, 22, 2):
                k.dma("pool", wd[:, c0:c0 + 2, :], wdv[:, c0:c0 + 2, :], wd, writes=[wd])
            xt = k.sb([128, 8, WMAX], F32, es=es)
            ht = k.sb([128, 8, WMAX], BF16, n=8, es=es)
            sq = k.sb([128, 8, WMAX], BF16, n=8, es=es)
            rstd = k.sb([128, WMAX], F32, es=es)
            at = k.sb([128, 22, WMAX], BF16, n=22, es=es)
            sg = [k.sb([128, WMAX], F32, es=es) for _ in range(2)]
            NWB = 3
            wg = [k.sb([128, 8, 256], BF16, es=es) for _ in range(NWB)]
            wu = [k.sb([128, 8, 256], BF16, es=es) for _ in range(NWB)]
            wgv = self.w_gate[f][l].rearrange("(c p) n -> p c n", p=128)
            wuv = self.w_up[f][l].rearrange("(c p) n -> p c n", p=128)
            nblk = 0
            for (t0, W) in self.tiles(WMAX):
                k.dma("sp", xt[:, :, :W], xTv[:, :, t0:t0 + W], xt, writes=[xt])
                self.rmsnorm_fm(xt, ht, W, lambda c: self.pv[:, l, f * 8 + c:f * 8 + c + 1], sq, rstd)
                cg = col_groups(W)
                for jb in range(11):
                    g_, u_ = wg[nblk % NWB], wu[nblk % NWB]
                    nblk += 1
                    k.dma("pool", g_[:], wgv[:, :, jb * 256:(jb + 1) * 256], g_, writes=[g_])
                    k.dma("pool", u_[:], wuv[:, :, jb * 256:(jb + 1) * 256], u_, writes=[u_])
                    for jj in range(2):
                        j = jb * 2 + jj
                        par = j % 2
                        pg = [self.ps[par * 4 + i] for i in range(2)]
                        pu = [self.ps[par * 4 + 2 + i] for i in range(2)]
                        for gi, (s0, n) in enumerate(cg):
                            for c in range(8):
                                k.op("pe", lambda e: e.matmul(pg[gi][:, :n], g_[:, c, jj * 128:(jj + 1) * 128], ht[:, c, s0:s0 + n], start=(c == 0), stop=(c == 7)),
                                     reads=[g_, ht.subs[c]], writes=[pg[gi]])
                            for c in range(8):
                                k.op("pe", lambda e: e.matmul(pu[gi][:, :n], u_[:, c, jj * 128:(jj + 1) * 128], ht[:, c, s0:s0 + n], start=(c == 0), stop=(c == 7)),
                                     reads=[u_, ht.subs[c]], writes=[pu[gi]])
                        s_ = sg[par]
                        for gi, (s0, n) in enumerate(cg):
                            k.op("act", lambda e: e.activation(out=s_[:, s0:s0 + n], in_=pg[gi][:, :n], func=AF.Silu), reads=[pg[gi]], writes=[s_])
                            k.op("dve", lambda e: e.tensor_tensor(out=at[:, j, s0:s0 + n], in0=s_[:, s0:s0 + n], in1=pu[gi][:, :n], op=ALU.mult),
                                 reads=[s_, pu[gi]], writes=[at.subs[j]])
                for m in range(8):
                    par = m % 2
                    pd = [self.ps[par * 4 + i] for i in range(2)]
                    for gi, (s0, n) in enumerate(cg):
                        for c in range(22):
                            k.op("pe", lambda e: e.matmul(pd[gi][:, :n], wd[:, c, m * 128:(m + 1) * 128], at[:, c, s0:s0 + n], start=(c == 0), stop=(c == 21)),
                                 reads=[wd, at.subs[c]], writes=[pd[gi]])
                        k.op("dve", lambda e: e.scalar_tensor_tensor(out=xt[:, m, s0:s0 + n], in0=pd[gi][:, :n], scalar=0.5, in1=xt[:, m, s0:s0 + n], op0=ALU.mult, op1=ALU.add),
                             reads=[pd[gi], xt], writes=[xt])
                k.dma("sp", xTv[:, :, t0:t0 + W], xt[:, :, :W], xt, reads=[xt])
            k.barrier()
            for b_ in [wd, xt] + wg + wu:
                k.release_dsem(b_)


    def small_norm(self, src, dst, nch, W, ones_ap, gcol, sq, rstd, pbank, gbuf=None):
        k = self.k
        gbuf = gbuf or self.pv
        for c in range(nch):
            k.op("act", lambda e: e.activation(out=sq[:, c, :W], in_=src[:, c, :W], func=AF.Square), reads=[src], writes=[sq])
        for c in range(nch):
            k.op("pe", lambda e: e.matmul(pbank[:, :W], ones_ap, sq[:, c, :W], start=(c == 0), stop=(c == nch - 1)),
                 reads=[self.onesv, sq], writes=[pbank])
        k.op("act", lambda e: e.activation(out=rstd[:, :W], in_=pbank[:, :W], func=AF.Sqrt, bias=self.epsb[:, 0:1], scale=1.0), reads=[pbank, self.epsb], writes=[rstd])
        k.op("dve", lambda e: e.reciprocal(out=rstd[:, :W], in_=rstd[:, :W]), reads=[rstd], writes=[rstd])
        for c in range(nch):
            k.op("dve", lambda e: e.scalar_tensor_tensor(out=dst[:, c, :W], in0=src[:, c, :W], scalar=gcol(c), in1=rstd[:, :W], op0=ALU.mult, op1=ALU.mult),
                 reads=[src, rstd, gbuf], writes=[dst])

    def phase_proj(self, l):
        k, cfg = self.k, self.cfg
        WM = 512
        xTv = self.xT.rearrange("(c p) t -> p c t", p=128)
        with ExitStack() as es:
            self.epsb = k.sb([128, 1], F32, es=es)
            k.op("dve", lambda e: e.memset(self.epsb[:], EPS), writes=[self.epsb])
            self.onesv = k.sb([128, 2, 128], BF16, es=es)
            k.op("dve", lambda e: e.memset(self.onesv[:, 0, :], 1.0 / 256), writes=[self.onesv])
            k.op("dve", lambda e: e.memset(self.onesv[:, 1, :], 1.0 / 128), writes=[self.onesv])
            fm = lambda ap: ap.rearrange("(c p) n -> p c n", p=128)
            wcq = k.sb([128, 8, 256], BF16, es=es)
            wckv = k.sb([128, 8, 128], BF16, es=es)
            wkr = k.sb([128, 8, 128], BF16, es=es)
            wkrr = k.sb([128, 8, 128], BF16, es=es)
            wpb = k.sb([128, 8, D_SHIFT], BF16, es=es)
            wuq = k.sb([128, 2, 768], BF16, es=es)
            wqro = k.sb([128, 2, 256], BF16, es=es)
            wqrr = k.sb([128, 2, 256], BF16, es=es)
            wuk = k.sb([64, 8, 128], BF16, es=es)
            k.dma("pool", wcq[:], fm(self.w_cq[l]), wcq, writes=[wcq])
            k.dma("pool", wckv[:], fm(self.w_ckv[l]), wckv, writes=[wckv])
            k.dma("pool", wkr[:], fm(self.w_kr4[l]), wkr, writes=[wkr])
            for c0 in range(0, 8, 2):
                k.dma("pool", wpb[:, c0:c0 + 2, :], fm(self.w_pb[l])[:, c0:c0 + 2, :], wpb, writes=[wpb])
            k.dma("pool", wuq[:], fm(self.w_uq[l]), wuq, writes=[wuq])
            k.dma("pool", wuk[:], self.w_ukT[l], wuk, writes=[wuk])
            wuq_r = wuq[:].rearrange("p c (h d) -> p c h d", d=96)[:, :, :, 64:96]
            wqro_v = wqro[:].rearrange("p c (h d) -> p c h d", d=32)
            wqrr_v = wqrr[:].rearrange("p c (h d) -> p c h d", d=32)
            for c in range(2):
                k.op("act", lambda e: e.copy(out=wqro_v[:, c], in_=wuq_r[:, c]), reads=[wuq], writes=[wqro])
                k.op("act", lambda e: e.mul(wqrr_v[:, c, :, 0:16], wuq_r[:, c, :, 16:32], -1.0), reads=[wuq], writes=[wqrr])
                k.op("act", lambda e: e.copy(out=wqrr_v[:, c, :, 16:32], in_=wuq_r[:, c, :, 0:16]), reads=[wuq], writes=[wqrr])
            wkr_v = wkr[:].rearrange("p c (h d) -> p c h d", d=32)
            wkrr_v = wkrr[:].rearrange("p c (h d) -> p c h d", d=32)
            for c0 in range(0, 8, 4):
                k.op("act", lambda e: e.mul(wkrr_v[:, c0:c0 + 4, :, 0:16].rearrange("p c h d -> p (c h) d"), wkr_v[:, c0:c0 + 4, :, 16:32].rearrange("p c h d -> p (c h) d"), -1.0), reads=[wkr], writes=[wkrr])
                k.op("act", lambda e: e.copy(out=wkrr_v[:, c0:c0 + 4, :, 16:32].rearrange("p c h d -> p (c h) d"), in_=wkr_v[:, c0:c0 + 4, :, 0:16].rearrange("p c h d -> p (c h) d")), reads=[wkr], writes=[wkrr])
            xt = k.sb([128, 8, WM], F32, es=es)
            ht = k.sb([128, 8, WM], BF16, n=8, es=es)
            sq = k.sb([128, 8, WM], BF16, n=8, es=es)
            rstd = k.sb([128, WM], F32, es=es)
            cq = k.sb([128, 2, WM], F32, es=es)
            cqn = k.sb([128, 2, WM], BF16, es=es)
            sq2 = k.sb([128, 2, WM], BF16, es=es)
            rstd2 = k.sb([128, WM], F32, es=es)
            qn = k.sb([64, 8, WM], BF16, n=8, es=es)
            qlat = k.sb([128, 8, WM], BF16, n=8, es=es)
            qro = k.sb([128, 2, WM], BF16, es=es)
            cs = k.sb([128, 2, WM], F32, es=es)
            t1 = k.sb([128, WM], F32, es=es)
            t2 = k.sb([128, WM], F32, es=es)
            ckv = k.sb([128, 1, WM], F32, es=es)
            ckvn = k.sb([128, 1, WM], F32, es=es)
            ckvb = k.sb([128, WM], BF16, es=es)
            krf = k.sb([128, WM], F32, es=es)
            krb = k.sb([128, WM], BF16, es=es)
            tmo = [k.sb([128, 160], F32, es=es) for _ in range(2)]
            pbs = k.sb([128, 14, WM], F32, n=14, es=es)
            ps = self.ps
            ntm = 0
            for (t0, W) in self.tiles(WM):
                k.dma("sp", xt[:, :, :W], xTv[:, :, t0:t0 + W], xt, writes=[xt])
                k.dma("sp", cs[:, :, :W], self.rope_cs.rearrange("g p t -> p g t")[:, :, t0:t0 + W], cs, writes=[cs])
                self.rmsnorm_fm(xt, ht, W, lambda c: self.pv[:, l, 16 + c:17 + c], sq, rstd)

                def lin(pb, wt, c0, m, rows=128):
                    for c in range(8):
                        k.op("pe", lambda e: e.matmul(pb[0:rows, :W], wt[:, c, c0:c0 + m], ht[:, c, :W], start=(c == 0), stop=(c == 7)),
                             reads=[wt, ht.subs[c]], writes=[pb])
                for c2 in range(2):
                    lin(ps[c2], wcq, c2 * 128, 128)
                    k.op("act", lambda e: e.copy(out=cq[:, c2, :W], in_=ps[c2][:, :W]), reads=[ps[c2]], writes=[cq])
                self.small_norm(cq, cqn, 2, W, self.onesv[:, 0, :], lambda c: self.pv[:, l, 40 + c:41 + c], sq2, rstd2, ps[2])
                for h in range(8):
                    pb = ps[h % 2]
                    for c in range(2):
                        k.op("pe", lambda e: e.matmul(pb[0:64, :W], wuq[:, c, h * 96:h * 96 + 64], cqn[:, c, :W], start=(c == 0), stop=(c == 1)),
                             reads=[wuq, cqn], writes=[pb])
                    k.op("act", lambda e: e.copy(out=qn[:, h, :W], in_=pb[0:64, :W]), reads=[pb], writes=[qn.subs[h]])
                    pl = ps[2 + h % 2]
                    k.op("pe", lambda e: e.matmul(pl[:, :W], wuk[:, h, :], qn[:, h, :W], start=True, stop=True), reads=[wuk, qn.subs[h]], writes=[pl])
                    k.op("dve", lambda e: e.tensor_copy(out=qlat[:, h, :W], in_=pl[:, :W]), reads=[pl], writes=[qlat.subs[h]])
                k.dma("sp", self.qlatT.rearrange("(h c) t -> c h t", c=128)[:, :, t0:t0 + W], qlat[:, :, :W], qlat, reads=qlat.subs)
                for g in range(2):
                    pr, pt = ps[4 + g], ps[6 + g]
                    for c in range(2):
                        k.op("pe", lambda e: e.matmul(pr[:, :W], wqro[:, c, g * 128:(g + 1) * 128], cqn[:, c, :W], start=(c == 0), stop=(c == 1)), reads=[wqro, cqn], writes=[pr])
                    for c in range(2):
                        k.op("pe", lambda e: e.matmul(pt[:, :W], wqrr[:, c, g * 128:(g + 1) * 128], cqn[:, c, :W], start=(c == 0), stop=(c == 1)), reads=[wqrr, cqn], writes=[pt])
                    k.op("dve", lambda e: e.tensor_tensor(out=t1[:, :W], in0=pr[:, :W], in1=cs[:, 0, :W], op=ALU.mult), reads=[pr, cs], writes=[t1])
                    k.op("dve", lambda e: e.tensor_tensor(out=t2[:, :W], in0=pt[:, :W], in1=cs[:, 1, :W], op=ALU.mult), reads=[pt, cs], writes=[t2])
                    k.op("dve", lambda e: e.tensor_tensor(out=qro[:, g, :W], in0=t1[:, :W], in1=t2[:, :W], op=ALU.add), reads=[t1, t2], writes=[qro])
                k.dma("sp", self.qropeT.rearrange("g p t -> p g t")[:, :, t0:t0 + W], qro[:, :, :W], qro, reads=[qro])
                lin(ps[0], wckv, 0, 128)
                k.op("act", lambda e: e.copy(out=ckv[:, 0, :W], in_=ps[0][:, :W]), reads=[ps[0]], writes=[ckv])
                self.small_norm(ckv, ckvn, 1, W, self.onesv[:, 1, :], lambda c: self.pv[:, l, 42:43], sq2, rstd2, ps[1])
                k.op("act", lambda e: e.copy(out=ckvb[:, :W], in_=ckvn[:, 0, :W]), reads=[ckvn], writes=[ckvb])
                k.dma("sp", self.ckvT[:, t0:t0 + W], ckvb[:, :W], ckvb, reads=[ckvb])
                lin(ps[2], wkr, 0, 128)
                lin(ps[3], wkrr, 0, 128)
                k.op("dve", lambda e: e.tensor_tensor(out=t1[:, :W], in0=ps[2][:, :W], in1=cs[:, 0, :W], op=ALU.mult), reads=[ps[2], cs], writes=[t1])
                k.op("dve", lambda e: e.tensor_tensor(out=t2[:, :W], in0=ps[3][:, :W], in1=cs[:, 1, :W], op=ALU.mult), reads=[ps[3], cs], writes=[t2])
                k.op("dve", lambda e: e.tensor_tensor(out=krf[:, :W], in0=t1[:, :W], in1=t2[:, :W], op=ALU.add), reads=[t1, t2], writes=[krf])
                k.op("act", lambda e: e.copy(out=krb[:, :W], in_=krf[:, :W]), reads=[krf], writes=[krb])
                k.dma("sp", self.kropeT[:, t0:t0 + W], krb[:, :W], krb, reads=[krb])
                for tb in range(W // 128):
                    pb = ps[4 + ntm % 4]
                    o_ = tmo[ntm % 2]
                    ntm += 1
                    k.op("pe", lambda e: e.transpose(pb[:, 0:128], ckvn[:, 0, tb * 128:(tb + 1) * 128], self.ident[:]), reads=[ckvn, self.ident], writes=[pb])
                    k.op("pe", lambda e: e.transpose(pb[:, 128:160], krf[0:32, tb * 128:(tb + 1) * 128], self.ident[0:32, 0:32]), reads=[krf, self.ident], writes=[pb])
                    k.op("act", lambda e: e.copy(out=o_[:, :], in_=pb[:, 0:160]), reads=[pb], writes=[o_])
                    tg = t0 + tb * 128
                    if tg < cfg.seq:
                        d1, d2 = self.o_ckv_p[l, tg:tg + 128, :], self.o_kr_p[l, tg:tg + 128, :]
                    else:
                        d1, d2 = self.o_ckv_s[l, tg - cfg.seq:tg - cfg.seq + 128, :], self.o_kr_s[l, tg - cfg.seq:tg - cfg.seq + 128, :]
                    k.dma("sp", d1, o_[:, 0:128], o_, reads=[o_])
                    k.dma("sp", d2, o_[:, 128:160], o_, reads=[o_])
                for j in range(14):
                    pb = ps[j % 4]
                    lin(pb, wpb, j * 128, 128)
                    if j % 2 == 0:
                        k.op("act", lambda e: e.copy(out=pbs[:, j, :W], in_=pb[:, :W]), reads=[pb], writes=[pbs.subs[j]])
                    else:
                        k.op("dve", lambda e: e.tensor_copy(out=pbs[:, j, :W], in_=pb[:, :W]), reads=[pb], writes=[pbs.subs[j]])
                k.dma("sp", self.pbT.rearrange("(c p) t -> p c t", p=128)[:, :, t0:t0 + W], pbs[:, :, :W], pbs, reads=pbs.subs)
            k.barrier()
            for b_ in [wcq, wckv, wkr, wpb, wuq, wuk, xt, cs, qlat, qro, ckvb, krb, pbs] + tmo:
                k.release_dsem(b_)

    def phase_mla(self, l):
        k, cfg = self.k, self.cfg
        ps = self.ps
        SEQ, NSS, PAST = cfg.seq, cfg.nss, cfg.past
        ymv = self.ymixT[0:512, :].rearrange("(h v) t -> v h t", v=64)
        with ExitStack() as es:
            wuv = k.sb([128, 512], BF16, es=es)
            k.dma("pool", wuv[:], self.w_uv[l], wuv, writes=[wuv])
            TKMAX = max(SEQ, PAST + 64)
            NKBMAX = (TKMAX + 127) // 128
            ckv_all = k.sb([128, NKBMAX * 128], BF16, es=es)
            kr_all = k.sb([128, NKBMAX * 128], BF16, es=es)
            V1 = k.sb([128, NKBMAX, 8, 65], BF16, n=NKBMAX, es=es)
            k.op("pool", lambda e: e.memset(V1[:, :, :, 64:65], 1.0), writes=[V1] + V1.subs)
            qlat = k.sb([128, 8, 512], BF16, es=es)
            qro = k.sb([128, 2, 512], BF16, es=es)
            qro_s = k.sb([128, 8, 64], BF16, es=es)
            k.op("dve", lambda e: e.memset(qro_s[:], 0.0), writes=[qro_s])
            k.op("dve", lambda e: e.memset(kr_all[:], 0.0), writes=[kr_all])
            krz = [k.sb([128, NKBMAX * 128], BF16, es=es) for _ in range(4)]
            for j in range(4):
                k.op("pool" if j % 2 else "dve", lambda e: e.memset(krz[j][:], 0.0), writes=[krz[j]])
            E = [k.sb([128, 512], BF16, es=es) for _ in range(4)]
            rl = k.sb([128, 512], F32, es=es)
            bcs = k.sb([64, 512], F32, es=es)
            yat = k.sb([64, 8, 512], BF16, n=8, es=es)
            yas = k.sb([64, 512], BF16, es=es)
            cst = k.sb([128, 4, 128], F32, es=es)
            cstk = k.sb([128, 4, 32], F32, es=es)
            nE = [0]
            nS = [0]

            def build_v(nkb, tk):
                for kb in range(nkb):
                    rows = min(128, tk - kb * 128)
                    pb = ps[4 + kb % 2]
                    k.op("pe", lambda e: e.matmul(pb[0:rows, :], ckv_all[:, kb * 128:kb * 128 + rows], wuv[:], start=True, stop=True), reads=[ckv_all, wuv], writes=[pb])
                    src = pb[0:rows, :].rearrange("p (h v) -> p h v", v=64)
                    if kb % 2 == 0:
                        k.op("act", lambda e: e.copy(out=V1[0:rows, kb, :, 0:64], in_=src), reads=[pb], writes=[V1.subs[kb]])
                    else:
                        k.op("dve", lambda e: e.tensor_copy(out=V1[0:rows, kb, :, 0:64], in_=src), reads=[pb], writes=[V1.subs[kb]])

            def finish(oT, ncol, dst_ap, dst_tr):
                k.op("dve", lambda e: e.reciprocal(out=rl[64:65, :ncol], in_=oT[64:65, :ncol]), reads=[oT], writes=[rl])
                bc = ps[6 + nS[0] % 2]
                nS[0] += 1
                k.op("pe", lambda e: e.matmul(bc[0:64, :ncol], self.ones1[64:65, 0:64], rl[64:65, :ncol], start=True, stop=True), reads=[self.ones1, rl], writes=[bc])
                k.op("act", lambda e: e.copy(out=bcs[:, :ncol], in_=bc[0:64, :ncol]), reads=[bc], writes=[bcs])
                k.op("dve", lambda e: e.tensor_tensor(out=dst_ap, in0=oT[0:64, :ncol], in1=bcs[:, :ncol], op=ALU.mult), reads=[oT, bcs], writes=[dst_tr])

            nkb = SEQ // 128
            k.dma("sp", ckv_all[:, 0:SEQ], self.ckvT[:, 0:SEQ], ckv_all, writes=[ckv_all])
            for j in range(4):
                k.dma("sp", krz[j][32 * j:32 * j + 32, 0:SEQ], self.kropeT[32 * j:32 * j + 32, 0:SEQ], krz[j], writes=[krz[j]])
            build_v(nkb, SEQ)
            for qt in range(SEQ // 512):
                q0 = qt * 512
                k.dma("sp", qlat[:], self.qlatT.rearrange("(h c) t -> c h t", c=128)[:, :, q0:q0 + 512], qlat, writes=[qlat])
                k.dma("sp", qro[:], self.qropeT.rearrange("g p t -> p g t")[:, :, q0:q0 + 512], qro, writes=[qro])
                last = qt * 4 + 3
                units = [(h, kb) for h in range(8) for kb in range(last + 1)]
                SKEW = 3
                slots = {}

                def emit_s(h, kb):
                    ca = 2 * kb - 8 * qt
                    c0 = max(ca, 0) * 64
                    st = ps[nE[0] % 4]
                    e_ = E[nE[0] % 4]
                    nE[0] += 1
                    slots[(h, kb)] = (e_, c0)
                    k.op("pe", lambda e: e.matmul(st[:, c0:512], ckv_all[:, kb * 128:(kb + 1) * 128], qlat[:, h, c0:512], start=True, stop=False), reads=[ckv_all, qlat], writes=[st])
                    k.op("pe", lambda e: e.matmul(st[:, c0:512], krz[h % 4][:, kb * 128:(kb + 1) * 128], qro[:, h // 4, c0:512], start=False, stop=True), reads=[krz[h % 4], qro], writes=[st])
                    if ca < 0:
                        k.op("act", lambda e: e.activation(out=e_[:, :], in_=st[:, :], func=AF.Exp, scale=MLA_SCALE), reads=[st], writes=[e_])
                    else:
                        k.op("act", lambda e: e.activation(out=e_[0:64, c0:512], in_=st[0:64, c0:512], func=AF.Exp, scale=MLA_SCALE), reads=[st], writes=[e_])
                        if c0 + 64 < 512:
                            k.op("act", lambda e: e.activation(out=e_[64:128, c0 + 64:512], in_=st[64:128, c0 + 64:512], func=AF.Exp, scale=MLA_SCALE), reads=[st], writes=[e_])
                        k.op("pool", lambda e: e.memset(e_[64:128, c0:c0 + 64], 0.0), writes=[e_])

                def emit_pv(h, kb):
                    e_, c0 = slots.pop((h, kb))
                    oT = ps[4 + h % 2]
                    k.op("pe", lambda e: e.matmul(oT[0:65, c0:512], V1[:, kb, h, :], e_[:, c0:512], start=(kb == 0), stop=(kb == last), skip_group_check=True),
                         reads=[V1.subs[kb], e_], writes=[oT])
                    if kb == last:
                        finish(oT, 512, yat[:, h, :], yat.subs[h])

                for i in range(len(units) + SKEW):
                    if i >= SKEW:
                        emit_pv(*units[i - SKEW])
                    if i < len(units):
                        emit_s(*units[i])
                k.dma("sp", ymv[:, :, q0:q0 + 512], yat[:], yat, reads=yat.subs)

            npb = PAST // 128
            tk = PAST + 64
            nkb = npb + 1
            for s_ in range(NSS):
                t0 = SEQ + s_ * 64
                for b0 in range(0, npb, 4):
                    nb_ = min(4, npb - b0)
                    k.dma("sp", cst[:, 0:nb_, :], self.c_ckv[l, s_, b0 * 128:(b0 + nb_) * 128, :].rearrange("(b p) c -> p b c", p=128), cst, writes=[cst])
                    k.dma("sp", cstk[:, 0:nb_, :], self.c_kr[l, s_, b0 * 128:(b0 + nb_) * 128, :].rearrange("(b p) c -> p b c", p=128), cstk, writes=[cstk])
                    pa, pk = ps[(b0 // 4) % 2], ps[2 + (b0 // 4) % 2]
                    for b in range(nb_):
                        k.op("pe", lambda e: e.transpose(pa[:, b * 128:(b + 1) * 128], cst[:, b, :], self.ident[:]), reads=[cst, self.ident], writes=[pa])
                    for b in range(nb_):
                        k.op("pe", lambda e: e.transpose(pk[0:32, b * 128:(b + 1) * 128], cstk[:, b, :], self.ident[:]), reads=[cstk, self.ident], writes=[pk])
                    k.op("act", lambda e: e.copy(out=ckv_all[:, b0 * 128:(b0 + nb_) * 128], in_=pa[:, 0:nb_ * 128]), reads=[pa], writes=[ckv_all])
                    k.op("dve", lambda e: e.tensor_copy(out=kr_all[0:32, b0 * 128:(b0 + nb_) * 128], in_=pk[0:32, 0:nb_ * 128]), reads=[pk], writes=[kr_all])
                k.dma("sp", ckv_all[:, PAST:PAST + 64], self.ckvT[:, t0:t0 + 64], ckv_all, writes=[ckv_all])
                k.dma("sp", kr_all[0:32, PAST:PAST + 64], self.kropeT[0:32, t0:t0 + 64], kr_all, writes=[kr_all])
                build_v(nkb, tk)
                k.dma("sp", qlat[:, :, 0:64], self.qlatT.rearrange("(h c) t -> c h t", c=128)[:, :, t0:t0 + 64], qlat, writes=[qlat])
                k.dma("sp", qro_s[0:32, :, :], self.qropeT.rearrange("g (hh j) t -> j (g hh) t", j=32)[:, :, t0:t0 + 64], qro_s, writes=[qro_s])
                oT = ps[4 + s_ % 2]
                slots = {}

                def emit_s2(kb):
                    rows = min(128, tk - kb * 128)
                    st = ps[nE[0] % 4]
                    e_ = E[nE[0] % 4]
                    nE[0] += 1
                    slots[kb] = (e_, rows)
                    stv = st[0:rows, :].rearrange("p (h t) -> p h t", t=64)
                    k.op("pe", lambda e: e.matmul(stv, ckv_all[:, kb * 128:kb * 128 + rows], qlat[:, :, 0:64], start=True, stop=False), reads=[ckv_all, qlat], writes=[st])
                    k.op("pe", lambda e: e.matmul(stv, kr_all[:, kb * 128:kb * 128 + rows], qro_s[:, :, :], start=False, stop=True), reads=[kr_all, qro_s], writes=[st])
                    k.op("act", lambda e: e.activation(out=e_[0:rows, :], in_=st[0:rows, :], func=AF.Exp, scale=MLA_SCALE), reads=[st], writes=[e_])

                def emit_pv2(kb):
                    e_, rows = slots.pop(kb)
                    for h in range(8):
                        k.op("pe", lambda e: e.matmul(oT[0:65, h * 64:(h + 1) * 64], V1[0:rows, kb, h, :], e_[0:rows, h * 64:(h + 1) * 64], start=(kb == 0 and h == 0), stop=(kb == nkb - 1), skip_group_check=True),
                             reads=[V1.subs[kb], e_], writes=[oT])

                for i in range(nkb + 3):
                    if i >= 3:
                        emit_pv2(i - 3)
                    if i < nkb:
                        emit_s2(i)
                finish(oT, 512, yas[:, :], yas.trk)
                k.dma("sp", ymv[:, :, t0:t0 + 64], yas[:, :].rearrange("p (h t) -> p h t", t=64), yas, reads=[yas])
            k.barrier()
            for b_ in [wuv, ckv_all, kr_all, qlat, qro, qro_s, yat, yas, cst, cstk] + krz:
                k.release_dsem(b_)


    def phase_rwkv(self, l):
        k, cfg = self.k, self.cfg
        nc = self.nc
        ps = self.ps
        SEQ, NSS = cfg.seq, cfg.nss
        WM = 256
        pbv = self.pbT.rearrange("(c p) t -> p c t", p=128)
        ybv = self.ymixT[512:1024, :].rearrange("(c p) t -> p c t", p=128)
        M1 = self.cmask[:, 0:128]
        M2 = self.cmask[0:64, 128:192]
        BLK = self.cmask[:, 256:384]
        ID = self.ident
        PVC = lambda a, b: self.pv[:, l, a:b]
        MU, W0, A0, KK_, KA, RK, GG, GB = PVC(48, 62), PVC(62, 66), PVC(66, 70), PVC(70, 74), PVC(74, 78), PVC(78, 82), PVC(82, 86), PVC(86, 90)
        bc3 = lambda ap, W: ap.unsqueeze(2).to_broadcast([128, ap.shape[1], W])
        with ExitStack() as es:
            S4 = lambda n=0: k.sb([128, 4, WM], F32, n=n, es=es)
            wup = k.sb([64, 512], BF16, es=es)
            aup = k.sb([128, 512], BF16, es=es)
            gup = k.sb([128, 512], BF16, es=es)
            k.dma("pool", wup[:], self.w_wup[l], wup, writes=[wup])
            k.dma("pool", aup[64:128, :], self.w_aup[l], aup, writes=[aup])
            k.dma("pool", gup[:], self.w_gup[l], gup, writes=[gup])
            cst = k.sb([128, 24], F32, es=es)
            k.op("dve", lambda e: e.memset(cst[:, 0:1], 1e-12), writes=[cst])
            k.op("dve", lambda e: e.memset(cst[:, 1:2], GN_EPS), writes=[cst])
            k.op("dve", lambda e: e.tensor_scalar(out=cst[:, 2:6], in0=KA, scalar1=-1.0, scalar2=1.0, op0=ALU.mult, op1=ALU.add), reads=[self.pv], writes=[cst])
            OMK = cst[:, 2:6]
            k.op("dve", lambda e: e.tensor_scalar(out=cst[:, 8:22], in0=MU, scalar1=-1.0, scalar2=1.0, op0=ALU.mult, op1=ALU.add), reads=[self.pv], writes=[cst])
            OMM = cst[:, 8:22]
            mask01 = k.sb([128, WM], F32, es=es)
            k.op("dve", lambda e: e.memset(mask01[:], 1.0), writes=[mask01])
            k.op("dve", lambda e: e.memset(mask01[:, 0:WM:64], 0.0), writes=[mask01])
            PB = k.sb([128, 14, WM], F32, es=es)
            PV_ = k.sb([128, 14, WM], F32, es=es)
            lw, cum, E1, E2, E3, av, kk, tm1, tm2, bon, gT, Ys, yc, tq, km = [S4() for _ in range(15)]
            AR = k.sb([128, 4, WM // 64, 2, 64], F32, es=es)
            BK = k.sb([128, 4, WM // 64, 2, 64], F32, es=es)
            vT = k.sb([128, 4, 64 + WM], F32, es=es)
            k.op("dve", lambda e: e.memset(vT[:, :, 0:64], 0.0), writes=[vT])
            txw = k.sb([64, WM], BF16, es=es)
            xab = k.sb([128, WM], BF16, es=es)
            sxg = k.sb([128, WM], BF16, es=es)
            ybT = k.sb([128, 4, WM], BF16, es=es)
            NCH = WM // 64
            G1m = [k.sb([128, 8, 128], F32, es=es) for _ in range(NCH)]
            Pm = [k.sb([128, 4, 64], F32, es=es) for _ in range(NCH)]
            BKtm = [k.sb([128, 8, 64], F32, es=es) for _ in range(NCH)]
            UV = [k.sb([128, 8, 64], F32, es=es) for _ in range(NCH)]
            RVs = [k.sb([128, 4, 64], F32, es=es) for _ in range(NCH)]
            XsC = [[k.sb([128, 4, 64], F32, es=es) for _ in range(2)] for _ in range(2)]
            XtsC = [[k.sb([128, 4, 64], F32, es=es) for _ in range(2)] for _ in range(2)]
            Rs = k.sb([128, 4, 64], F32, es=es)
            Zt = k.sb([128, 4, 64], F32, es=es)
            tmz = k.sb([128, 4, 64], F32, es=es)
            Sio = k.sb([64, 8, 64], F32, es=es)
            sh0 = k.sb([128, 14], F32, es=es)
            shl = k.sb([128, 14], F32, es=es)

            def tt(e, out, a, b, op, reads, writes):
                k.op(e, lambda en: en.tensor_tensor(out=out, in0=a, in1=b, op=op), reads=reads, writes=writes)

            def actf(out, a, func, reads, writes, bias=None, scale=1.0):
                if bias is None:
                    k.op("act", lambda en: en.activation(out=out, in_=a, func=func, scale=scale), reads=reads, writes=writes)
                else:
                    k.op("act", lambda en: en.activation(out=out, in_=a, func=func, bias=bias, scale=scale), reads=reads, writes=writes)

            c4 = lambda ap: ap.rearrange("p h (c t) -> p h c t", t=64)
            f2 = lambda ap: ap.rearrange("p a t -> p (a t)")

            def do_tile(t0, W, seq_start, shift_src):
                nch = W // 64
                k.dma("sp", PB[:, :, :W], pbv[:, :, t0:t0 + W], PB, writes=[PB])
                if seq_start:
                    if W > 1:
                        k.dma("sp", PV_[:, :, 1:W], pbv[:, :, t0:t0 + W - 1], PV_, writes=[PV_])
                    if shift_src is None:
                        k.op("pool", lambda e: e.memset(PV_[:, :, 0:1], 0.0), writes=[PV_])
                    else:
                        k.dma("sp", sh0[:], shift_src, sh0, writes=[sh0])
                        k.op("pool", lambda e: e.tensor_copy(out=PV_[:, :, 0:1], in_=sh0[:].unsqueeze(2)), reads=[sh0], writes=[PV_])
                else:
                    k.dma("sp", PV_[:, :, :W], pbv[:, :, t0 - 1:t0 + W - 1], PV_, writes=[PV_])
                for cch in range(14):
                    k.op("act", lambda e: e.activation(out=PV_[:, cch, :W], in_=PV_[:, cch, :W], func=AF.Identity, scale=MU[:, cch:cch + 1]), reads=[PV_, self.pv], writes=[PV_])
                    k.op("dve", lambda e: e.scalar_tensor_tensor(out=PB[:, cch, :W], in0=PB[:, cch, :W], scalar=OMM[:, cch:cch + 1], in1=PV_[:, cch, :W], op0=ALU.mult, op1=ALU.add), reads=[PB, PV_, cst], writes=[PB])
                r_, kx, v_ = PB[:, 0:4, :W], PB[:, 4:8, :W], PB[:, 8:12, :W]
                actf(txw[0:64, :W], PB[0:64, 12, :W], AF.Tanh, [PB], [txw])
                tt("dve", kk[:, :, :W], kx, bc3(KK_, W), ALU.mult, [PB, self.pv], [kk])
                actf(xab[64:128, :W], PB[64:128, 12, :W], AF.Copy, [PB], [xab])
                actf(sxg[:, :W], PB[:, 13, :W], AF.Sigmoid, [PB], [sxg])
                actf(tq[:, :, :W], kk[:, :, :W], AF.Square, [kk], [tq])
                for hp in range(4):
                    pz, pa = ps[hp % 2], ps[2 + hp % 2]
                    k.op("pe", lambda e: e.matmul(pz[:, :W], wup[0:64, hp * 128:(hp + 1) * 128], txw[0:64, :W], start=True, stop=True), reads=[wup, txw], writes=[pz])
                    k.op("pe", lambda e: e.matmul(pa[:, :W], aup[64:128, hp * 128:(hp + 1) * 128], xab[64:128, :W], start=True, stop=True), reads=[aup, xab], writes=[pa])
                    actf(lw[:, hp, :W], pz[:, :W], AF.Sigmoid, [pz, self.pv], [lw], bias=W0[:, hp:hp + 1])
                    actf(av[:, hp, :W], pa[:, :W], AF.Sigmoid, [pa, self.pv], [av], bias=A0[:, hp:hp + 1])
                    k.op("dve", lambda e: e.tensor_tensor_scan(out=cum[:, hp, :W], data0=mask01[:, :W], data1=lw[:, hp, :W], initial=0.0, op0=ALU.mult, op1=ALU.add),
                         reads=[mask01, lw], writes=[cum])
                for hp in range(4):
                    pg, pn = ps[4 + hp % 2], ps[6 + hp % 2]
                    k.op("pe", lambda e: e.matmul(pn[:, :W], BLK, tq[:, hp, :W], start=True, stop=True), reads=[self.cmask, tq], writes=[pn])
                    k.op("pe", lambda e: e.matmul(pg[:, :W], gup[:, hp * 128:(hp + 1) * 128], sxg[:, :W], start=True, stop=True), reads=[gup, sxg], writes=[pg])
                    actf(tm2[:, hp, :W], pn[:, :W], AF.Sqrt, [pn, cst], [tm2], bias=cst[:, 0:1])
                    k.op("dve", lambda e: e.tensor_copy(out=gT[:, hp, :W], in_=pg[:, :W]), reads=[pg], writes=[gT])
                actf(E1[:, :, :W], cum[:, :, :W], AF.Exp, [cum], [E1], scale=-DECAY_C)
                tt("dve", tm1[:, :, :W], cum[:, :, :W], lw[:, :, :W], ALU.subtract, [cum, lw], [tm1])
                actf(E2[:, :, :W], cum[:, :, :W], AF.Exp, [cum], [E2], scale=DECAY_C)
                k.op("dve", lambda e: e.reciprocal(out=tm2[:, :, :W], in_=tm2[:, :, :W]), reads=[tm2], writes=[tm2])
                actf(E3[:, :, :W], tm1[:, :, :W], AF.Exp, [tm1], [E3], scale=-DECAY_C)
                tt("dve", kk[:, :, :W], kk[:, :, :W], tm2[:, :, :W], ALU.mult, [kk, tm2], [kk])
                for hp in range(4):
                    k.op("act", lambda e: e.activation(out=km[:, hp, :W], in_=av[:, hp, :W], func=AF.Identity, bias=OMK[:, hp:hp + 1], scale=KA[:, hp:hp + 1]), reads=[av, self.pv, cst], writes=[km])
                k.op("act", lambda e: e.copy(out=vT[:, :, 64:64 + W], in_=v_), reads=[PB], writes=[vT])
                tt("dve", km[:, :, :W], km[:, :, :W], kx, ALU.mult, [km, PB], [km])
                k.op("dve", lambda e: e.scalar_tensor_tensor(out=AR[:, :, 0:nch, 0, :], in0=c4(kk[:, :, :W]), scalar=-1.0, in1=c4(E3[:, :, :W]), op0=ALU.mult, op1=ALU.mult), reads=[kk, E3], writes=[AR])
                tt("dve", AR[:, :, 0:nch, 1, :], c4(r_), c4(E1[:, :, :W]), ALU.mult, [PB, E1], [AR])
                tt("dve", tm2[:, :, :W], kk[:, :, :W], av[:, :, :W], ALU.mult, [kk, av], [tm2])
                tt("dve", BK[:, :, 0:nch, 1, :], c4(km[:, :, :W]), c4(E2[:, :, :W]), ALU.mult, [km, E2], [BK])
                tt("dve", BK[:, :, 0:nch, 0, :], c4(tm2[:, :, :W]), c4(E2[:, :, :W]), ALU.mult, [tm2, E2], [BK])
                tt("pool", tq[:, :, :W], r_, km[:, :, :W], ALU.mult, [PB, km], [tq])
                tt("pool", tq[:, :, :W], tq[:, :, :W], bc3(RK, W), ALU.mult, [tq, self.pv], [tq])

                hv = lambda ap: ap.rearrange("p (a b) t -> p a b t", b=2)
                p3 = lambda ap, t: ap.rearrange("p (h t) -> p h t", t=t)
                HEADS = [(h, h // 2, h % 2, (h % 2) * 64) for h in range(8)]

                def gen_B(c, q, Xs, Xts):
                    g1, pm, bkt, uv, rv = G1m[c], Pm[c], BKtm[c], UV[c], RVs[c]
                    for (h, hp, hh, b0) in HEADS:
                        k.op("pe", lambda e: e.matmul(q[hh][:, hp * 128:(hp + 1) * 128], f2(BK[b0:b0 + 64, hp, c, :, :]), f2(AR[b0:b0 + 64, hp, c, :, :]), start=True, stop=True),
                             reads=[BK, AR], writes=[q[hh]])
                    yield
                    x_prev, xt_prev = Xs[0], Xts[0]
                    for hh in range(2):
                        b0 = hh * 64
                        tt("dve", hv(g1[:, :, :])[:, :, hh, :], p3(q[hh][:, :], 128), M1.unsqueeze(1).to_broadcast([128, 4, 128]), ALU.mult, [q[hh], self.cmask], [g1])
                        tt("dve", x_prev[b0:b0 + 64, :, :], p3(q[hh][0:64, :], 128)[:, :, 0:64], M1[0:64, 0:64].unsqueeze(1).to_broadcast([64, 4, 64]), ALU.mult, [q[hh], self.cmask], [x_prev])
                    for (h, hp, hh, b0) in HEADS:
                        k.op("pe", lambda e: e.matmul(q[hh][b0:b0 + 64, hp * 64:(hp + 1) * 64], AR[b0:b0 + 64, hp, c, 0, :], BK[b0:b0 + 64, hp, c, 0, :], start=True, stop=True), reads=[AR, BK], writes=[q[hh]])
                    yield
                    for hh in range(2):
                        b0 = hh * 64
                        tt("dve", xt_prev[b0:b0 + 64, :, :], p3(q[hh][b0:b0 + 64, 0:256], 64), self.cmask[b0:b0 + 64, 128:192].unsqueeze(1).to_broadcast([64, 4, 64]), ALU.mult, [q[hh], self.cmask], [xt_prev])
                        tt("pool", pm[b0:b0 + 64, :, :], x_prev[b0:b0 + 64, :, :], ID[b0:b0 + 64, b0:b0 + 64].unsqueeze(1).to_broadcast([64, 4, 64]), ALU.add, [x_prev, ID], [pm])
                    yield
                    for lev in range(1, 6):
                        xn, xtn = Xs[lev % 2], Xts[lev % 2]
                        for (h, hp, hh, b0) in HEADS:
                            k.op("pe", lambda e: e.matmul(q[hh][b0:b0 + 64, 256 + hp * 64:256 + (hp + 1) * 64], x_prev[b0:b0 + 64, hp, :], xt_prev[b0:b0 + 64, hp, :], start=True, stop=True), reads=[xt_prev, x_prev], writes=[q[hh]])
                        if lev < 5:
                            for (h, hp, hh, b0) in HEADS:
                                k.op("pe", lambda e: e.matmul(q[hh][b0:b0 + 64, hp * 64:(hp + 1) * 64], xt_prev[b0:b0 + 64, hp, :], x_prev[b0:b0 + 64, hp, :], start=True, stop=True), reads=[xt_prev, x_prev], writes=[q[hh]])
                        for hh in range(2):
                            b0 = hh * 64
                            k.op("dve", lambda e: e.tensor_copy(out=xtn[b0:b0 + 64, :, :], in_=p3(q[hh][b0:b0 + 64, 256:512], 64)), reads=[q[hh]], writes=[xtn])
                            if lev < 5:
                                k.op("act", lambda e: e.copy(out=xn[b0:b0 + 64, :, :], in_=p3(q[hh][b0:b0 + 64, 0:256], 64)), reads=[q[hh]], writes=[xn])
                        yield
                        for (h, hp, hh, b0) in HEADS:
                            k.op("pe", lambda e: e.matmul(q[hh][b0:b0 + 64, hp * 64:(hp + 1) * 64], xtn[b0:b0 + 64, hp, :], pm[b0:b0 + 64, hp, :], start=True, stop=True), reads=[xtn, pm], writes=[q[hh]])
                        for hh in range(2):
                            b0 = hh * 64
                            tt("dve", pm[b0:b0 + 64, :, :], p3(q[hh][b0:b0 + 64, 0:256], 64), pm[b0:b0 + 64, :, :], ALU.add, [q[hh], pm], [pm])
                        yield
                        x_prev, xt_prev = xn, xtn
                    for (h, hp, hh, b0) in HEADS:
                        k.op("pe", lambda e: e.transpose(q[hh][:, hp * 64:(hp + 1) * 64], f2(BK[b0:b0 + 64, hp, c, :, :]), ID[b0:b0 + 64, b0:b0 + 64]), reads=[BK, ID], writes=[q[hh]])
                    for (h, hp, hh, b0) in HEADS:
                        k.op("pe", lambda e: e.transpose(q[hh][:, 256 + hp * 64:256 + (hp + 1) * 64], vT[b0:b0 + 64, hp, c * 64:c * 64 + 128], ID[b0:b0 + 64, b0:b0 + 64]), reads=[vT, ID], writes=[q[hh]])
                    for hh in range(2):
                        k.op("act", lambda e: e.copy(out=hv(bkt[:, :, :])[:, :, hh, :], in_=p3(q[hh][:, 0:256], 64)), reads=[q[hh]], writes=[bkt])
                        k.op("dve", lambda e: e.tensor_copy(out=hv(uv[64:128, :, :])[:, :, hh, :], in_=p3(q[hh][64:128, 256:512], 64)), reads=[q[hh]], writes=[uv])
                    yield
                    for (h, hp, hh, b0) in HEADS:
                        k.op("pe", lambda e: e.matmul(q[hh][b0:b0 + 64, hp * 64:(hp + 1) * 64], g1[64:128, h, 0:64], uv[64:128, h, :], start=True, stop=True), reads=[g1, uv], writes=[q[hh]])
                    for hh in range(2):
                        b0 = hh * 64
                        k.op("act", lambda e: e.copy(out=rv[b0:b0 + 64, :, :], in_=p3(q[hh][b0:b0 + 64, 0:256], 64)), reads=[q[hh]], writes=[rv])
                    yield

                def gen_C(c):
                    cols = slice(c * 64, (c + 1) * 64)
                    g1, pm, bkt, uv, rv = G1m[c], Pm[c], BKtm[c], UV[c], RVs[c]
                    pRU = pYZ = (ps[0], ps[1])
                    for (h, hp, hh, b0) in HEADS:
                        k.op("pe", lambda e: e.matmul(pRU[hh][b0:b0 + 64, hp * 64:(hp + 1) * 64], AR[b0:b0 + 64, hp, c, 0, :], Zt[b0:b0 + 64, hp, :], start=True, stop=True), reads=[AR, Zt], writes=[pRU[hh]])
                    for hh in range(2):
                        b0 = hh * 64
                        tt("dve", Rs[b0:b0 + 64, :, :], p3(pRU[hh][b0:b0 + 64, 0:256], 64), rv[b0:b0 + 64, :, :], ALU.add, [pRU[hh], rv], [Rs])
                    yield
                    for (h, hp, hh, b0) in HEADS:
                        k.op("pe", lambda e: e.matmul(pRU[hh][0:64, 256 + hp * 64:256 + (hp + 1) * 64], pm[b0:b0 + 64, hp, :], Rs[b0:b0 + 64, hp, :], start=True, stop=True), reads=[pm, Rs], writes=[pRU[hh]])
                    k.op("act", lambda e: e.copy(out=hv(uv[0:64, :, :])[:, :, 0, :], in_=p3(pRU[0][0:64, 256:512], 64)), reads=[pRU[0]], writes=[uv])
                    k.op("dve", lambda e: e.tensor_copy(out=hv(uv[0:64, :, :])[:, :, 1, :], in_=p3(pRU[1][0:64, 256:512], 64)), reads=[pRU[1]], writes=[uv])
                    yield
                    for (h, hp, hh, b0) in HEADS:
                        k.op("pe", lambda e: e.matmul(pYZ[hh][b0:b0 + 64, hp * 64:(hp + 1) * 64], Zt[b0:b0 + 64, hp, :], AR[b0:b0 + 64, hp, c, 1, :], start=(hp == 0), stop=False, skip_group_check=True), reads=[Zt, AR], writes=[pYZ[hh]])
                    for (h, hp, hh, b0) in HEADS:
                        k.op("pe", lambda e: e.matmul(pYZ[hh][b0:b0 + 64, 256 + hp * 64:256 + (hp + 1) * 64], bkt[:, h, :], uv[:, h, :], start=False, stop=True, skip_group_check=True), reads=[bkt, uv], writes=[pYZ[hh]])
                    for (h, hp, hh, b0) in HEADS:
                        k.op("pe", lambda e: e.matmul(pYZ[hh][b0:b0 + 64, hp * 64:(hp + 1) * 64], uv[:, h, :], g1[:, h, 64:128], start=False, stop=True, skip_group_check=True), reads=[uv, g1], writes=[pYZ[hh]])
                    for hh in range(2):
                        b0 = hh * 64
                        tt("dve", tmz[b0:b0 + 64, :, :], p3(pYZ[hh][b0:b0 + 64, 256:512], 64), Zt[b0:b0 + 64, :, :], ALU.add, [pYZ[hh], Zt], [tmz])
                    tt("dve", Zt[:, :, :], tmz[:, :, :], E1[:, :, c * 64 + 63:c * 64 + 64].to_broadcast([128, 4, 64]), ALU.mult, [tmz, E1], [Zt])
                    for hh in range(2):
                        b0 = hh * 64
                        k.op("act", lambda e: e.copy(out=Ys[b0:b0 + 64, :, cols], in_=p3(pYZ[hh][b0:b0 + 64, 0:256], 64)), reads=[pYZ[hh]], writes=[Ys])
                    yield

                def run_interleaved(*gens):
                    gens = [g for g in gens if g is not None]
                    while gens:
                        for g in list(gens):
                            try:
                                next(g)
                            except StopIteration:
                                gens.remove(g)

                def chain(*gs):
                    for g in gs:
                        yield from g

                QA, QB = (ps[2], ps[3]), (ps[4], ps[5])
                gB = lambda c: gen_B(c, QA if c % 2 == 0 else QB, XsC[c % 2], XtsC[c % 2]) if c < nch else None
                run_interleaved(gB(0), gB(1))
                for c in range(0, nch, 2):
                    run_interleaved(chain(*[gen_C(cc) for cc in range(c, min(c + 2, nch))]), gB(c + 2), gB(c + 3))

                for hp in range(4):
                    pn = ps[6 + hp % 2]
                    k.op("pe", lambda e: e.matmul(pn[:, :W], BLK, tq[:, hp, :W], start=True, stop=True), reads=[self.cmask, tq], writes=[pn])
                    tt("dve", bon[:, hp, :W], pn[:, :W], PB[:, 8 + hp, :W], ALU.mult, [pn, PB], [bon])
                for hp in range(4):
                    pn = ps[6 + hp % 2]
                    k.op("pe", lambda e: e.matmul(pn[:, :W], BLK, Ys[:, hp, :W], start=True, stop=True), reads=[self.cmask, Ys], writes=[pn])
                    k.op("dve", lambda e: e.scalar_tensor_tensor(out=yc[:, hp, :W], in0=pn[:, :W], scalar=-1.0 / 64, in1=Ys[:, hp, :W], op0=ALU.mult, op1=ALU.add), reads=[pn, Ys], writes=[yc])
                actf(tm1[:, :, :W], yc[:, :, :W], AF.Square, [yc], [tm1])
                for hp in range(4):
                    pn = ps[6 + hp % 2]
                    k.op("pe", lambda e: e.matmul(pn[:, :W], BLK, tm1[:, hp, :W], start=True, stop=True), reads=[self.cmask, tm1], writes=[pn])
                    actf(tm2[:, hp, :W], pn[:, :W], AF.Sqrt, [pn, cst], [tm2], bias=cst[:, 1:2], scale=1.0 / 64)
                k.op("dve", lambda e: e.reciprocal(out=tm2[:, :, :W], in_=tm2[:, :, :W]), reads=[tm2], writes=[tm2])
                tt("dve", yc[:, :, :W], yc[:, :, :W], tm2[:, :, :W], ALU.mult, [yc, tm2], [yc])
                for hp in range(4):
                    k.op("act", lambda e: e.activation(out=yc[:, hp, :W], in_=yc[:, hp, :W], func=AF.Identity, bias=GB[:, hp:hp + 1], scale=GG[:, hp:hp + 1]), reads=[yc, self.pv], writes=[yc])
                tt("dve", yc[:, :, :W], yc[:, :, :W], bon[:, :, :W], ALU.add, [yc, bon], [yc])
                tt("dve", ybT[:, :, :W], yc[:, :, :W], gT[:, :, :W], ALU.mult, [yc, gT], [ybT])
                k.dma("sp", ybv[:, :, t0:t0 + W], ybT[:, :, :W], ybT, reads=[ybT])

            def finish_seq(wkv_dst, sh_dst, t_last):
                if cfg.rstop < 5:
                    return
                pT = ps[6]
                for hp in range(4):
                    k.op("pe", lambda e: e.transpose(pT[0:64, hp * 128:(hp + 1) * 128], Zt[:, hp, :], ID[:, :]), reads=[Zt, ID], writes=[pT])
                k.op("act", lambda e: e.copy(out=Sio[:, :, :], in_=pT[0:64, :].rearrange("p (h t) -> p h t", t=64)), reads=[pT], writes=[Sio])
                k.dma("sp", wkv_dst.rearrange("h v k -> v h k"), Sio[:, :, :], Sio, reads=[Sio])
                with nc.allow_non_contiguous_dma(reason="single token-shift row, 7 KiB"):
                    k.dma("sp", shl[:], pbv[:, :, t_last:t_last + 1].rearrange("p c o -> p (c o)"), shl, writes=[shl])
                    k.dma("sp", sh_dst.rearrange("(c p) -> p c", p=128), shl[:], shl, reads=[shl])

            k.op("dve", lambda e: e.memset(Zt[:], 0.0), writes=[Zt])
            for t0 in range(0, SEQ, WM):
                do_tile(t0, min(WM, SEQ - t0), t0 == 0, None)
            finish_seq(self.o_wkv_p[l], self.o_sh_p[l, 0], SEQ - 1)
            for s_ in range(NSS):
                t0 = SEQ + s_ * 64
                k.dma("sp", Sio[:, :, :], self.st_wkv[l, s_].rearrange("h v k -> v h k"), Sio, writes=[Sio])
                pT = ps[7]
                for hp in range(4):
                    k.op("pe", lambda e: e.transpose(pT[:, hp * 64:(hp + 1) * 64], Sio[:, hp * 2:hp * 2 + 2, :].rearrange("p h t -> p (h t)"), ID[0:64, 0:64]), reads=[Sio, ID], writes=[pT])
                k.op("act", lambda e: e.copy(out=Zt[:, :, :], in_=pT[:, 0:256].rearrange("p (h t) -> p h t", t=64)), reads=[pT], writes=[Zt])
                do_tile(t0, 64, True, self.st_sh[l, s_])
                finish_seq(self.o_wkv_s[l, s_], self.o_sh_s[l, s_, 0], t0 + 63)
            k.barrier()
            for b_ in [wup, aup, gup, PB, PV_, ybT, Sio, sh0, shl]:
                k.release_dsem(b_)

    def phase_xattn(self, l):
        k, cfg = self.k, self.cfg
        ps = self.ps
        SEQ, NSS = cfg.seq, cfg.nss
        WM = 512
        xTv = self.xT.rearrange("(c p) t -> p c t", p=128)
        ymT = self.ymixT.rearrange("(c p) t -> p c t", p=128)
        fm = lambda ap: ap.rearrange("(c p) n -> p c n", p=128)
        with ExitStack() as es:
            self.epsb = k.sb([128, 1], F32, es=es)
            k.op("dve", lambda e: e.memset(self.epsb[:], EPS), writes=[self.epsb])
            oneb = k.sb([128, 128], BF16, es=es)
            k.op("dve", lambda e: e.memset(oneb[:], 1.0), writes=[oneb])
            wout = k.sb([128, 8, D], BF16, es=es)
            wmq = k.sb([128, 8, D], BF16, es=es)
            wmo = k.sb([128, 8, D], BF16, es=es)
            for wt, src in ((wout, self.w_out), (wmq, self.w_mq), (wmo, self.w_mo)):
                for c0 in range(0, 8, 2):
                    k.dma("pool", wt[:, c0:c0 + 2, :], fm(src[l])[:, c0:c0 + 2, :], wt, writes=[wt])
            mkT = [k.sb([128, 8, N_MEM], BF16, es=es) for _ in range(1 + NSS)]
            mvs = [k.sb([128, 2, D], BF16, es=es) for _ in range(1 + NSS)]
            with ExitStack() as es2:
                wmk = k.sb([128, 8, D], BF16, es=es2)
                wmv = k.sb([128, 8, D], BF16, es=es2)
                for wt, src in ((wmk, self.w_mk), (wmv, self.w_mv)):
                    for c0 in range(0, 8, 2):
                        k.dma("pool", wt[:, c0:c0 + 2, :], fm(src[l])[:, c0:c0 + 2, :], wt, writes=[wt])
                mtm = k.sb([128, 2, D], F32, es=es2)
                memT = k.sb([128, 8, N_MEM], F32, es=es2)
                mn = k.sb([128, 8, N_MEM], BF16, n=8, es=es2)
                sq = k.sb([128, 8, N_MEM], BF16, n=8, es=es2)
                rstd = k.sb([128, N_MEM], F32, es=es2)
                otm = [k.sb([128, D], F32, es=es2) for _ in range(2)]
                k.dma("sp", mtm[:], self.mem_p.rearrange("(b p) d -> p b d", p=128), mtm, writes=[mtm])
                for c in range(8):
                    pb = ps[c % 4]
                    for mb in range(2):
                        k.op("pe", lambda e: e.transpose(pb[:, mb * 128:(mb + 1) * 128], mtm[:, mb, c * 128:(c + 1) * 128], self.ident[:]), reads=[mtm, self.ident], writes=[pb])
                    k.op("act", lambda e: e.copy(out=memT[:, c, :], in_=pb[:, 0:256]), reads=[pb], writes=[memT])
                XS = cfg.xstop
                if XS >= 2:
                    self.rmsnorm_fm(memT, mn, N_MEM, lambda c: self.pv[:, l, 32 + c:33 + c], sq, rstd)
                for j in range(8 if XS >= 3 else 0):
                    pb = ps[4 + j % 4]
                    for c in range(8):
                        k.op("pe", lambda e: e.matmul(pb[:, 0:256], wmk[:, c, j * 128:(j + 1) * 128], mn[:, c, :], start=(c == 0), stop=(c == 7)), reads=[wmk, mn.subs[c]], writes=[pb])
                    k.op("act", lambda e: e.copy(out=mkT[0][:, j, :], in_=pb[:, 0:256]), reads=[pb], writes=[mkT[0]])
                ntm = 0
                for wt, dst, keep in (((wmk, self.o_mk, False), (wmv, self.o_mv, True)) if XS >= 4 else ()):
                    for mb in range(2):
                        o_ = otm[ntm % 2]
                        ntm += 1
                        for hh in range(2):
                            pb = ps[(mb * 2 + hh) % 4]
                            for c in range(8):
                                k.op("pe", lambda e: e.matmul(pb[:, :], mn[:, c, mb * 128:(mb + 1) * 128], wt[:, c, hh * 512:(hh + 1) * 512], start=(c == 0), stop=(c == 7)), reads=[wt, mn.subs[c]], writes=[pb])
                            k.op("act", lambda e: e.copy(out=o_[:, hh * 512:(hh + 1) * 512], in_=pb[:, :]), reads=[pb], writes=[o_])
                            if keep:
                                k.op("dve", lambda e: e.tensor_copy(out=mvs[0][:, mb, hh * 512:(hh + 1) * 512], in_=o_[:, hh * 512:(hh + 1) * 512]), reads=[o_], writes=[mvs[0]])
                        k.dma("sp", dst[l, mb * 128:(mb + 1) * 128, :], o_[:], o_, reads=[o_])
                for s_ in range(NSS if "xsamp" not in cfg.skip else 0):
                    k.dma("sp", mtm[:], self.c_mk[l, s_].rearrange("(b p) d -> p b d", p=128), mtm, writes=[mtm])
                    k.dma("pool", mvs[1 + s_][:], self.c_mv[l, s_].rearrange("(b p) d -> p b d", p=128), mvs[1 + s_], writes=[mvs[1 + s_]])
                    for j in range(8):
                        pb = ps[j % 4]
                        for mb in range(2):
                            k.op("pe", lambda e: e.transpose(pb[:, mb * 128:(mb + 1) * 128], mtm[:, mb, j * 128:(j + 1) * 128], self.ident[:]), reads=[mtm, self.ident], writes=[pb])
                        if j % 2 == 0:
                            k.op("act", lambda e: e.copy(out=mkT[1 + s_][:, j, :], in_=pb[:, 0:256]), reads=[pb], writes=[mkT[1 + s_]])
                        else:
                            k.op("dve", lambda e: e.tensor_copy(out=mkT[1 + s_][:, j, :], in_=pb[:, 0:256]), reads=[pb], writes=[mkT[1 + s_]])
                k.barrier()
                for b_ in [wmk, wmv, mtm] + otm:
                    k.release_dsem(b_)
            xt = k.sb([128, 8, WM], F32, es=es)
            ym = k.sb([128, 8, WM], BF16, es=es)
            ht = k.sb([128, 8, WM], BF16, n=8, es=es)
            sq = k.sb([128, 8, WM], BF16, n=8, es=es)
            rstd = k.sb([128, WM], F32, es=es)
            qx = k.sb([128, 8, WM], BF16, n=8, es=es)
            ox = k.sb([128, 8, WM], BF16, n=8, es=es)
            E = [k.sb([128, WM], BF16, es=es) for _ in range(4)]
            rl = k.sb([128, WM], F32, es=es)
            for (t0, W) in (self.tiles(WM) if "xtok" not in cfg.skip else []):
                if t0 < SEQ:
                    segs = [(0, W, mkT[0], mvs[0])]
                else:
                    segs = [(s_ * 64, 64, mkT[1 + s_], mvs[1 + s_]) for s_ in range(NSS)]
                k.dma("sp", xt[:, :, :W], xTv[:, :, t0:t0 + W], xt, writes=[xt])
                k.dma("sp", ym[:, :, :W], ymT[:, :, t0:t0 + W], ym, writes=[ym])
                for m in range(8):
                    pb = ps[m % 2]
                    for c in range(8):
                        k.op("pe", lambda e: e.matmul(pb[:, :W], wout[:, c, m * 128:(m + 1) * 128], ym[:, c, :W], start=(c == 0), stop=(c == 7)), reads=[wout, ym], writes=[pb])
                    k.op("dve", lambda e: e.tensor_tensor(out=xt[:, m, :W], in0=pb[:, :W], in1=xt[:, m, :W], op=ALU.add), reads=[pb, xt], writes=[xt])
                self.rmsnorm_fm(xt, ht, W, lambda c: self.pv[:, l, 24 + c:25 + c], sq, rstd)
                for j in range(8):
                    pb = ps[2 + j % 2]
                    for c in range(8):
                        k.op("pe", lambda e: e.matmul(pb[:, :W], wmq[:, c, j * 128:(j + 1) * 128], ht[:, c, :W], start=(c == 0), stop=(c == 7)), reads=[wmq, ht.subs[c]], writes=[pb])
                    if j % 2 == 0:
                        k.op("act", lambda e: e.copy(out=qx[:, j, :W], in_=pb[:, :W]), reads=[pb], writes=[qx.subs[j]])
                    else:
                        k.op("dve", lambda e: e.tensor_copy(out=qx[:, j, :W], in_=pb[:, :W]), reads=[pb], writes=[qx.subs[j]])
                for (c0, n, mk_, mv_) in segs:
                    for h in range(4):
                        es_ = []
                        for mb in range(2):
                            st = ps[(h % 2) * 2 + mb]
                            e_ = E[(h % 2) * 2 + mb]
                            for dc in range(2):
                                k.op("pe", lambda e: e.matmul(st[:, :n], mk_[:, h * 2 + dc, mb * 128:(mb + 1) * 128], qx[:, h * 2 + dc, c0:c0 + n], start=(dc == 0), stop=(dc == 1)),
                                     reads=[mk_, qx.subs[h * 2 + dc]], writes=[st])
                            k.op("act", lambda e: e.activation(out=e_[:, :n], in_=st[:, :n], func=AF.Exp, scale=1.0 / 16.0), reads=[st], writes=[e_])
                            es_.append(e_)
                        pl = ps[4]
                        for mb in range(2):
                            k.op("pe", lambda e: e.matmul(pl[:, :n], oneb[:], es_[mb][:, :n], start=(mb == 0), stop=(mb == 1)), reads=[oneb, es_[mb]], writes=[pl])
                        k.op("dve", lambda e: e.reciprocal(out=rl[:, :n], in_=pl[:, :n]), reads=[pl], writes=[rl])
                        for dc in range(2):
                            po = ps[5 + dc]
                            for mb in range(2):
                                k.op("pe", lambda e: e.matmul(po[:, :n], mv_[:, mb, h * 256 + dc * 128:h * 256 + (dc + 1) * 128], es_[mb][:, :n], start=(mb == 0), stop=(mb == 1)), reads=[mv_, es_[mb]], writes=[po])
                            k.op("dve", lambda e: e.tensor_tensor(out=ox[:, h * 2 + dc, c0:c0 + n], in0=po[:, :n], in1=rl[:, :n], op=ALU.mult), reads=[po, rl], writes=[ox.subs[h * 2 + dc]])
                for m in range(8):
                    pb = ps[m % 2]
                    for j in range(8):
                        k.op("pe", lambda e: e.matmul(pb[:, :W], wmo[:, j, m * 128:(m + 1) * 128], ox[:, j, :W], start=(j == 0), stop=(j == 7)), reads=[wmo, ox.subs[j]], writes=[pb])
                    k.op("dve", lambda e: e.tensor_tensor(out=xt[:, m, :W], in0=pb[:, :W], in1=xt[:, m, :W], op=ALU.add), reads=[pb, xt], writes=[xt])
                k.dma("sp", xTv[:, :, t0:t0 + W], xt[:, :, :W], xt, reads=[xt])
            k.barrier()
            for b_ in [wout, wmq, wmo, xt, ym] + mvs:
                k.release_dsem(b_)

    def phase_out(self):
        k, cfg = self.k, self.cfg
        xTv = self.xT.rearrange("(c p) t -> p c t", p=128)
        W = 512
        with ExitStack() as es:
            self.epsb = k.sb([128, 1], F32, es=es)
            k.op("dve", lambda e: e.memset(self.epsb[:], EPS), writes=[self.epsb])
            xt = k.sb([128, 8, W], F32, es=es)
            sq = k.sb([128, 8, W], BF16, n=8, es=es)
            rstd = k.sb([128, W], F32, es=es)
            yt = k.sb([128, 8, W], F32, n=8, es=es)
            ob = [k.sb([128, D], F32, es=es) for _ in range(2)]
            nb = 0
            for (t0, Wt) in self.tiles(W):
                k.dma("sp", xt[:, :, :Wt], xTv[:, :, t0:t0 + Wt], xt, writes=[xt])
                self.rmsnorm_fm(xt, yt, Wt, lambda c: self.fing[:, c:c + 1], sq, rstd, gbuf=self.fing)
                for tb in range(Wt // 128):
                    o_ = ob[nb % 2]
                    for g in range(2):
                        pb = self.ps[4 + (nb * 2 + g) % 4]
                        for c in range(4):
                            cc = g * 4 + c
                            k.op("pe", lambda e: e.transpose(pb[:, c * 128:(c + 1) * 128], yt[:, cc, tb * 128:(tb + 1) * 128], self.ident[:]),
                                 reads=[yt.subs[cc], self.ident], writes=[pb])
                        if g == 0:
                            k.op("act", lambda e: e.copy(out=o_[:, 0:512], in_=pb[:]), reads=[pb], writes=[o_])
                        else:
                            k.op("dve", lambda e: e.tensor_copy(out=o_[:, 512:1024], in_=pb[:]), reads=[pb], writes=[o_])
                    tg = t0 + tb * 128
                    if tg < cfg.seq:
                        dst = self.y_p[tg:tg + 128, :]
                    else:
                        dst = self.y_s[tg - cfg.seq:tg - cfg.seq + 128, :]
                    k.dma("sp", dst, o_[:], o_, reads=[o_])
                    nb += 1
            k.barrier()
            for b_ in [xt] + ob:
                k.release_dsem(b_)


def host_consts():
    c = np.zeros((128, 512), np.float32)
    c[:, 0:128] = np.eye(128, dtype=np.float32)
    s_ = np.arange(128)[:, None] % 64
    t_ = np.arange(64)[None, :]
    c[:, 128:192] = (s_ < t_)
    c[:, 192:256] = (s_ <= t_)
    c[:, 256:320] = (s_ > t_)
    blk = (np.arange(128)[:, None] // 64) == (np.arange(128)[None, :] // 64)
    c[:, 384:512] = blk
    return c


def fm_vec(v):
    return np.ascontiguousarray(v.reshape(-1, 128).T)


def make_in_maps(cfg, inp, n_cores=8):
    L = cfg.depth
    pvec = np.zeros((L, 128, 128), np.float32)
    for l in range(L):
        pvec[l, :, 0:8] = fm_vec(inp["ffn1_norm"][l])
        pvec[l, :, 8:16] = fm_vec(inp["ffn2_norm"][l])
        pvec[l, :, 16:24] = fm_vec(inp["mix_norm"][l])
        pvec[l, :, 24:32] = fm_vec(inp["xattn_norm"][l])
        pvec[l, :, 32:40] = fm_vec(inp["mem_kv_norm"][l])
        pvec[l, :, 40:42] = fm_vec(inp["q_norm"][l])
        pvec[l, :, 42:43] = fm_vec(inp["kv_norm"][l])
        pvec[l, :, 48:62] = fm_vec(inp["shift_mu"][l])
        for j, n_ in enumerate(["w0", "a0", "k_k", "k_a", "r_k", "gn_gain", "gn_bias"]):
            pvec[l, :, 62 + 4 * j:66 + 4 * j] = fm_vec(inp[n_][l].reshape(-1))
    w_in = inp["w_in"][:L]
    pos = np.concatenate([np.arange(cfg.seq)] + [cfg.past + np.arange(64)] * cfg.nss).astype(np.float32)
    inv = (np.float32(10000.0) ** (-np.arange(16, dtype=np.float32) / np.float32(16))).astype(np.float32)
    ang = (pos[None, :] * inv[:, None]).astype(np.float32)
    cosf = np.tile(np.cos(ang).astype(np.float32), (8, 1))
    sinf = np.tile(np.sin(ang).astype(np.float32), (8, 1))
    shared = {
        "consts": host_consts(), "pvec": pvec, "fin_g": fm_vec(inp["final_norm"]),
        "w_gate1": inp["ffn1_w_gate"][:L], "w_up1": inp["ffn1_w_up"][:L], "w_down1": inp["ffn1_w_down"][:L],
        "w_gate2": inp["ffn2_w_gate"][:L], "w_up2": inp["ffn2_w_up"][:L], "w_down2": inp["ffn2_w_down"][:L],
        "w_cq": w_in[:, :, 0:256], "w_ckv": w_in[:, :, 256:384], "w_kr4": np.tile(w_in[:, :, 384:416], (1, 1, 4)),
        "w_pb": w_in[:, :, 416:], "w_uq": inp["w_uq"][:L],
        "w_ukT": inp["w_uk"][:L].transpose(0, 3, 1, 2), "w_uv": inp["w_uv"][:L].transpose(0, 2, 1, 3).reshape(L, 128, 512),
        "rope_cs": np.stack([cosf, sinf]),
        "w_wup": inp["w_up"][:L], "w_aup": inp["a_up"][:L], "w_gup": inp["g_up"][:L],
        "w_out": inp["w_out"][:L], "w_mq": inp["w_mq"][:L], "w_mk": inp["w_mk"][:L], "w_mv": inp["w_mv"][:L], "w_mo": inp["w_mo"][:L],
    }
    shared = {k_: np.ascontiguousarray(v, dtype=np.float32) for k_, v in shared.items()}
    maps = []
    nb = inp["x_prompt"].shape[0]
    for c in range(n_cores):
        m = dict(shared)
        b = c % nb
        m["x_p"] = np.ascontiguousarray(inp["x_prompt"][b, :cfg.seq])
        sl = slice(c * cfg.nss, (c + 1) * cfg.nss)
        m["x_s"] = np.ascontiguousarray(inp["x_sample"][sl].reshape(cfg.nss * 64, D))
        m["st_wkv"] = np.ascontiguousarray(inp["state_wkv"][:L, sl])
        m["st_sh"] = np.ascontiguousarray(inp["state_shift"][:L, sl, 0].reshape(L, cfg.nss, 14, 128).transpose(0, 1, 3, 2))
        m["mem_p"] = np.ascontiguousarray(inp["mem_prompt"][b])
        m["c_mk"] = np.ascontiguousarray(inp["cache_mem_k"][:L, sl].reshape(L, cfg.nss, N_MEM, D))
        m["c_mv"] = np.ascontiguousarray(inp["cache_mem_v"][:L, sl].reshape(L, cfg.nss, N_MEM, D))
        m["c_ckv"] = np.ascontiguousarray(inp["cache_ckv"][:L, sl])
        m["c_kr"] = np.ascontiguousarray(inp["cache_krope"][:L, sl])
        maps.append(m)
    return maps


def run(cfg, inp, n_cores=8):
    prog = Prog(cfg)
    nc = prog.build()
    maps = make_in_maps(cfg, inp, n_cores)
    res = run_bass_kernel_spmd(nc, maps, core_ids=list(range(n_cores)))
    return res.results


def assemble(cfg, res, n_cores=8, nb=4):
    L, nss = cfg.depth, cfg.nss
    g = lambda c, n, shp: res[c][n] if n in res[c] else np.zeros(shp, np.float32)
    P = range(nb)
    C = range(n_cores)
    y_p = np.stack([res[b]["y_p"] for b in P])
    y_s = np.concatenate([res[c]["y_s"].reshape(nss, 64, D) for c in C])
    ckv_p = np.stack([res[b]["o_ckv_p"] for b in P], axis=1)
    kr_p = np.stack([res[b]["o_kr_p"] for b in P], axis=1)
    mk_p = np.stack([res[b]["o_mk"].reshape(L, N_MEM, MEM_HEADS, MEM_HD) for b in P], axis=1)
    mv_p = np.stack([res[b]["o_mv"].reshape(L, N_MEM, MEM_HEADS, MEM_HD) for b in P], axis=1)
    wkv_p = np.stack([g(b, "o_wkv_p", (L, 8, 64, 64)) for b in P], axis=1)
    sh_p = np.stack([g(b, "o_sh_p", (L, 1, D_SHIFT)) for b in P], axis=1)
    ckv_s = np.concatenate([res[c]["o_ckv_s"].reshape(L, nss, 64, 128) for c in C], axis=1)
    kr_s = np.concatenate([res[c]["o_kr_s"].reshape(L, nss, 64, 32) for c in C], axis=1)
    wkv_s = np.concatenate([g(c, "o_wkv_s", (L, nss, 8, 64, 64)) for c in C], axis=1)
    sh_s = np.concatenate([g(c, "o_sh_s", (L, nss, 1, D_SHIFT)) for c in C], axis=1)
    outs = (y_p, y_s, ckv_p, kr_p, mk_p, mv_p, wkv_p, sh_p, ckv_s, kr_s, wkv_s, sh_s)
    return tuple(np.ascontiguousarray(o, dtype=np.float32) for o in outs)


def kernel(**inputs):
    cfg = Cfg()
    inp = {k_: np.asarray(v) for k_, v in inputs.items()}
    res = run(cfg, inp)
    return assemble(cfg, res)
```

```python
import math
from contextlib import ExitStack
import numpy as np
import concourse.bass as bass
import concourse.mybir as mybir
from concourse.bass_utils import run_bass_kernel_spmd

F32 = mybir.dt.float32
BF16 = mybir.dt.bfloat16
AF = mybir.ActivationFunctionType
ALU = mybir.AluOpType
AX = mybir.AxisListType

D = 1024
DFF = 2816
H_A = 8
DN = 64
DR = 32
DV = 64
Q_RANK = 256
KV_RANK = 128
H_B = 8
N_B = 64
D_B = 512
D_SHIFT = 1792
N_MEM = 256
MEM_HEADS = 4
MEM_HD = 256
CH = 64
EPS = 1e-6
GN_EPS = 64e-5
MLA_SCALE = 1.0 / math.sqrt(DN + DR)
DECAY_C = math.exp(-0.5)


class Cfg:
    def __init__(self, depth=4, seq=4096, nss=4, past=2048, debug=False):
        self.depth, self.seq, self.nss, self.past = depth, seq, nss, past
        self.debug = debug
        self.zero_yb = False
        self.skip = set()
        self.xstop = 9
        self.rstop = 9
        self.ts = 64
        self.tt = seq + nss * 64


class Trk:
    __slots__ = ("w", "rs", "excl")

    def __init__(self, excl=False):
        self.w = None
        self.rs = []
        self.excl = excl


class Buf:
    def __init__(self, t, n=0, excl=False):
        self.t = t
        self.trk = Trk(excl)
        self.subs = [Trk(excl) for _ in range(n)]
        self.dsem = None
        self.dcnt = 0

    def __getitem__(self, k):
        return self.t[k]


class KB:
    ENGS = ("pe", "act", "dve", "pool", "sp")

    def __init__(self, nc, es):
        self.nc, self.es = nc, es
        self.eng = {"pe": nc.tensor, "act": nc.scalar, "dve": nc.vector, "pool": nc.gpsimd, "sp": nc.sync}
        self.sem = {e: es.enter_context(nc.semaphore("sem_" + e)) for e in self.ENGS}
        self.cnt = {e: 0 for e in self.ENGS}
        self.seen = {e: {} for e in self.ENGS}
        self.semname = {}
        self.dma_pending = {}
        self.free_dsems = {}
        self.nbuf = 0

    def sb(self, shape, dt, n=0, es=None):
        self.nbuf += 1
        t = (es or self.es).enter_context(self.nc.sbuf_tensor("sb%d" % self.nbuf, list(shape), dt))
        return Buf(t, n)

    def psb(self, shape, dt, n=0, es=None):
        self.nbuf += 1
        t = (es or self.es).enter_context(self.nc.psum_tensor("ps%d" % self.nbuf, list(shape), dt))
        return Buf(t, n, excl=True)

    def _dsem(self, buf, kind):
        if buf.dsem is None:
            buf.dsem = {}
            buf.dcnt = {}
        if kind not in buf.dsem:
            fl = self.free_dsems.setdefault(kind, [])
            if fl:
                buf.dsem[kind], buf.dcnt[kind] = fl.pop()
            else:
                self.nbuf += 1
                buf.dsem[kind] = self.es.enter_context(self.nc.semaphore("dsem%d" % self.nbuf))
                buf.dcnt[kind] = 0
        return buf.dsem[kind]

    def release_dsem(self, buf):
        if buf.dsem is not None:
            for kind in buf.dsem:
                self.free_dsems.setdefault(kind, []).append((buf.dsem[kind], buf.dcnt[kind]))
            buf.dsem = None

    def _wait(self, e, ev):
        if ev is None:
            return
        sem, val = ev
        k = id(sem)
        if self.seen[e].get(k, 0) >= val:
            return
        self.seen[e][k] = val
        self.eng[e].wait_ge(sem, val)

    def _deps(self, e, reads, writes):
        need = {}

        def add(ev):
            if ev is None or (e == "pe" and ev[0] is self.sem["pe"]):
                return
            k_ = id(ev[0])
            if k_ not in need or need[k_][1] < ev[1]:
                need[k_] = ev
        for tr in reads:
            add(tr.w)
        for tr in writes:
            add(tr.w)
            for r in tr.rs:
                add(r)
        for ev in need.values():
            self._wait(e, ev)

    def _commit(self, ev, reads, writes):
        for tr in reads:
            tr.rs.append(ev)
            if len(tr.rs) > 24:
                best = {}
                for s, v in tr.rs:
                    if id(s) not in best or best[id(s)][1] < v:
                        best[id(s)] = (s, v)
                tr.rs = list(best.values())
        for tr in writes:
            tr.w = ev
            tr.rs = []

    @staticmethod
    def _trks(lst):
        out = []
        for b in lst:
            out.append(b.trk if isinstance(b, Buf) else b)
        return out

    def op(self, e, fn, reads=(), writes=()):
        reads, writes = self._trks(reads), self._trks(writes)
        writes = writes + [t for t in reads if t.excl]
        reads = [t for t in reads if not t.excl]
        self._deps(e, reads, writes)
        inst = fn(self.eng[e])
        self.cnt[e] += 1
        inst.then_inc(self.sem[e], 1)
        self._commit((self.sem[e], self.cnt[e]), reads, writes)
        return inst

    def dma(self, e, out, in_, sbuf_buf, reads=(), writes=(), **kw):
        reads, writes = self._trks(reads), self._trks(writes)
        self._deps(e, reads, writes)
        kind = "sw" if e == "pool" else "hw"
        sem = self._dsem(sbuf_buf, kind)
        inst = self.eng[e].dma_start(out=out, in_=in_, **kw)
        sbuf_buf.dcnt[kind] += 16
        inst.then_inc(sem, 16)
        ev = (sem, sbuf_buf.dcnt[kind])
        self.dma_pending[id(sem)] = ev
        self._commit(ev, reads, writes)

    def barrier(self):
        sp = "sp"
        for ev in self.dma_pending.values():
            self._wait(sp, ev)
        self.dma_pending = {}
        for e in self.ENGS:
            if e != sp and self.cnt[e] > 0:
                self._wait(sp, (self.sem[e], self.cnt[e]))
        self.eng[sp].sem_inc(self.sem[sp], 1)
        self.cnt[sp] += 1
        for e in self.ENGS:
            if e != sp:
                self._wait(e, (self.sem[sp], self.cnt[sp]))


def col_groups(w, g=512):
    return [(s, min(g, w - s)) for s in range(0, w, g)]


class Prog:
    def __init__(self, cfg):
        self.cfg = cfg
        self.nc = bass.Bass("TRN2", target_bir_lowering=False)
        self.inputs = {}
        self.outputs = {}

    def din(self, name, shape, dt=F32):
        ap = self.nc.dram_tensor(name, list(shape), dt, kind="ExternalInput").ap()
        self.inputs[name] = ap
        return ap

    def dout(self, name, shape, dt=F32):
        ap = self.nc.dram_tensor(name, list(shape), dt, kind="ExternalOutput").ap()
        self.outputs[name] = ap
        return ap

    def dscr(self, name, shape, dt):
        return self.nc.dram_tensor(name, list(shape), dt, kind="Internal").ap()

    def build(self):
        cfg = self.cfg
        nc = self.nc
        L, SEQ, NSS, PAST, TT = cfg.depth, cfg.seq, cfg.nss, cfg.past, cfg.tt
        I = self.din
        self.x_p = I("x_p", [SEQ, D])
        self.x_s = I("x_s", [NSS * 64, D])
        self.consts = I("consts", [128, 512])
        self.pvec = I("pvec", [L, 128, 128])
        self.fin_g = I("fin_g", [128, 8])
        self.w_gate = [I("w_gate%d" % f, [L, D, DFF]) for f in (1, 2)]
        self.w_up = [I("w_up%d" % f, [L, D, DFF]) for f in (1, 2)]
        self.w_down = [I("w_down%d" % f, [L, DFF, D]) for f in (1, 2)]
        self.w_cq = I("w_cq", [L, D, 256])
        self.w_ckv = I("w_ckv", [L, D, 128])
        self.w_kr4 = I("w_kr4", [L, D, 128])
        self.w_pb = I("w_pb", [L, D, D_SHIFT])
        self.w_uq = I("w_uq", [L, 256, 768])
        self.w_ukT = I("w_ukT", [L, 64, 8, 128])
        self.w_uv = I("w_uv", [L, 128, 512])
        self.rope_cs = I("rope_cs", [2, 128, TT])
        self.w_out = I("w_out", [L, D, D])
        self.w_mq = I("w_mq", [L, D, D])
        self.w_mk = I("w_mk", [L, D, D])
        self.w_mv = I("w_mv", [L, D, D])
        self.w_mo = I("w_mo", [L, D, D])
        self.w_wup = I("w_wup", [L, 64, 512])
        self.w_aup = I("w_aup", [L, 64, 512])
        self.w_gup = I("w_gup", [L, 128, 512])
        self.st_wkv = I("st_wkv", [L, NSS, 8, 64, 64])
        self.st_sh = I("st_sh", [L, NSS, 128, 14])
        self.mem_p = I("mem_p", [N_MEM, D])
        self.c_mk = I("c_mk", [L, NSS, N_MEM, D])
        self.c_mv = I("c_mv", [L, NSS, N_MEM, D])
        self.c_ckv = I("c_ckv", [L, NSS, PAST, 128])
        self.c_kr = I("c_kr", [L, NSS, PAST, 32])
        O = self.dout
        self.y_p = O("y_p", [SEQ, D])
        self.y_s = O("y_s", [NSS * 64, D])
        self.o_ckv_p = O("o_ckv_p", [L, SEQ, 128])
        self.o_kr_p = O("o_kr_p", [L, SEQ, 32])
        self.o_mk = O("o_mk", [L, N_MEM, D])
        self.o_mv = O("o_mv", [L, N_MEM, D])
        self.o_wkv_p = O("o_wkv_p", [L, 8, 64, 64])
        self.o_sh_p = O("o_sh_p", [L, 1, D_SHIFT])
        self.o_wkv_s = O("o_wkv_s", [L, NSS, 8, 64, 64])
        self.o_sh_s = O("o_sh_s", [L, NSS, 1, D_SHIFT])
        self.o_ckv_s = O("o_ckv_s", [L, NSS * 64, 128])
        self.o_kr_s = O("o_kr_s", [L, NSS * 64, 32])
        self.xT = self.dscr("xT", [D, TT], F32)
        self.qlatT = self.dscr("qlatT", [8 * 128, TT], BF16)
        self.qropeT = self.dscr("qropeT", [2, 128, TT], BF16)
        self.ckvT = self.dscr("ckvT", [128, TT], BF16)
        self.kropeT = self.dscr("kropeT", [128, TT], BF16)
        self.pbT = self.dscr("pbT", [D_SHIFT, TT], F32)
        if cfg.debug:
            self.ymixT = O("ymixT", [D, TT], BF16)
        else:
            self.ymixT = self.dscr("ymixT", [D, TT], BF16)

        with ExitStack() as es:
            k = self.k = KB(nc, es)
            self.ident = k.sb([128, 128], F32)
            self.onesb = k.sb([128, 128], BF16)
            self.ones1 = k.sb([128, 128], F32)
            k.dma("sp", self.ident[:], self.consts[:, 0:128], self.ident, writes=[self.ident])
            k.op("dve", lambda e: e.memset(self.onesb[:], 1.0 / 1024), writes=[self.onesb])
            k.op("dve", lambda e: e.memset(self.ones1[:], 1.0), writes=[self.ones1])
            self.cmask = k.sb([128, 384], F32)
            k.dma("sp", self.cmask[:], self.consts[:, 128:512], self.cmask, writes=[self.cmask])
            self.fing = k.sb([128, 8], F32)
            k.dma("sp", self.fing[:], self.fin_g, self.fing, writes=[self.fing])
            self.pv = k.sb([128, L, 128], F32)
            k.dma("sp", self.pv[:], self.pvec.rearrange("l p c -> p l c"), self.pv, writes=[self.pv])
            self.ps = [k.psb([128, 512], F32) for _ in range(8)]
            k.barrier()

            self.phase_in()
            if cfg.zero_yb:
                with ExitStack() as es0:
                    z = k.sb([128, 4, TT], BF16, es=es0)
                    k.op("dve", lambda e: e.memset(z[:], 0.0), writes=[z])
                    k.dma("sp", self.ymixT[512:1024, :].rearrange("(c p) t -> p c t", p=128), z[:], z, reads=[z])
                    k.barrier()
                    k.release_dsem(z)
            for l in range(L):
                self.phase_ffn(l, 0)
                self.phase_proj(l)
                self.phase_mla(l)
                if "rwkv" not in cfg.skip:
                    self.phase_rwkv(l)
                if "xattn" not in cfg.skip:
                    self.phase_xattn(l)
                self.phase_ffn(l, 1)
            self.phase_out()
            k.barrier()
        return nc

    def warm(self, buf, lhsT, rhs, n=28, bank=7):
        k = self.k
        pb = self.ps[bank]
        for i in range(n):
            k.op("pe", lambda e: e.matmul(pb[:, :], lhsT, rhs, start=True, stop=True), reads=[buf], writes=[pb])

    def tiles(self, w=1024):
        cfg = self.cfg
        out = []
        for s in range(0, cfg.seq, w):
            out.append((s, min(w, cfg.seq - s)))
        out.append((cfg.seq, cfg.nss * 64))
        return out

    def phase_in(self):
        k, cfg = self.k, self.cfg
        with ExitStack() as es:
            NB = 2
            tin = [k.sb([128, D], F32, es=es) for _ in range(NB)]
            tout = [k.sb([128, 8, 128], F32, es=es) for _ in range(NB)]
            blocks = [(self.x_p, i * 128, i * 128) for i in range(cfg.seq // 128)]
            blocks += [(self.x_s, i * 128, cfg.seq + i * 128) for i in range(cfg.nss * 64 // 128)]
            for bi, (src, r0, c0) in enumerate(blocks):
                a, b = tin[bi % NB], tout[bi % NB]
                k.dma("sp", a[:], src[r0:r0 + 128, :], a, writes=[a])
                for g in range(2):
                    pb = self.ps[(bi * 2 + g) % 8]
                    for c in range(4):
                        cc = g * 4 + c
                        k.op("pe", lambda e: e.transpose(pb[:, c * 128:(c + 1) * 128], a[:, cc * 128:(cc + 1) * 128], self.ident[:]),
                             reads=[a, self.ident], writes=[pb])
                    eng = "act" if g == 0 else "dve"
                    if eng == "act":
                        k.op("act", lambda e: e.copy(out=b[:, g * 4:(g + 1) * 4, :], in_=pb[:].rearrange("p (c t) -> p c t", c=4)), reads=[pb], writes=[b])
                    else:
                        k.op("dve", lambda e: e.tensor_copy(out=b[:, g * 4:(g + 1) * 4, :], in_=pb[:].rearrange("p (c t) -> p c t", c=4)), reads=[pb], writes=[b])
                k.dma("sp", self.xT.rearrange("(c p) t -> p c t", p=128)[:, :, c0:c0 + 128], b[:], b, reads=[b])
            k.barrier()
            for b_ in tin + tout:
                k.release_dsem(b_)

    def rmsnorm_fm(self, xt, ht, W, gcol, sq, rstd, gbuf=None):
        k = self.k
        gbuf = gbuf or self.pv
        for c in range(8):
            k.op("act", lambda e: e.activation(out=sq[:, c, :W], in_=xt[:, c, :W], func=AF.Square), reads=[xt], writes=[sq.subs[c]])
        for gi, (s0, n) in enumerate(col_groups(W)):
            pb = self.ps[gi % 2]
            for c in range(8):
                k.op("pe", lambda e: e.matmul(pb[:, :n], self.onesb[:], sq[:, c, s0:s0 + n], start=(c == 0), stop=(c == 7)),
                     reads=[self.onesb, sq.subs[c]], writes=[pb])
            k.op("act", lambda e: e.activation(out=rstd[:, s0:s0 + n], in_=pb[:, :n], func=AF.Sqrt, bias=self.epsb[:, 0:1], scale=1.0), reads=[pb, self.epsb], writes=[rstd])
        k.op("dve", lambda e: e.reciprocal(out=rstd[:, :W], in_=rstd[:, :W]), reads=[rstd], writes=[rstd])
        for c in range(8):
            k.op("dve", lambda e: e.scalar_tensor_tensor(out=ht[:, c, :W], in0=xt[:, c, :W], scalar=gcol(c), in1=rstd[:, :W], op0=ALU.mult, op1=ALU.mult),
                 reads=[xt, rstd, gbuf], writes=[ht.subs[c]])

    def phase_ffn(self, l, f):
        k, cfg = self.k, self.cfg
        WMAX = 1024
        xTv = self.xT.rearrange("(c p) t -> p c t", p=128)
        with ExitStack() as es:
            self.epsb = k.sb([128, 1], F32, es=es)
            k.op("dve", lambda e: e.memset(self.epsb[:], EPS), writes=[self.epsb])
            wd = k.sb([128, 22, D], BF16, es=es)
            wdv = self.w_down[f][l].rearrange("(c p) n -> p c n", p=128)
            for c0 in range(0, 22, 2):
                k.dma("pool", wd[:, c0:c0 + 2, :], wdv[:, c0:c0 + 2, :], wd, writes=[wd])
            xt = k.sb([128, 8, WMAX], F32, es=es)
            ht = k.sb([128, 8, WMAX], BF16, n=8, es=es)
            sq = k.sb([128, 8, WMAX], BF16, n=8, es=es)
            rstd = k.sb([128, WMAX], F32, es=es)
            at = k.sb([128, 22, WMAX], BF16, n=22, es=es)
            sg = [k.sb([128, WMAX], F32, es=es) for _ in range(2)]
            NWB = 3
            wg = [k.sb([128, 8, 256], BF16, es=es) for _ in range(NWB)]
            wu = [k.sb([128, 8, 256], BF16, es=es) for _ in range(NWB)]
            wgv = self.w_gate[f][l].rearrange("(c p) n -> p c n", p=128)
            wuv = self.w_up[f][l].rearrange("(c p) n -> p c n", p=128)
            nblk = 0
            for (t0, W) in self.tiles(WMAX):
                k.dma("sp", xt[:, :, :W], xTv[:, :, t0:t0 + W], xt, writes=[xt])
                self.rmsnorm_fm(xt, ht, W, lambda c: self.pv[:, l, f * 8 + c:f * 8 + c + 1], sq, rstd)
                cg = col_groups(W)
                for jb in range(11):
                    g_, u_ = wg[nblk % NWB], wu[nblk % NWB]
                    nblk += 1
                    k.dma("pool", g_[:], wgv[:, :, jb * 256:(jb + 1) * 256], g_, writes=[g_])
                    k.dma("pool", u_[:], wuv[:, :, jb * 256:(jb + 1) * 256], u_, writes=[u_])
                    for jj in range(2):
                        j = jb * 2 + jj
                        par = j % 2
                        pg = [self.ps[par * 4 + i] for i in range(2)]
                        pu = [self.ps[par * 4 + 2 + i] for i in range(2)]
                        for gi, (s0, n) in enumerate(cg):
                            for c in range(8):
                                k.op("pe", lambda e: e.matmul(pg[gi][:, :n], g_[:, c, jj * 128:(jj + 1) * 128], ht[:, c, s0:s0 + n], start=(c == 0), stop=(c == 7)),
                                     reads=[g_, ht.subs[c]], writes=[pg[gi]])
                            for c in range(8):
                                k.op("pe", lambda e: e.matmul(pu[gi][:, :n], u_[:, c, jj * 128:(jj + 1) * 128], ht[:, c, s0:s0 + n], start=(c == 0), stop=(c == 7)),
                                     reads=[u_, ht.subs[c]], writes=[pu[gi]])
                        s_ = sg[par]
                        for gi, (s0, n) in enumerate(cg):
                            k.op("act", lambda e: e.activation(out=s_[:, s0:s0 + n], in_=pg[gi][:, :n], func=AF.Silu), reads=[pg[gi]], writes=[s_])
                            k.op("dve", lambda e: e.tensor_tensor(out=at[:, j, s0:s0 + n], in0=s_[:, s0:s0 + n], in1=pu[gi][:, :n], op=ALU.mult),
                                 reads=[s_, pu[gi]], writes=[at.subs[j]])
                for m in range(8):
                    par = m % 2
                    pd = [self.ps[par * 4 + i] for i in range(2)]
                    for gi, (s0, n) in enumerate(cg):
                        for c in range(22):
                            k.op("pe", lambda e: e.matmul(pd[gi][:, :n], wd[:, c, m * 128:(m + 1) * 128], at[:, c, s0:s0 + n], start=(c == 0), stop=(c == 21)),
                                 reads=[wd, at.subs[c]], writes=[pd[gi]])
                        k.op("dve", lambda e: e.scalar_tensor_tensor(out=xt[:, m, s0:s0 + n], in0=pd[gi][:, :n], scalar=0.5, in1=xt[:, m, s0:s0 + n], op0=ALU.mult, op1=ALU.add),
                             reads=[pd[gi], xt], writes=[xt])
                k.dma("sp", xTv[:, :, t0:t0 + W], xt[:, :, :W], xt, reads=[xt])
            k.barrier()
            for b_ in [wd, xt] + wg + wu:
                k.release_dsem(b_)


    def small_norm(self, src, dst, nch, W, ones_ap, gcol, sq, rstd, pbank, gbuf=None):
        k = self.k
        gbuf = gbuf or self.pv
        for c in range(nch):
            k.op("act", lambda e: e.activation(out=sq[:, c, :W], in_=src[:, c, :W], func=AF.Square), reads=[src], writes=[sq])
        for c in range(nch):
            k.op("pe", lambda e: e.matmul(pbank[:, :W], ones_ap, sq[:, c, :W], start=(c == 0), stop=(c == nch - 1)),
                 reads=[self.onesv, sq], writes=[pbank])
        k.op("act", lambda e: e.activation(out=rstd[:, :W], in_=pbank[:, :W], func=AF.Sqrt, bias=self.epsb[:, 0:1], scale=1.0), reads=[pbank, self.epsb], writes=[rstd])
        k.op("dve", lambda e: e.reciprocal(out=rstd[:, :W], in_=rstd[:, :W]), reads=[rstd], writes=[rstd])
        for c in range(nch):
            k.op("dve", lambda e: e.scalar_tensor_tensor(out=dst[:, c, :W], in0=src[:, c, :W], scalar=gcol(c), in1=rstd[:, :W], op0=ALU.mult, op1=ALU.mult),
                 reads=[src, rstd, gbuf], writes=[dst])

    def phase_proj(self, l):
        k, cfg = self.k, self.cfg
        WM = 512
        xTv = self.xT.rearrange("(c p) t -> p c t", p=128)
        with ExitStack() as es:
            self.epsb = k.sb([128, 1], F32, es=es)
            k.op("dve", lambda e: e.memset(self.epsb[:], EPS), writes=[self.epsb])
            self.onesv = k.sb([128, 2, 128], BF16, es=es)
            k.op("dve", lambda e: e.memset(self.onesv[:, 0, :], 1.0 / 256), writes=[self.onesv])
            k.op("dve", lambda e: e.memset(self.onesv[:, 1, :], 1.0 / 128), writes=[self.onesv])
            fm = lambda ap: ap.rearrange("(c p) n -> p c n", p=128)
            wcq = k.sb([128, 8, 256], BF16, es=es)
            wckv = k.sb([128, 8, 128], BF16, es=es)
            wkr = k.sb([128, 8, 128], BF16, es=es)
            wkrr = k.sb([128, 8, 128], BF16, es=es)
            wpb = k.sb([128, 8, D_SHIFT], BF16, es=es)
            wuq = k.sb([128, 2, 768], BF16, es=es)
            wqro = k.sb([128, 2, 256], BF16, es=es)
            wqrr = k.sb([128, 2, 256], BF16, es=es)
            wuk = k.sb([64, 8, 128], BF16, es=es)
            k.dma("pool", wcq[:], fm(self.w_cq[l]), wcq, writes=[wcq])
            k.dma("pool", wckv[:], fm(self.w_ckv[l]), wckv, writes=[wckv])
            k.dma("pool", wkr[:], fm(self.w_kr4[l]), wkr, writes=[wkr])
            for c0 in range(0, 8, 2):
                k.dma("pool", wpb[:, c0:c0 + 2, :], fm(self.w_pb[l])[:, c0:c0 + 2, :], wpb, writes=[wpb])
            k.dma("pool", wuq[:], fm(self.w_uq[l]), wuq, writes=[wuq])
            k.dma("pool", wuk[:], self.w_ukT[l], wuk, writes=[wuk])
            wuq_r = wuq[:].rearrange("p c (h d) -> p c h d", d=96)[:, :, :, 64:96]
            wqro_v = wqro[:].rearrange("p c (h d) -> p c h d", d=32)
            wqrr_v = wqrr[:].rearrange("p c (h d) -> p c h d", d=32)
            for c in range(2):
                k.op("act", lambda e: e.copy(out=wqro_v[:, c], in_=wuq_r[:, c]), reads=[wuq], writes=[wqro])
                k.op("act", lambda e: e.mul(wqrr_v[:, c, :, 0:16], wuq_r[:, c, :, 16:32], -1.0), reads=[wuq], writes=[wqrr])
                k.op("act", lambda e: e.copy(out=wqrr_v[:, c, :, 16:32], in_=wuq_r[:, c, :, 0:16]), reads=[wuq], writes=[wqrr])
            wkr_v = wkr[:].rearrange("p c (h d) -> p c h d", d=32)
            wkrr_v = wkrr[:].rearrange("p c (h d) -> p c h d", d=32)
            for c0 in range(0, 8, 4):
                k.op("act", lambda e: e.mul(wkrr_v[:, c0:c0 + 4, :, 0:16].rearrange("p c h d -> p (c h) d"), wkr_v[:, c0:c0 + 4, :, 16:32].rearrange("p c h d -> p (c h) d"), -1.0), reads=[wkr], writes=[wkrr])
                k.op("act", lambda e: e.copy(out=wkrr_v[:, c0:c0 + 4, :, 16:32].rearrange("p c h d -> p (c h) d"), in_=wkr_v[:, c0:c0 + 4, :, 0:16].rearrange("p c h d -> p (c h) d")), reads=[wkr], writes=[wkrr])
            xt = k.sb([128, 8, WM], F32, es=es)
            ht = k.sb([128, 8, WM], BF16, n=8, es=es)
            sq = k.sb([128, 8, WM], BF16, n=8, es=es)
            rstd = k.sb([128, WM], F32, es=es)
            cq = k.sb([128, 2, WM], F32, es=es)
            cqn = k.sb([128, 2, WM], BF16, es=es)
            sq2 = k.sb([128, 2, WM], BF16, es=es)
            rstd2 = k.sb([128, WM], F32, es=es)
            qn = k.sb([64, 8, WM], BF16, n=8, es=es)
            qlat = k.sb([128, 8, WM], BF16, n=8, es=es)
            qro = k.sb([128, 2, WM], BF16, es=es)
            cs = k.sb([128, 2, WM], F32, es=es)
            t1 = k.sb([128, WM], F32, es=es)
            t2 = k.sb([128, WM], F32, es=es)
            ckv = k.sb([128, 1, WM], F32, es=es)
            ckvn = k.sb([128, 1, WM], F32, es=es)
            ckvb = k.sb([128, WM], BF16, es=es)
            krf = k.sb([128, WM], F32, es=es)
            krb = k.sb([128, WM], BF16, es=es)
            tmo = [k.sb([128, 160], F32, es=es) for _ in range(2)]
            pbs = k.sb([128, 14, WM], F32, n=14, es=es)
            ps = self.ps
            ntm = 0
            for (t0, W) in self.tiles(WM):
                k.dma("sp", xt[:, :, :W], xTv[:, :, t0:t0 + W], xt, writes=[xt])
                k.dma("sp", cs[:, :, :W], self.rope_cs.rearrange("g p t -> p g t")[:, :, t0:t0 + W], cs, writes=[cs])
                self.rmsnorm_fm(xt, ht, W, lambda c: self.pv[:, l, 16 + c:17 + c], sq, rstd)

                def lin(pb, wt, c0, m, rows=128):
                    for c in range(8):
                        k.op("pe", lambda e: e.matmul(pb[0:rows, :W], wt[:, c, c0:c0 + m], ht[:, c, :W], start=(c == 0), stop=(c == 7)),
                             reads=[wt, ht.subs[c]], writes=[pb])
                for c2 in range(2):
                    lin(ps[c2], wcq, c2 * 128, 128)
                    k.op("act", lambda e: e.copy(out=cq[:, c2, :W], in_=ps[c2][:, :W]), reads=[ps[c2]], writes=[cq])
                self.small_norm(cq, cqn, 2, W, self.onesv[:, 0, :], lambda c: self.pv[:, l, 40 + c:41 + c], sq2, rstd2, ps[2])
                for h in range(8):
                    pb = ps[h % 2]
                    for c in range(2):
                        k.op("pe", lambda e: e.matmul(pb[0:64, :W], wuq[:, c, h * 96:h * 96 + 64], cqn[:, c, :W], start=(c == 0), stop=(c == 1)),
                             reads=[wuq, cqn], writes=[pb])
                    k.op("act", lambda e: e.copy(out=qn[:, h, :W], in_=pb[0:64, :W]), reads=[pb], writes=[qn.subs[h]])
                for h in range(8):
                    pl = ps[2 + h % 2]
                    k.op("pe", lambda e: e.matmul(pl[:, :W], wuk[:, h, :], qn[:, h, :W], start=True, stop=True), reads=[wuk, qn.subs[h]], writes=[pl])
                    k.op("dve", lambda e: e.tensor_copy(out=qlat[:, h, :W], in_=pl[:, :W]), reads=[pl], writes=[qlat.subs[h]])
                k.dma("sp", self.qlatT.rearrange("(h c) t -> c h t", c=128)[:, :, t0:t0 + W], qlat[:, :, :W], qlat, reads=qlat.subs)
                for g in range(2):
                    pr, pt = ps[4 + g], ps[6 + g]
                    for c in range(2):
                        k.op("pe", lambda e: e.matmul(pr[:, :W], wqro[:, c, g * 128:(g + 1) * 128], cqn[:, c, :W], start=(c == 0), stop=(c == 1)), reads=[wqro, cqn], writes=[pr])
                    for c in range(2):
                        k.op("pe", lambda e: e.matmul(pt[:, :W], wqrr[:, c, g * 128:(g + 1) * 128], cqn[:, c, :W], start=(c == 0), stop=(c == 1)), reads=[wqrr, cqn], writes=[pt])
                    k.op("dve", lambda e: e.tensor_tensor(out=t1[:, :W], in0=pr[:, :W], in1=cs[:, 0, :W], op=ALU.mult), reads=[pr, cs], writes=[t1])
                    k.op("dve", lambda e: e.tensor_tensor(out=t2[:, :W], in0=pt[:, :W], in1=cs[:, 1, :W], op=ALU.mult), reads=[pt, cs], writes=[t2])
                    k.op("dve", lambda e: e.tensor_tensor(out=qro[:, g, :W], in0=t1[:, :W], in1=t2[:, :W], op=ALU.add), reads=[t1, t2], writes=[qro])
                k.dma("sp", self.qropeT.rearrange("g p t -> p g t")[:, :, t0:t0 + W], qro[:, :, :W], qro, reads=[qro])
                lin(ps[0], wckv, 0, 128)
                k.op("act", lambda e: e.copy(out=ckv[:, 0, :W], in_=ps[0][:, :W]), reads=[ps[0]], writes=[ckv])
                self.small_norm(ckv, ckvn, 1, W, self.onesv[:, 1, :], lambda c: self.pv[:, l, 42:43], sq2, rstd2, ps[1])
                k.op("act", lambda e: e.copy(out=ckvb[:, :W], in_=ckvn[:, 0, :W]), reads=[ckvn], writes=[ckvb])
                k.dma("sp", self.ckvT[:, t0:t0 + W], ckvb[:, :W], ckvb, reads=[ckvb])
                lin(ps[2], wkr, 0, 128)
                lin(ps[3], wkrr, 0, 128)
                k.op("dve", lambda e: e.tensor_tensor(out=t1[:, :W], in0=ps[2][:, :W], in1=cs[:, 0, :W], op=ALU.mult), reads=[ps[2], cs], writes=[t1])
                k.op("dve", lambda e: e.tensor_tensor(out=t2[:, :W], in0=ps[3][:, :W], in1=cs[:, 1, :W], op=ALU.mult), reads=[ps[3], cs], writes=[t2])
                k.op("dve", lambda e: e.tensor_tensor(out=krf[:, :W], in0=t1[:, :W], in1=t2[:, :W], op=ALU.add), reads=[t1, t2], writes=[krf])
                k.op("act", lambda e: e.copy(out=krb[:, :W], in_=krf[:, :W]), reads=[krf], writes=[krb])
                k.dma("sp", self.kropeT[:, t0:t0 + W], krb[:, :W], krb, reads=[krb])
                for tb in range(W // 128):
                    pb = ps[4 + ntm % 4]
                    o_ = tmo[ntm % 2]
                    ntm += 1
                    k.op("pe", lambda e: e.transpose(pb[:, 0:128], ckvn[:, 0, tb * 128:(tb + 1) * 128], self.ident[:]), reads=[ckvn, self.ident], writes=[pb])
                    k.op("pe", lambda e: e.transpose(pb[:, 128:256], krf[:, tb * 128:(tb + 1) * 128], self.ident[:]), reads=[krf, self.ident], writes=[pb])
                    k.op("act", lambda e: e.copy(out=o_[:, :], in_=pb[:, 0:160]), reads=[pb], writes=[o_])
                    tg = t0 + tb * 128
                    if tg < cfg.seq:
                        d1, d2 = self.o_ckv_p[l, tg:tg + 128, :], self.o_kr_p[l, tg:tg + 128, :]
                    else:
                        d1, d2 = self.o_ckv_s[l, tg - cfg.seq:tg - cfg.seq + 128, :], self.o_kr_s[l, tg - cfg.seq:tg - cfg.seq + 128, :]
                    k.dma("sp", d1, o_[:, 0:128], o_, reads=[o_])
                    k.dma("sp", d2, o_[:, 128:160], o_, reads=[o_])
                for j in range(14):
                    pb = ps[j % 4]
                    lin(pb, wpb, j * 128, 128)
                    if j % 2 == 0:
                        k.op("act", lambda e: e.copy(out=pbs[:, j, :W], in_=pb[:, :W]), reads=[pb], writes=[pbs.subs[j]])
                    else:
                        k.op("dve", lambda e: e.tensor_copy(out=pbs[:, j, :W], in_=pb[:, :W]), reads=[pb], writes=[pbs.subs[j]])
                k.dma("sp", self.pbT.rearrange("(c p) t -> p c t", p=128)[:, :, t0:t0 + W], pbs[:, :, :W], pbs, reads=pbs.subs)
            k.barrier()
            for b_ in [wcq, wckv, wkr, wpb, wuq, wuk, xt, cs, qlat, qro, ckvb, krb, pbs] + tmo:
                k.release_dsem(b_)

    def phase_mla(self, l):
        k, cfg = self.k, self.cfg
        ps = self.ps
        SEQ, NSS, PAST = cfg.seq, cfg.nss, cfg.past
        ymv = self.ymixT[0:512, :].rearrange("(h v) t -> v h t", v=64)
        with ExitStack() as es:
            wuv = k.sb([128, 512], BF16, es=es)
            k.dma("pool", wuv[:], self.w_uv[l], wuv, writes=[wuv])
            TKMAX = max(SEQ, PAST + 64)
            NKBMAX = (TKMAX + 127) // 128
            ckv_all = k.sb([128, NKBMAX * 128], BF16, es=es)
            kr_all = k.sb([128, NKBMAX * 128], BF16, es=es)
            V1 = k.sb([128, NKBMAX, 8, 65], BF16, n=NKBMAX, es=es)
            k.op("pool", lambda e: e.memset(V1[:, :, :, 64:65], 1.0), writes=[V1] + V1.subs)
            qlat = k.sb([128, 8, 512], BF16, es=es)
            qro = k.sb([128, 2, 512], BF16, es=es)
            qro_s = k.sb([128, 8, 64], BF16, es=es)
            k.op("dve", lambda e: e.memset(qro_s[:], 0.0), writes=[qro_s])
            k.op("dve", lambda e: e.memset(kr_all[:], 0.0), writes=[kr_all])
            krz = [k.sb([128, NKBMAX * 128], BF16, es=es) for _ in range(4)]
            for j in range(4):
                k.op("pool" if j % 2 else "dve", lambda e: e.memset(krz[j][:], 0.0), writes=[krz[j]])
            E = [k.sb([128, 512], BF16, es=es) for _ in range(4)]
            rl = k.sb([128, 512], F32, es=es)
            bcs = k.sb([64, 512], F32, es=es)
            yat = k.sb([64, 8, 512], BF16, n=8, es=es)
            yas = k.sb([64, 512], BF16, es=es)
            cst = k.sb([128, 4, 128], F32, es=es)
            cstk = k.sb([128, 4, 32], F32, es=es)
            nE = [0]
            nS = [0]

            def build_v(nkb, tk):
                for kb in range(nkb):
                    rows = min(128, tk - kb * 128)
                    pb = ps[4 + kb % 2]
                    k.op("pe", lambda e: e.matmul(pb[0:rows, :], ckv_all[:, kb * 128:kb * 128 + rows], wuv[:], start=True, stop=True), reads=[ckv_all, wuv], writes=[pb])
                    src = pb[0:rows, :].rearrange("p (h v) -> p h v", v=64)
                    if kb % 2 == 0:
                        k.op("act", lambda e: e.copy(out=V1[0:rows, kb, :, 0:64], in_=src), reads=[pb], writes=[V1.subs[kb]])
                    else:
                        k.op("dve", lambda e: e.tensor_copy(out=V1[0:rows, kb, :, 0:64], in_=src), reads=[pb], writes=[V1.subs[kb]])

            def finish(oT, ncol, dst_ap, dst_tr):
                k.op("dve", lambda e: e.reciprocal(out=rl[64:65, :ncol], in_=oT[64:65, :ncol]), reads=[oT], writes=[rl])
                bc = ps[6 + nS[0] % 2]
                nS[0] += 1
                k.op("pe", lambda e: e.matmul(bc[0:64, :ncol], self.ones1[64:65, 0:64], rl[64:65, :ncol], start=True, stop=True), reads=[self.ones1, rl], writes=[bc])
                k.op("act", lambda e: e.copy(out=bcs[:, :ncol], in_=bc[0:64, :ncol]), reads=[bc], writes=[bcs])
                k.op("dve", lambda e: e.tensor_tensor(out=dst_ap, in0=oT[0:64, :ncol], in1=bcs[:, :ncol], op=ALU.mult), reads=[oT, bcs], writes=[dst_tr])

            nkb = SEQ // 128
            k.dma("sp", ckv_all[:, 0:SEQ], self.ckvT[:, 0:SEQ], ckv_all, writes=[ckv_all])
            for j in range(4):
                k.dma("sp", krz[j][32 * j:32 * j + 32, 0:SEQ], self.kropeT[32 * j:32 * j + 32, 0:SEQ], krz[j], writes=[krz[j]])
            build_v(nkb, SEQ)
            for qt in range(SEQ // 512):
                q0 = qt * 512
                k.dma("sp", qlat[:], self.qlatT.rearrange("(h c) t -> c h t", c=128)[:, :, q0:q0 + 512], qlat, writes=[qlat])
                k.dma("sp", qro[:], self.qropeT.rearrange("g p t -> p g t")[:, :, q0:q0 + 512], qro, writes=[qro])
                last = qt * 4 + 3
                units = [(h, kb) for h in range(8) for kb in range(last + 1)]
                SKEW = 3
                slots = {}

                def emit_s(h, kb):
                    ca = 2 * kb - 8 * qt
                    c0 = max(ca, 0) * 64
                    st = ps[nE[0] % 4]
                    e_ = E[nE[0] % 4]
                    nE[0] += 1
                    slots[(h, kb)] = (e_, c0)
                    k.op("pe", lambda e: e.matmul(st[:, c0:512], ckv_all[:, kb * 128:(kb + 1) * 128], qlat[:, h, c0:512], start=True, stop=False), reads=[ckv_all, qlat], writes=[st])
                    k.op("pe", lambda e: e.matmul(st[:, c0:512], krz[h % 4][:, kb * 128:(kb + 1) * 128], qro[:, h // 4, c0:512], start=False, stop=True), reads=[krz[h % 4], qro], writes=[st])
                    if ca < 0:
                        k.op("act", lambda e: e.activation(out=e_[:, :], in_=st[:, :], func=AF.Exp, scale=MLA_SCALE), reads=[st], writes=[e_])
                    else:
                        k.op("act", lambda e: e.activation(out=e_[0:64, c0:512], in_=st[0:64, c0:512], func=AF.Exp, scale=MLA_SCALE), reads=[st], writes=[e_])
                        if c0 + 64 < 512:
                            k.op("act", lambda e: e.activation(out=e_[64:128, c0 + 64:512], in_=st[64:128, c0 + 64:512], func=AF.Exp, scale=MLA_SCALE), reads=[st], writes=[e_])
                        k.op("pool", lambda e: e.memset(e_[64:128, c0:c0 + 64], 0.0), writes=[e_])

                def emit_pv(h, kb):
                    e_, c0 = slots.pop((h, kb))
                    oT = ps[4 + h % 2]
                    k.op("pe", lambda e: e.matmul(oT[0:65, c0:512], V1[:, kb, h, :], e_[:, c0:512], start=(kb == 0), stop=(kb == last), skip_group_check=True),
                         reads=[V1.subs[kb], e_], writes=[oT])
                    if kb == last:
                        finish(oT, 512, yat[:, h, :], yat.subs[h])

                for i in range(len(units) + SKEW):
                    if i >= SKEW:
                        emit_pv(*units[i - SKEW])
                    if i < len(units):
                        emit_s(*units[i])
                k.dma("sp", ymv[:, :, q0:q0 + 512], yat[:], yat, reads=yat.subs)

            npb = PAST // 128
            tk = PAST + 64
            nkb = npb + 1
            for s_ in range(NSS):
                t0 = SEQ + s_ * 64
                for b0 in range(0, npb, 4):
                    nb_ = min(4, npb - b0)
                    k.dma("sp", cst[:, 0:nb_, :], self.c_ckv[l, s_, b0 * 128:(b0 + nb_) * 128, :].rearrange("(b p) c -> p b c", p=128), cst, writes=[cst])
                    k.dma("sp", cstk[:, 0:nb_, :], self.c_kr[l, s_, b0 * 128:(b0 + nb_) * 128, :].rearrange("(b p) c -> p b c", p=128), cstk, writes=[cstk])
                    pa, pk = ps[(b0 // 4) % 2], ps[2 + (b0 // 4) % 2]
                    for b in range(nb_):
                        k.op("pe", lambda e: e.transpose(pa[:, b * 128:(b + 1) * 128], cst[:, b, :], self.ident[:]), reads=[cst, self.ident], writes=[pa])
                    for b in range(nb_):
                        k.op("pe", lambda e: e.transpose(pk[0:32, b * 128:(b + 1) * 128], cstk[:, b, :], self.ident[:]), reads=[cstk, self.ident], writes=[pk])
                    k.op("act", lambda e: e.copy(out=ckv_all[:, b0 * 128:(b0 + nb_) * 128], in_=pa[:, 0:nb_ * 128]), reads=[pa], writes=[ckv_all])
                    k.op("dve", lambda e: e.tensor_copy(out=kr_all[0:32, b0 * 128:(b0 + nb_) * 128], in_=pk[0:32, 0:nb_ * 128]), reads=[pk], writes=[kr_all])
                k.dma("sp", ckv_all[:, PAST:PAST + 64], self.ckvT[:, t0:t0 + 64], ckv_all, writes=[ckv_all])
                k.dma("sp", kr_all[0:32, PAST:PAST + 64], self.kropeT[0:32, t0:t0 + 64], kr_all, writes=[kr_all])
                build_v(nkb, tk)
                k.dma("sp", qlat[:, :, 0:64], self.qlatT.rearrange("(h c) t -> c h t", c=128)[:, :, t0:t0 + 64], qlat, writes=[qlat])
                k.dma("sp", qro_s[0:32, :, :], self.qropeT.rearrange("g (hh j) t -> j (g hh) t", j=32)[:, :, t0:t0 + 64], qro_s, writes=[qro_s])
                oT = ps[4 + s_ % 2]
                slots = {}

                def emit_s2(kb):
                    rows = min(128, tk - kb * 128)
                    st = ps[nE[0] % 4]
                    e_ = E[nE[0] % 4]
                    nE[0] += 1
                    slots[kb] = (e_, rows)
                    stv = st[0:rows, :].rearrange("p (h t) -> p h t", t=64)
                    k.op("pe", lambda e: e.matmul(stv, ckv_all[:, kb * 128:kb * 128 + rows], qlat[:, :, 0:64], start=True, stop=False), reads=[ckv_all, qlat], writes=[st])
                    k.op("pe", lambda e: e.matmul(stv, kr_all[:, kb * 128:kb * 128 + rows], qro_s[:, :, :], start=False, stop=True), reads=[kr_all, qro_s], writes=[st])
                    k.op("act", lambda e: e.activation(out=e_[0:rows, :], in_=st[0:rows, :], func=AF.Exp, scale=MLA_SCALE), reads=[st], writes=[e_])

                def emit_pv2(kb):
                    e_, rows = slots.pop(kb)
                    for h in range(8):
                        k.op("pe", lambda e: e.matmul(oT[0:65, h * 64:(h + 1) * 64], V1[0:rows, kb, h, :], e_[0:rows, h * 64:(h + 1) * 64], start=(kb == 0 and h == 0), stop=(kb == nkb - 1), skip_group_check=True),
                             reads=[V1.subs[kb], e_], writes=[oT])

                for i in range(nkb + 3):
                    if i >= 3:
                        emit_pv2(i - 3)
                    if i < nkb:
                        emit_s2(i)
                finish(oT, 512, yas[:, :], yas.trk)
                k.dma("sp", ymv[:, :, t0:t0 + 64], yas[:, :].rearrange("p (h t) -> p h t", t=64), yas, reads=[yas])
            k.barrier()
            for b_ in [wuv, ckv_all, kr_all, qlat, qro, qro_s, yat, yas, cst, cstk] + krz:
                k.release_dsem(b_)


    def phase_rwkv(self, l):
        k, cfg = self.k, self.cfg
        nc = self.nc
        ps = self.ps
        SEQ, NSS = cfg.seq, cfg.nss
        WM = 256
        pbv = self.pbT.rearrange("(c p) t -> p c t", p=128)
        ybv = self.ymixT[512:1024, :].rearrange("(c p) t -> p c t", p=128)
        M1 = self.cmask[:, 0:128]
        M2 = self.cmask[0:64, 128:192]
        BLK = self.cmask[:, 256:384]
        ID = self.ident
        PVC = lambda a, b: self.pv[:, l, a:b]
        MU, W0, A0, KK_, KA, RK, GG, GB = PVC(48, 62), PVC(62, 66), PVC(66, 70), PVC(70, 74), PVC(74, 78), PVC(78, 82), PVC(82, 86), PVC(86, 90)
        bc3 = lambda ap, W: ap.unsqueeze(2).to_broadcast([128, ap.shape[1], W])
        with ExitStack() as es:
            S4 = lambda n=0: k.sb([128, 4, WM], F32, n=n, es=es)
            wup = k.sb([64, 512], BF16, es=es)
            aup = k.sb([128, 512], BF16, es=es)
            gup = k.sb([128, 512], BF16, es=es)
            k.dma("pool", wup[:], self.w_wup[l], wup, writes=[wup])
            k.dma("pool", aup[64:128, :], self.w_aup[l], aup, writes=[aup])
            k.dma("pool", gup[:], self.w_gup[l], gup, writes=[gup])
            cst = k.sb([128, 24], F32, es=es)
            k.op("dve", lambda e: e.memset(cst[:, 0:1], 1e-12), writes=[cst])
            k.op("dve", lambda e: e.memset(cst[:, 1:2], GN_EPS), writes=[cst])
            k.op("dve", lambda e: e.tensor_scalar(out=cst[:, 2:6], in0=KA, scalar1=-1.0, scalar2=1.0, op0=ALU.mult, op1=ALU.add), reads=[self.pv], writes=[cst])
            OMK = cst[:, 2:6]
            k.op("dve", lambda e: e.tensor_scalar(out=cst[:, 8:22], in0=MU, scalar1=-1.0, scalar2=1.0, op0=ALU.mult, op1=ALU.add), reads=[self.pv], writes=[cst])
            OMM = cst[:, 8:22]
            mask01 = k.sb([128, WM], F32, es=es)
            k.op("dve", lambda e: e.memset(mask01[:], 1.0), writes=[mask01])
            k.op("dve", lambda e: e.memset(mask01[:, 0:WM:64], 0.0), writes=[mask01])
            PB = k.sb([128, 14, WM], F32, es=es)
            PV_ = k.sb([128, 14, WM], F32, es=es)
            lw, cum, E1, E2, E3, av, kk, tm1, tm2, bon, gT, Ys, yc, tq, km = [S4() for _ in range(15)]
            AR = k.sb([128, 4, WM // 64, 2, 64], F32, es=es)
            BK = k.sb([128, 4, WM // 64, 2, 64], F32, es=es)
            vT = k.sb([128, 4, 64 + WM], F32, es=es)
            k.op("dve", lambda e: e.memset(vT[:, :, 0:64], 0.0), writes=[vT])
            txw = k.sb([64, WM], BF16, es=es)
            xab = k.sb([128, WM], BF16, es=es)
            sxg = k.sb([128, WM], BF16, es=es)
            ybT = k.sb([128, 4, WM], BF16, es=es)
            NCH = WM // 64
            G1m = [k.sb([128, 8, 128], F32, es=es) for _ in range(NCH)]
            Pm = [k.sb([128, 4, 64], F32, es=es) for _ in range(NCH)]
            BKtm = [k.sb([128, 8, 64], F32, es=es) for _ in range(NCH)]
            UV = [k.sb([128, 8, 64], F32, es=es) for _ in range(NCH)]
            RVs = [k.sb([128, 4, 64], F32, es=es) for _ in range(NCH)]
            XsC = [[k.sb([128, 4, 64], F32, es=es) for _ in range(2)] for _ in range(2)]
            XtsC = [[k.sb([128, 4, 64], F32, es=es) for _ in range(2)] for _ in range(2)]
            Rs = k.sb([128, 4, 64], F32, es=es)
            Zt = k.sb([128, 4, 64], F32, es=es)
            tmz = k.sb([128, 4, 64], F32, es=es)
            Sio = k.sb([64, 8, 64], F32, es=es)
            sh0 = k.sb([128, 14], F32, es=es)
            shl = k.sb([128, 14], F32, es=es)

            def tt(e, out, a, b, op, reads, writes):
                k.op(e, lambda en: en.tensor_tensor(out=out, in0=a, in1=b, op=op), reads=reads, writes=writes)

            def actf(out, a, func, reads, writes, bias=None, scale=1.0):
                if bias is None:
                    k.op("act", lambda en: en.activation(out=out, in_=a, func=func, scale=scale), reads=reads, writes=writes)
                else:
                    k.op("act", lambda en: en.activation(out=out, in_=a, func=func, bias=bias, scale=scale), reads=reads, writes=writes)

            c4 = lambda ap: ap.rearrange("p h (c t) -> p h c t", t=64)
            f2 = lambda ap: ap.rearrange("p a t -> p (a t)")

            def do_tile(t0, W, seq_start, shift_src):
                nch = W // 64
                k.dma("sp", PB[:, :, :W], pbv[:, :, t0:t0 + W], PB, writes=[PB])
                if seq_start:
                    if W > 1:
                        k.dma("sp", PV_[:, :, 1:W], pbv[:, :, t0:t0 + W - 1], PV_, writes=[PV_])
                    if shift_src is None:
                        k.op("pool", lambda e: e.memset(PV_[:, :, 0:1], 0.0), writes=[PV_])
                    else:
                        k.dma("sp", sh0[:], shift_src, sh0, writes=[sh0])
                        k.op("pool", lambda e: e.tensor_copy(out=PV_[:, :, 0:1], in_=sh0[:].unsqueeze(2)), reads=[sh0], writes=[PV_])
                else:
                    k.dma("sp", PV_[:, :, :W], pbv[:, :, t0 - 1:t0 + W - 1], PV_, writes=[PV_])
                for cch in range(14):
                    k.op("act", lambda e: e.activation(out=PV_[:, cch, :W], in_=PV_[:, cch, :W], func=AF.Identity, scale=MU[:, cch:cch + 1]), reads=[PV_, self.pv], writes=[PV_])
                    k.op("dve", lambda e: e.scalar_tensor_tensor(out=PB[:, cch, :W], in0=PB[:, cch, :W], scalar=OMM[:, cch:cch + 1], in1=PV_[:, cch, :W], op0=ALU.mult, op1=ALU.add), reads=[PB, PV_, cst], writes=[PB])
                r_, kx, v_ = PB[:, 0:4, :W], PB[:, 4:8, :W], PB[:, 8:12, :W]
                actf(txw[0:64, :W], PB[0:64, 12, :W], AF.Tanh, [PB], [txw])
                tt("dve", kk[:, :, :W], kx, bc3(KK_, W), ALU.mult, [PB, self.pv], [kk])
                actf(xab[64:128, :W], PB[64:128, 12, :W], AF.Copy, [PB], [xab])
                actf(sxg[:, :W], PB[:, 13, :W], AF.Sigmoid, [PB], [sxg])
                actf(tq[:, :, :W], kk[:, :, :W], AF.Square, [kk], [tq])
                for hp in range(4):
                    pz, pa = ps[hp % 2], ps[2 + hp % 2]
                    k.op("pe", lambda e: e.matmul(pz[:, :W], wup[0:64, hp * 128:(hp + 1) * 128], txw[0:64, :W], start=True, stop=True), reads=[wup, txw], writes=[pz])
                    k.op("pe", lambda e: e.matmul(pa[:, :W], aup[64:128, hp * 128:(hp + 1) * 128], xab[64:128, :W], start=True, stop=True), reads=[aup, xab], writes=[pa])
                    actf(lw[:, hp, :W], pz[:, :W], AF.Sigmoid, [pz, self.pv], [lw], bias=W0[:, hp:hp + 1])
                    actf(av[:, hp, :W], pa[:, :W], AF.Sigmoid, [pa, self.pv], [av], bias=A0[:, hp:hp + 1])
                    k.op("dve", lambda e: e.tensor_tensor_scan(out=cum[:, hp, :W], data0=mask01[:, :W], data1=lw[:, hp, :W], initial=0.0, op0=ALU.mult, op1=ALU.add),
                         reads=[mask01, lw], writes=[cum])
                for hp in range(4):
                    pg, pn = ps[4 + hp % 2], ps[6 + hp % 2]
                    k.op("pe", lambda e: e.matmul(pn[:, :W], BLK, tq[:, hp, :W], start=True, stop=True), reads=[self.cmask, tq], writes=[pn])
                    k.op("pe", lambda e: e.matmul(pg[:, :W], gup[:, hp * 128:(hp + 1) * 128], sxg[:, :W], start=True, stop=True), reads=[gup, sxg], writes=[pg])
                    actf(tm2[:, hp, :W], pn[:, :W], AF.Sqrt, [pn, cst], [tm2], bias=cst[:, 0:1])
                    k.op("dve", lambda e: e.tensor_copy(out=gT[:, hp, :W], in_=pg[:, :W]), reads=[pg], writes=[gT])
                actf(E1[:, :, :W], cum[:, :, :W], AF.Exp, [cum], [E1], scale=-DECAY_C)
                tt("dve", tm1[:, :, :W], cum[:, :, :W], lw[:, :, :W], ALU.subtract, [cum, lw], [tm1])
                actf(E2[:, :, :W], cum[:, :, :W], AF.Exp, [cum], [E2], scale=DECAY_C)
                k.op("dve", lambda e: e.reciprocal(out=tm2[:, :, :W], in_=tm2[:, :, :W]), reads=[tm2], writes=[tm2])
                actf(E3[:, :, :W], tm1[:, :, :W], AF.Exp, [tm1], [E3], scale=-DECAY_C)
                tt("dve", kk[:, :, :W], kk[:, :, :W], tm2[:, :, :W], ALU.mult, [kk, tm2], [kk])
                for hp in range(4):
                    k.op("act", lambda e: e.activation(out=km[:, hp, :W], in_=av[:, hp, :W], func=AF.Identity, bias=OMK[:, hp:hp + 1], scale=KA[:, hp:hp + 1]), reads=[av, self.pv, cst], writes=[km])
                k.op("act", lambda e: e.copy(out=vT[:, :, 64:64 + W], in_=v_), reads=[PB], writes=[vT])
                tt("dve", km[:, :, :W], km[:, :, :W], kx, ALU.mult, [km, PB], [km])
                k.op("dve", lambda e: e.scalar_tensor_tensor(out=AR[:, :, 0:nch, 0, :], in0=c4(kk[:, :, :W]), scalar=-1.0, in1=c4(E3[:, :, :W]), op0=ALU.mult, op1=ALU.mult), reads=[kk, E3], writes=[AR])
                tt("dve", AR[:, :, 0:nch, 1, :], c4(r_), c4(E1[:, :, :W]), ALU.mult, [PB, E1], [AR])
                tt("dve", tm2[:, :, :W], kk[:, :, :W], av[:, :, :W], ALU.mult, [kk, av], [tm2])
                tt("dve", BK[:, :, 0:nch, 1, :], c4(km[:, :, :W]), c4(E2[:, :, :W]), ALU.mult, [km, E2], [BK])
                tt("dve", BK[:, :, 0:nch, 0, :], c4(tm2[:, :, :W]), c4(E2[:, :, :W]), ALU.mult, [tm2, E2], [BK])
                tt("pool", tq[:, :, :W], r_, km[:, :, :W], ALU.mult, [PB, km], [tq])
                tt("pool", tq[:, :, :W], tq[:, :, :W], bc3(RK, W), ALU.mult, [tq, self.pv], [tq])

                hv = lambda ap: ap.rearrange("p (a b) t -> p a b t", b=2)
                p3 = lambda ap, t: ap.rearrange("p (h t) -> p h t", t=t)
                HEADS = [(h, h // 2, h % 2, (h % 2) * 64) for h in range(8)]

                def gen_B(c, q, Xs, Xts):
                    g1, pm, bkt, uv, rv = G1m[c], Pm[c], BKtm[c], UV[c], RVs[c]
                    for (h, hp, hh, b0) in HEADS:
                        k.op("pe", lambda e: e.matmul(q[hh][:, hp * 128:(hp + 1) * 128], f2(BK[b0:b0 + 64, hp, c, :, :]), f2(AR[b0:b0 + 64, hp, c, :, :]), start=True, stop=True),
                             reads=[BK, AR], writes=[q[hh]])
                    yield
                    x_prev, xt_prev = Xs[0], Xts[0]
                    for hh in range(2):
                        b0 = hh * 64
                        tt("dve", hv(g1[:, :, :])[:, :, hh, :], p3(q[hh][:, :], 128), M1.unsqueeze(1).to_broadcast([128, 4, 128]), ALU.mult, [q[hh], self.cmask], [g1])
                        tt("dve", x_prev[b0:b0 + 64, :, :], p3(q[hh][0:64, :], 128)[:, :, 0:64], M1[0:64, 0:64].unsqueeze(1).to_broadcast([64, 4, 64]), ALU.mult, [q[hh], self.cmask], [x_prev])
                    for (h, hp, hh, b0) in HEADS:
                        k.op("pe", lambda e: e.matmul(q[hh][b0:b0 + 64, hp * 64:(hp + 1) * 64], AR[b0:b0 + 64, hp, c, 0, :], BK[b0:b0 + 64, hp, c, 0, :], start=True, stop=True), reads=[AR, BK], writes=[q[hh]])
                    yield
                    for hh in range(2):
                        b0 = hh * 64
                        tt("dve", xt_prev[b0:b0 + 64, :, :], p3(q[hh][b0:b0 + 64, 0:256], 64), self.cmask[b0:b0 + 64, 128:192].unsqueeze(1).to_broadcast([64, 4, 64]), ALU.mult, [q[hh], self.cmask], [xt_prev])
                        tt("pool", pm[b0:b0 + 64, :, :], x_prev[b0:b0 + 64, :, :], ID[b0:b0 + 64, b0:b0 + 64].unsqueeze(1).to_broadcast([64, 4, 64]), ALU.add, [x_prev, ID], [pm])
                    yield
                    for lev in range(1, 6):
                        xn, xtn = Xs[lev % 2], Xts[lev % 2]
                        for (h, hp, hh, b0) in HEADS:
                            k.op("pe", lambda e: e.matmul(q[hh][b0:b0 + 64, 256 + hp * 64:256 + (hp + 1) * 64], x_prev[b0:b0 + 64, hp, :], xt_prev[b0:b0 + 64, hp, :], start=True, stop=True), reads=[xt_prev, x_prev], writes=[q[hh]])
                        if lev < 5:
                            for (h, hp, hh, b0) in HEADS:
                                k.op("pe", lambda e: e.matmul(q[hh][b0:b0 + 64, hp * 64:(hp + 1) * 64], xt_prev[b0:b0 + 64, hp, :], x_prev[b0:b0 + 64, hp, :], start=True, stop=True), reads=[xt_prev, x_prev], writes=[q[hh]])
                        for hh in range(2):
                            b0 = hh * 64
                            k.op("dve", lambda e: e.tensor_copy(out=xtn[b0:b0 + 64, :, :], in_=p3(q[hh][b0:b0 + 64, 256:512], 64)), reads=[q[hh]], writes=[xtn])
                            if lev < 5:
                                k.op("act", lambda e: e.copy(out=xn[b0:b0 + 64, :, :], in_=p3(q[hh][b0:b0 + 64, 0:256], 64)), reads=[q[hh]], writes=[xn])
                        yield
                        for (h, hp, hh, b0) in HEADS:
                            k.op("pe", lambda e: e.matmul(q[hh][b0:b0 + 64, hp * 64:(hp + 1) * 64], xtn[b0:b0 + 64, hp, :], pm[b0:b0 + 64, hp, :], start=True, stop=True), reads=[xtn, pm], writes=[q[hh]])
                        for hh in range(2):
                            b0 = hh * 64
                            tt("dve", pm[b0:b0 + 64, :, :], p3(q[hh][b0:b0 + 64, 0:256], 64), pm[b0:b0 + 64, :, :], ALU.add, [q[hh], pm], [pm])
                        yield
                        x_prev, xt_prev = xn, xtn
                    for (h, hp, hh, b0) in HEADS:
                        k.op("pe", lambda e: e.transpose(q[hh][:, hp * 64:(hp + 1) * 64], f2(BK[b0:b0 + 64, hp, c, :, :]), ID[b0:b0 + 64, b0:b0 + 64]), reads=[BK, ID], writes=[q[hh]])
                    for (h, hp, hh, b0) in HEADS:
                        k.op("pe", lambda e: e.transpose(q[hh][:, 256 + hp * 64:256 + (hp + 1) * 64], vT[b0:b0 + 64, hp, c * 64:c * 64 + 128], ID[b0:b0 + 64, b0:b0 + 64]), reads=[vT, ID], writes=[q[hh]])
                    for hh in range(2):
                        k.op("act", lambda e: e.copy(out=hv(bkt[:, :, :])[:, :, hh, :], in_=p3(q[hh][:, 0:256], 64)), reads=[q[hh]], writes=[bkt])
                        k.op("dve", lambda e: e.tensor_copy(out=hv(uv[64:128, :, :])[:, :, hh, :], in_=p3(q[hh][64:128, 256:512], 64)), reads=[q[hh]], writes=[uv])
                    yield
                    for (h, hp, hh, b0) in HEADS:
                        k.op("pe", lambda e: e.matmul(q[hh][b0:b0 + 64, hp * 64:(hp + 1) * 64], g1[64:128, h, 0:64], uv[64:128, h, :], start=True, stop=True), reads=[g1, uv], writes=[q[hh]])
                    for hh in range(2):
                        b0 = hh * 64
                        k.op("act", lambda e: e.copy(out=rv[b0:b0 + 64, :, :], in_=p3(q[hh][b0:b0 + 64, 0:256], 64)), reads=[q[hh]], writes=[rv])
                    yield

                def gen_C(c):
                    cols = slice(c * 64, (c + 1) * 64)
                    g1, pm, bkt, uv, rv = G1m[c], Pm[c], BKtm[c], UV[c], RVs[c]
                    pRU = pYZ = (ps[0], ps[1])
                    for (h, hp, hh, b0) in HEADS:
                        k.op("pe", lambda e: e.matmul(pRU[hh][b0:b0 + 64, hp * 64:(hp + 1) * 64], AR[b0:b0 + 64, hp, c, 0, :], Zt[b0:b0 + 64, hp, :], start=True, stop=True), reads=[AR, Zt], writes=[pRU[hh]])
                    for hh in range(2):
                        b0 = hh * 64
                        tt("dve", Rs[b0:b0 + 64, :, :], p3(pRU[hh][b0:b0 + 64, 0:256], 64), rv[b0:b0 + 64, :, :], ALU.add, [pRU[hh], rv], [Rs])
                    yield
                    for (h, hp, hh, b0) in HEADS:
                        k.op("pe", lambda e: e.matmul(pRU[hh][0:64, 256 + hp * 64:256 + (hp + 1) * 64], pm[b0:b0 + 64, hp, :], Rs[b0:b0 + 64, hp, :], start=True, stop=True), reads=[pm, Rs], writes=[pRU[hh]])
                    k.op("act", lambda e: e.copy(out=hv(uv[0:64, :, :])[:, :, 0, :], in_=p3(pRU[0][0:64, 256:512], 64)), reads=[pRU[0]], writes=[uv])
                    k.op("dve", lambda e: e.tensor_copy(out=hv(uv[0:64, :, :])[:, :, 1, :], in_=p3(pRU[1][0:64, 256:512], 64)), reads=[pRU[1]], writes=[uv])
                    yield
                    for (h, hp, hh, b0) in HEADS:
                        k.op("pe", lambda e: e.matmul(pYZ[hh][b0:b0 + 64, hp * 64:(hp + 1) * 64], Zt[b0:b0 + 64, hp, :], AR[b0:b0 + 64, hp, c, 1, :], start=(hp == 0), stop=False, skip_group_check=True), reads=[Zt, AR], writes=[pYZ[hh]])
                    for (h, hp, hh, b0) in HEADS:
                        k.op("pe", lambda e: e.matmul(pYZ[hh][b0:b0 + 64, 256 + hp * 64:256 + (hp + 1) * 64], bkt[:, h, :], uv[:, h, :], start=False, stop=True, skip_group_check=True), reads=[bkt, uv], writes=[pYZ[hh]])
                    for (h, hp, hh, b0) in HEADS:
                        k.op("pe", lambda e: e.matmul(pYZ[hh][b0:b0 + 64, hp * 64:(hp + 1) * 64], uv[:, h, :], g1[:, h, 64:128], start=False, stop=True, skip_group_check=True), reads=[uv, g1], writes=[pYZ[hh]])
                    for hh in range(2):
                        b0 = hh * 64
                        tt("dve", tmz[b0:b0 + 64, :, :], p3(pYZ[hh][b0:b0 + 64, 256:512], 64), Zt[b0:b0 + 64, :, :], ALU.add, [pYZ[hh], Zt], [tmz])
                    tt("dve", Zt[:, :, :], tmz[:, :, :], E1[:, :, c * 64 + 63:c * 64 + 64].to_broadcast([128, 4, 64]), ALU.mult, [tmz, E1], [Zt])
                    for hh in range(2):
                        b0 = hh * 64
                        k.op("act", lambda e: e.copy(out=Ys[b0:b0 + 64, :, cols], in_=p3(pYZ[hh][b0:b0 + 64, 0:256], 64)), reads=[pYZ[hh]], writes=[Ys])
                    yield

                def run_interleaved(*gens):
                    gens = [g for g in gens if g is not None]
                    while gens:
                        for g in list(gens):
                            try:
                                next(g)
                            except StopIteration:
                                gens.remove(g)

                def chain(*gs):
                    for g in gs:
                        yield from g

                QA, QB = (ps[2], ps[3]), (ps[4], ps[5])
                gB = lambda c: gen_B(c, QA if c % 2 == 0 else QB, XsC[c % 2], XtsC[c % 2]) if c < nch else None
                run_interleaved(gB(0), gB(1))
                for c in range(0, nch, 2):
                    run_interleaved(chain(*[gen_C(cc) for cc in range(c, min(c + 2, nch))]), gB(c + 2), gB(c + 3))

                for hp in range(4):
                    pn = ps[6 + hp % 2]
                    k.op("pe", lambda e: e.matmul(pn[:, :W], BLK, tq[:, hp, :W], start=True, stop=True), reads=[self.cmask, tq], writes=[pn])
                    tt("dve", bon[:, hp, :W], pn[:, :W], PB[:, 8 + hp, :W], ALU.mult, [pn, PB], [bon])
                for hp in range(4):
                    pn = ps[6 + hp % 2]
                    k.op("pe", lambda e: e.matmul(pn[:, :W], BLK, Ys[:, hp, :W], start=True, stop=True), reads=[self.cmask, Ys], writes=[pn])
                    k.op("dve", lambda e: e.scalar_tensor_tensor(out=yc[:, hp, :W], in0=pn[:, :W], scalar=-1.0 / 64, in1=Ys[:, hp, :W], op0=ALU.mult, op1=ALU.add), reads=[pn, Ys], writes=[yc])
                actf(tm1[:, :, :W], yc[:, :, :W], AF.Square, [yc], [tm1])
                for hp in range(4):
                    pn = ps[6 + hp % 2]
                    k.op("pe", lambda e: e.matmul(pn[:, :W], BLK, tm1[:, hp, :W], start=True, stop=True), reads=[self.cmask, tm1], writes=[pn])
                    actf(tm2[:, hp, :W], pn[:, :W], AF.Sqrt, [pn, cst], [tm2], bias=cst[:, 1:2], scale=1.0 / 64)
                k.op("dve", lambda e: e.reciprocal(out=tm2[:, :, :W], in_=tm2[:, :, :W]), reads=[tm2], writes=[tm2])
                tt("dve", yc[:, :, :W], yc[:, :, :W], tm2[:, :, :W], ALU.mult, [yc, tm2], [yc])
                for hp in range(4):
                    k.op("act", lambda e: e.activation(out=yc[:, hp, :W], in_=yc[:, hp, :W], func=AF.Identity, bias=GB[:, hp:hp + 1], scale=GG[:, hp:hp + 1]), reads=[yc, self.pv], writes=[yc])
                tt("dve", yc[:, :, :W], yc[:, :, :W], bon[:, :, :W], ALU.add, [yc, bon], [yc])
                tt("dve", ybT[:, :, :W], yc[:, :, :W], gT[:, :, :W], ALU.mult, [yc, gT], [ybT])
                k.dma("sp", ybv[:, :, t0:t0 + W], ybT[:, :, :W], ybT, reads=[ybT])

            def finish_seq(wkv_dst, sh_dst, t_last):
                if cfg.rstop < 5:
                    return
                pT = ps[6]
                for hp in range(4):
                    k.op("pe", lambda e: e.transpose(pT[0:64, hp * 128:(hp + 1) * 128], Zt[:, hp, :], ID[:, :]), reads=[Zt, ID], writes=[pT])
                k.op("act", lambda e: e.copy(out=Sio[:, :, :], in_=pT[0:64, :].rearrange("p (h t) -> p h t", t=64)), reads=[pT], writes=[Sio])
                k.dma("sp", wkv_dst.rearrange("h v k -> v h k"), Sio[:, :, :], Sio, reads=[Sio])
                with nc.allow_non_contiguous_dma(reason="single token-shift row, 7 KiB"):
                    k.dma("sp", shl[:], pbv[:, :, t_last:t_last + 1].rearrange("p c o -> p (c o)"), shl, writes=[shl])
                    k.dma("sp", sh_dst.rearrange("(c p) -> p c", p=128), shl[:], shl, reads=[shl])

            k.op("dve", lambda e: e.memset(Zt[:], 0.0), writes=[Zt])
            for t0 in range(0, SEQ, WM):
                do_tile(t0, min(WM, SEQ - t0), t0 == 0, None)
            finish_seq(self.o_wkv_p[l], self.o_sh_p[l, 0], SEQ - 1)
            for s_ in range(NSS):
                t0 = SEQ + s_ * 64
                k.dma("sp", Sio[:, :, :], self.st_wkv[l, s_].rearrange("h v k -> v h k"), Sio, writes=[Sio])
                pT = ps[7]
                for hp in range(4):
                    k.op("pe", lambda e: e.transpose(pT[:, hp * 64:(hp + 1) * 64], Sio[:, hp * 2:hp * 2 + 2, :].rearrange("p h t -> p (h t)"), ID[0:64, 0:64]), reads=[Sio, ID], writes=[pT])
                k.op("act", lambda e: e.copy(out=Zt[:, :, :], in_=pT[:, 0:256].rearrange("p (h t) -> p h t", t=64)), reads=[pT], writes=[Zt])
                do_tile(t0, 64, True, self.st_sh[l, s_])
                finish_seq(self.o_wkv_s[l, s_], self.o_sh_s[l, s_, 0], t0 + 63)
            k.barrier()
            for b_ in [wup, aup, gup, PB, PV_, ybT, Sio, sh0, shl]:
                k.release_dsem(b_)

    def phase_xattn(self, l):
        k, cfg = self.k, self.cfg
        ps = self.ps
        SEQ, NSS = cfg.seq, cfg.nss
        WM = 512
        xTv = self.xT.rearrange("(c p) t -> p c t", p=128)
        ymT = self.ymixT.rearrange("(c p) t -> p c t", p=128)
        fm = lambda ap: ap.rearrange("(c p) n -> p c n", p=128)
        with ExitStack() as es:
            self.epsb = k.sb([128, 1], F32, es=es)
            k.op("dve", lambda e: e.memset(self.epsb[:], EPS), writes=[self.epsb])
            oneb = k.sb([128, 128], BF16, es=es)
            k.op("dve", lambda e: e.memset(oneb[:], 1.0), writes=[oneb])
            wout = k.sb([128, 8, D], BF16, es=es)
            wmq = k.sb([128, 8, D], BF16, es=es)
            wmo = k.sb([128, 8, D], BF16, es=es)
            for wt, src in ((wout, self.w_out), (wmq, self.w_mq), (wmo, self.w_mo)):
                for c0 in range(0, 8, 2):
                    k.dma("pool", wt[:, c0:c0 + 2, :], fm(src[l])[:, c0:c0 + 2, :], wt, writes=[wt])
            mkT = [k.sb([128, 8, N_MEM], BF16, es=es) for _ in range(1 + NSS)]
            mvs = [k.sb([128, 2, D], BF16, es=es) for _ in range(1 + NSS)]
            with ExitStack() as es2:
                wmk = k.sb([128, 8, D], BF16, es=es2)
                wmv = k.sb([128, 8, D], BF16, es=es2)
                for wt, src in ((wmk, self.w_mk), (wmv, self.w_mv)):
                    for c0 in range(0, 8, 2):
                        k.dma("pool", wt[:, c0:c0 + 2, :], fm(src[l])[:, c0:c0 + 2, :], wt, writes=[wt])
                mtm = k.sb([128, 2, D], F32, es=es2)
                memT = k.sb([128, 8, N_MEM], F32, es=es2)
                mn = k.sb([128, 8, N_MEM], BF16, n=8, es=es2)
                sq = k.sb([128, 8, N_MEM], BF16, n=8, es=es2)
                rstd = k.sb([128, N_MEM], F32, es=es2)
                otm = [k.sb([128, D], F32, es=es2) for _ in range(2)]
                k.dma("sp", mtm[:], self.mem_p.rearrange("(b p) d -> p b d", p=128), mtm, writes=[mtm])
                for c in range(8):
                    pb = ps[c % 4]
                    for mb in range(2):
                        k.op("pe", lambda e: e.transpose(pb[:, mb * 128:(mb + 1) * 128], mtm[:, mb, c * 128:(c + 1) * 128], self.ident[:]), reads=[mtm, self.ident], writes=[pb])
                    k.op("act", lambda e: e.copy(out=memT[:, c, :], in_=pb[:, 0:256]), reads=[pb], writes=[memT])
                XS = cfg.xstop
                if XS >= 2:
                    self.rmsnorm_fm(memT, mn, N_MEM, lambda c: self.pv[:, l, 32 + c:33 + c], sq, rstd)
                for j in range(8 if XS >= 3 else 0):
                    pb = ps[4 + j % 4]
                    for c in range(8):
                        k.op("pe", lambda e: e.matmul(pb[:, 0:256], wmk[:, c, j * 128:(j + 1) * 128], mn[:, c, :], start=(c == 0), stop=(c == 7)), reads=[wmk, mn.subs[c]], writes=[pb])
                    k.op("act", lambda e: e.copy(out=mkT[0][:, j, :], in_=pb[:, 0:256]), reads=[pb], writes=[mkT[0]])
                ntm = 0
                for wt, dst, keep in (((wmk, self.o_mk, False), (wmv, self.o_mv, True)) if XS >= 4 else ()):
                    for mb in range(2):
                        o_ = otm[ntm % 2]
                        ntm += 1
                        for hh in range(2):
                            pb = ps[(mb * 2 + hh) % 4]
                            for c in range(8):
                                k.op("pe", lambda e: e.matmul(pb[:, :], mn[:, c, mb * 128:(mb + 1) * 128], wt[:, c, hh * 512:(hh + 1) * 512], start=(c == 0), stop=(c == 7)), reads=[wt, mn.subs[c]], writes=[pb])
                            k.op("act", lambda e: e.copy(out=o_[:, hh * 512:(hh + 1) * 512], in_=pb[:, :]), reads=[pb], writes=[o_])
                            if keep:
                                k.op("dve", lambda e: e.tensor_copy(out=mvs[0][:, mb, hh * 512:(hh + 1) * 512], in_=o_[:, hh * 512:(hh + 1) * 512]), reads=[o_], writes=[mvs[0]])
                        k.dma("sp", dst[l, mb * 128:(mb + 1) * 128, :], o_[:], o_, reads=[o_])
                for s_ in range(NSS if "xsamp" not in cfg.skip else 0):
                    k.dma("sp", mtm[:], self.c_mk[l, s_].rearrange("(b p) d -> p b d", p=128), mtm, writes=[mtm])
                    k.dma("pool", mvs[1 + s_][:], self.c_mv[l, s_].rearrange("(b p) d -> p b d", p=128), mvs[1 + s_], writes=[mvs[1 + s_]])
                    for j in range(8):
                        pb = ps[j % 4]
                        for mb in range(2):
                            k.op("pe", lambda e: e.transpose(pb[:, mb * 128:(mb + 1) * 128], mtm[:, mb, j * 128:(j + 1) * 128], self.ident[:]), reads=[mtm, self.ident], writes=[pb])
                        if j % 2 == 0:
                            k.op("act", lambda e: e.copy(out=mkT[1 + s_][:, j, :], in_=pb[:, 0:256]), reads=[pb], writes=[mkT[1 + s_]])
                        else:
                            k.op("dve", lambda e: e.tensor_copy(out=mkT[1 + s_][:, j, :], in_=pb[:, 0:256]), reads=[pb], writes=[mkT[1 + s_]])
                k.barrier()
                for b_ in [wmk, wmv, mtm] + otm:
                    k.release_dsem(b_)
            xt = k.sb([128, 8, WM], F32, es=es)
            ym = k.sb([128, 8, WM], BF16, es=es)
            ht = k.sb([128, 8, WM], BF16, n=8, es=es)
            sq = k.sb([128, 8, WM], BF16, n=8, es=es)
            rstd = k.sb([128, WM], F32, es=es)
            qx = k.sb([128, 8, WM], BF16, n=8, es=es)
            ox = k.sb([128, 8, WM], BF16, n=8, es=es)
            E = [k.sb([128, WM], BF16, es=es) for _ in range(4)]
            rl = k.sb([128, WM], F32, es=es)
            for (t0, W) in (self.tiles(WM) if "xtok" not in cfg.skip else []):
                if t0 < SEQ:
                    segs = [(0, W, mkT[0], mvs[0])]
                else:
                    segs = [(s_ * 64, 64, mkT[1 + s_], mvs[1 + s_]) for s_ in range(NSS)]
                k.dma("sp", xt[:, :, :W], xTv[:, :, t0:t0 + W], xt, writes=[xt])
                k.dma("sp", ym[:, :, :W], ymT[:, :, t0:t0 + W], ym, writes=[ym])
                for m in range(8):
                    pb = ps[m % 2]
                    for c in range(8):
                        k.op("pe", lambda e: e.matmul(pb[:, :W], wout[:, c, m * 128:(m + 1) * 128], ym[:, c, :W], start=(c == 0), stop=(c == 7)), reads=[wout, ym], writes=[pb])
                    k.op("dve", lambda e: e.tensor_tensor(out=xt[:, m, :W], in0=pb[:, :W], in1=xt[:, m, :W], op=ALU.add), reads=[pb, xt], writes=[xt])
                self.rmsnorm_fm(xt, ht, W, lambda c: self.pv[:, l, 24 + c:25 + c], sq, rstd)
                for j in range(8):
                    pb = ps[2 + j % 2]
                    for c in range(8):
                        k.op("pe", lambda e: e.matmul(pb[:, :W], wmq[:, c, j * 128:(j + 1) * 128], ht[:, c, :W], start=(c == 0), stop=(c == 7)), reads=[wmq, ht.subs[c]], writes=[pb])
                    if j % 2 == 0:
                        k.op("act", lambda e: e.copy(out=qx[:, j, :W], in_=pb[:, :W]), reads=[pb], writes=[qx.subs[j]])
                    else:
                        k.op("dve", lambda e: e.tensor_copy(out=qx[:, j, :W], in_=pb[:, :W]), reads=[pb], writes=[qx.subs[j]])
                for (c0, n, mk_, mv_) in segs:
                    for h in range(4):
                        es_ = []
                        for mb in range(2):
                            st = ps[(h % 2) * 2 + mb]
                            e_ = E[(h % 2) * 2 + mb]
                            for dc in range(2):
                                k.op("pe", lambda e: e.matmul(st[:, :n], mk_[:, h * 2 + dc, mb * 128:(mb + 1) * 128], qx[:, h * 2 + dc, c0:c0 + n], start=(dc == 0), stop=(dc == 1)),
                                     reads=[mk_, qx.subs[h * 2 + dc]], writes=[st])
                            k.op("act", lambda e: e.activation(out=e_[:, :n], in_=st[:, :n], func=AF.Exp, scale=1.0 / 16.0), reads=[st], writes=[e_])
                            es_.append(e_)
                        pl = ps[4]
                        for mb in range(2):
                            k.op("pe", lambda e: e.matmul(pl[:, :n], oneb[:], es_[mb][:, :n], start=(mb == 0), stop=(mb == 1)), reads=[oneb, es_[mb]], writes=[pl])
                        k.op("dve", lambda e: e.reciprocal(out=rl[:, :n], in_=pl[:, :n]), reads=[pl], writes=[rl])
                        for dc in range(2):
                            po = ps[5 + dc]
                            for mb in range(2):
                                k.op("pe", lambda e: e.matmul(po[:, :n], mv_[:, mb, h * 256 + dc * 128:h * 256 + (dc + 1) * 128], es_[mb][:, :n], start=(mb == 0), stop=(mb == 1)), reads=[mv_, es_[mb]], writes=[po])
                            k.op("dve", lambda e: e.tensor_tensor(out=ox[:, h * 2 + dc, c0:c0 + n], in0=po[:, :n], in1=rl[:, :n], op=ALU.mult), reads=[po, rl], writes=[ox.subs[h * 2 + dc]])
                for m in range(8):
                    pb = ps[m % 2]
                    for j in range(8):
                        k.op("pe", lambda e: e.matmul(pb[:, :W], wmo[:, j, m * 128:(m + 1) * 128], ox[:, j, :W], start=(j == 0), stop=(j == 7)), reads=[wmo, ox.subs[j]], writes=[pb])
                    k.op("dve", lambda e: e.tensor_tensor(out=xt[:, m, :W], in0=pb[:, :W], in1=xt[:, m, :W], op=ALU.add), reads=[pb, xt], writes=[xt])
                k.dma("sp", xTv[:, :, t0:t0 + W], xt[:, :, :W], xt, reads=[xt])
            k.barrier()
            for b_ in [wout, wmq, wmo, xt, ym] + mvs:
                k.release_dsem(b_)

    def phase_out(self):
        k, cfg = self.k, self.cfg
        xTv = self.xT.rearrange("(c p) t -> p c t", p=128)
        W = 512
        with ExitStack() as es:
            self.epsb = k.sb([128, 1], F32, es=es)
            k.op("dve", lambda e: e.memset(self.epsb[:], EPS), writes=[self.epsb])
            xt = k.sb([128, 8, W], F32, es=es)
            sq = k.sb([128, 8, W], BF16, n=8, es=es)
            rstd = k.sb([128, W], F32, es=es)
            yt = k.sb([128, 8, W], F32, n=8, es=es)
            ob = [k.sb([128, D], F32, es=es) for _ in range(2)]
            nb = 0
            for (t0, Wt) in self.tiles(W):
                k.dma("sp", xt[:, :, :Wt], xTv[:, :, t0:t0 + Wt], xt, writes=[xt])
                self.rmsnorm_fm(xt, yt, Wt, lambda c: self.fing[:, c:c + 1], sq, rstd, gbuf=self.fing)
                for tb in range(Wt // 128):
                    o_ = ob[nb % 2]
                    for g in range(2):
                        pb = self.ps[4 + (nb * 2 + g) % 4]
                        for c in range(4):
                            cc = g * 4 + c
                            k.op("pe", lambda e: e.transpose(pb[:, c * 128:(c + 1) * 128], yt[:, cc, tb * 128:(tb + 1) * 128], self.ident[:]),
                                 reads=[yt.subs[cc], self.ident], writes=[pb])
                        if g == 0:
                            k.op("act", lambda e: e.copy(out=o_[:, 0:512], in_=pb[:]), reads=[pb], writes=[o_])
                        else:
                            k.op("dve", lambda e: e.tensor_copy(out=o_[:, 512:1024], in_=pb[:]), reads=[pb], writes=[o_])
                    tg = t0 + tb * 128
                    if tg < cfg.seq:
                        dst = self.y_p[tg:tg + 128, :]
                    else:
                        dst = self.y_s[tg - cfg.seq:tg - cfg.seq + 128, :]
                    k.dma("sp", dst, o_[:], o_, reads=[o_])
                    nb += 1
            k.barrier()
            for b_ in [xt] + ob:
                k.release_dsem(b_)


def host_consts():
    c = np.zeros((128, 512), np.float32)
    c[:, 0:128] = np.eye(128, dtype=np.float32)
    s_ = np.arange(128)[:, None] % 64
    t_ = np.arange(64)[None, :]
    c[:, 128:192] = (s_ < t_)
    c[:, 192:256] = (s_ <= t_)
    c[:, 256:320] = (s_ > t_)
    blk = (np.arange(128)[:, None] // 64) == (np.arange(128)[None, :] // 64)
    c[:, 384:512] = blk
    return c


def fm_vec(v):
    return np.ascontiguousarray(v.reshape(-1, 128).T)


def make_in_maps(cfg, inp, n_cores=8):
    L = cfg.depth
    pvec = np.zeros((L, 128, 128), np.float32)
    for l in range(L):
        pvec[l, :, 0:8] = fm_vec(inp["ffn1_norm"][l])
        pvec[l, :, 8:16] = fm_vec(inp["ffn2_norm"][l])
        pvec[l, :, 16:24] = fm_vec(inp["mix_norm"][l])
        pvec[l, :, 24:32] = fm_vec(inp["xattn_norm"][l])
        pvec[l, :, 32:40] = fm_vec(inp["mem_kv_norm"][l])
        pvec[l, :, 40:42] = fm_vec(inp["q_norm"][l])
        pvec[l, :, 42:43] = fm_vec(inp["kv_norm"][l])
        pvec[l, :, 48:62] = fm_vec(inp["shift_mu"][l])
        for j, n_ in enumerate(["w0", "a0", "k_k", "k_a", "r_k", "gn_gain", "gn_bias"]):
            pvec[l, :, 62 + 4 * j:66 + 4 * j] = fm_vec(inp[n_][l].reshape(-1))
    w_in = inp["w_in"][:L]
    pos = np.concatenate([np.arange(cfg.seq)] + [cfg.past + np.arange(64)] * cfg.nss).astype(np.float32)
    inv = (np.float32(10000.0) ** (-np.arange(16, dtype=np.float32) / np.float32(16))).astype(np.float32)
    ang = (pos[None, :] * inv[:, None]).astype(np.float32)
    cosf = np.tile(np.cos(ang).astype(np.float32), (8, 1))
    sinf = np.tile(np.sin(ang).astype(np.float32), (8, 1))
    shared = {
        "consts": host_consts(), "pvec": pvec, "fin_g": fm_vec(inp["final_norm"]),
        "w_gate1": inp["ffn1_w_gate"][:L], "w_up1": inp["ffn1_w_up"][:L], "w_down1": inp["ffn1_w_down"][:L],
        "w_gate2": inp["ffn2_w_gate"][:L], "w_up2": inp["ffn2_w_up"][:L], "w_down2": inp["ffn2_w_down"][:L],
        "w_cq": w_in[:, :, 0:256], "w_ckv": w_in[:, :, 256:384], "w_kr4": np.tile(w_in[:, :, 384:416], (1, 1, 4)),
        "w_pb": w_in[:, :, 416:], "w_uq": inp["w_uq"][:L],
        "w_ukT": inp["w_uk"][:L].transpose(0, 3, 1, 2), "w_uv": inp["w_uv"][:L].transpose(0, 2, 1, 3).reshape(L, 128, 512),
        "rope_cs": np.stack([cosf, sinf]),
        "w_wup": inp["w_up"][:L], "w_aup": inp["a_up"][:L], "w_gup": inp["g_up"][:L],
        "w_out": inp["w_out"][:L], "w_mq": inp["w_mq"][:L], "w_mk": inp["w_mk"][:L], "w_mv": inp["w_mv"][:L], "w_mo": inp["w_mo"][:L],
    }
    shared = {k_: np.ascontiguousarray(v, dtype=np.float32) for k_, v in shared.items()}
    maps = []
    nb = inp["x_prompt"].shape[0]
    for c in range(n_cores):
        m = dict(shared)
        b = c % nb
        m["x_p"] = np.ascontiguousarray(inp["x_prompt"][b, :cfg.seq])
        sl = slice(c * cfg.nss, (c + 1) * cfg.nss)
        m["x_s"] = np.ascontiguousarray(inp["x_sample"][sl].reshape(cfg.nss * 64, D))
        m["st_wkv"] = np.ascontiguousarray(inp["state_wkv"][:L, sl])
        m["st_sh"] = np.ascontiguousarray(inp["state_shift"][:L, sl, 0].reshape(L, cfg.nss, 14, 128).transpose(0, 1, 3, 2))
        m["mem_p"] = np.ascontiguousarray(inp["mem_prompt"][b])
        m["c_mk"] = np.ascontiguousarray(inp["cache_mem_k"][:L, sl].reshape(L, cfg.nss, N_MEM, D))
        m["c_mv"] = np.ascontiguousarray(inp["cache_mem_v"][:L, sl].reshape(L, cfg.nss, N_MEM, D))
        m["c_ckv"] = np.ascontiguousarray(inp["cache_ckv"][:L, sl])
        m["c_kr"] = np.ascontiguousarray(inp["cache_krope"][:L, sl])
        maps.append(m)
    return maps


def run(cfg, inp, n_cores=8):
    prog = Prog(cfg)
    nc = prog.build()
    maps = make_in_maps(cfg, inp, n_cores)
    res = run_bass_kernel_spmd(nc, maps, core_ids=list(range(n_cores)))
    return res.results


def assemble(cfg, res, n_cores=8, nb=4):
    L, nss = cfg.depth, cfg.nss
    g = lambda c, n, shp: res[c][n] if n in res[c] else np.zeros(shp, np.float32)
    P = range(nb)
    C = range(n_cores)
    y_p = np.stack([res[b]["y_p"] for b in P])
    y_s = np.concatenate([res[c]["y_s"].reshape(nss, 64, D) for c in C])
    ckv_p = np.stack([res[b]["o_ckv_p"] for b in P], axis=1)
    kr_p = np.stack([res[b]["o_kr_p"] for b in P], axis=1)
    mk_p = np.stack([res[b]["o_mk"].reshape(L, N_MEM, MEM_HEADS, MEM_HD) for b in P], axis=1)
    mv_p = np.stack([res[b]["o_mv"].reshape(L, N_MEM, MEM_HEADS, MEM_HD) for b in P], axis=1)
    wkv_p = np.stack([g(b, "o_wkv_p", (L, 8, 64, 64)) for b in P], axis=1)
    sh_p = np.stack([g(b, "o_sh_p", (L, 1, D_SHIFT)) for b in P], axis=1)
    ckv_s = np.concatenate([res[c]["o_ckv_s"].reshape(L, nss, 64, 128) for c in C], axis=1)
    kr_s = np.concatenate([res[c]["o_kr_s"].reshape(L, nss, 64, 32) for c in C], axis=1)
    wkv_s = np.concatenate([g(c, "o_wkv_s", (L, nss, 8, 64, 64)) for c in C], axis=1)
    sh_s = np.concatenate([g(c, "o_sh_s", (L, nss, 1, D_SHIFT)) for c in C], axis=1)
    outs = (y_p, y_s, ckv_p, kr_p, mk_p, mv_p, wkv_p, sh_p, ckv_s, kr_s, wkv_s, sh_s)
    return tuple(np.ascontiguousarray(o, dtype=np.float32) for o in outs)


def kernel(**inputs):
    cfg = Cfg()
    inp = {k_: np.asarray(v) for k_, v in inputs.items()}
    res = run(cfg, inp)
    return assemble(cfg, res)
```

```python
import math
from contextlib import ExitStack
import numpy as np
import concourse.bass as bass
import concourse.mybir as mybir
from concourse.bass_utils import run_bass_kernel_spmd

F32 = mybir.dt.float32
BF16 = mybir.dt.bfloat16
AF = mybir.ActivationFunctionType
ALU = mybir.AluOpType
AX = mybir.AxisListType

D = 1024
DFF = 2816
H_A = 8
DN = 64
DR = 32
DV = 64
Q_RANK = 256
KV_RANK = 128
H_B = 8
N_B = 64
D_B = 512
D_SHIFT = 1792
N_MEM = 256
MEM_HEADS = 4
MEM_HD = 256
CH = 64
EPS = 1e-6
GN_EPS = 64e-5
MLA_SCALE = 1.0 / math.sqrt(DN + DR)
DECAY_C = math.exp(-0.5)


class Cfg:
    def __init__(self, depth=4, seq=4096, nss=4, past=2048, debug=False):
        self.depth, self.seq, self.nss, self.past = depth, seq, nss, past
        self.debug = debug
        self.zero_yb = False
        self.skip = set()
        self.xstop = 9
        self.rstop = 9
        self.ts = 64
        self.tt = seq + nss * 64


class Trk:
    __slots__ = ("w", "rs", "excl")

    def __init__(self, excl=False):
        self.w = None
        self.rs = []
        self.excl = excl


class Buf:
    def __init__(self, t, n=0, excl=False):
        self.t = t
        self.trk = Trk(excl)
        self.subs = [Trk(excl) for _ in range(n)]
        self.dsem = None
        self.dcnt = 0

    def __getitem__(self, k):
        return self.t[k]


class KB:
    ENGS = ("pe", "act", "dve", "pool", "sp")

    def __init__(self, nc, es):
        self.nc, self.es = nc, es
        self.eng = {"pe": nc.tensor, "act": nc.scalar, "dve": nc.vector, "pool": nc.gpsimd, "sp": nc.sync}
        self.sem = {e: es.enter_context(nc.semaphore("sem_" + e)) for e in self.ENGS}
        self.cnt = {e: 0 for e in self.ENGS}
        self.seen = {e: {} for e in self.ENGS}
        self.semname = {}
        self.dma_pending = {}
        self.free_dsems = {}
        self.nbuf = 0

    def sb(self, shape, dt, n=0, es=None):
        self.nbuf += 1
        t = (es or self.es).enter_context(self.nc.sbuf_tensor("sb%d" % self.nbuf, list(shape), dt))
        return Buf(t, n)

    def psb(self, shape, dt, n=0, es=None):
        self.nbuf += 1
        t = (es or self.es).enter_context(self.nc.psum_tensor("ps%d" % self.nbuf, list(shape), dt))
        return Buf(t, n, excl=True)

    def _dsem(self, buf, kind):
        if buf.dsem is None:
            buf.dsem = {}
            buf.dcnt = {}
        if kind not in buf.dsem:
            fl = self.free_dsems.setdefault(kind, [])
            if fl:
                buf.dsem[kind], buf.dcnt[kind] = fl.pop()
            else:
                self.nbuf += 1
                buf.dsem[kind] = self.es.enter_context(self.nc.semaphore("dsem%d" % self.nbuf))
                buf.dcnt[kind] = 0
        return buf.dsem[kind]

    def release_dsem(self, buf):
        if buf.dsem is not None:
            for kind in buf.dsem:
                self.free_dsems.setdefault(kind, []).append((buf.dsem[kind], buf.dcnt[kind]))
            buf.dsem = None

    def _wait(self, e, ev):
        if ev is None:
            return
        sem, val = ev
        k = id(sem)
        if self.seen[e].get(k, 0) >= val:
            return
        self.seen[e][k] = val
        self.eng[e].wait_ge(sem, val)

    def _deps(self, e, reads, writes):
        need = {}

        def add(ev):
            if ev is None or (e == "pe" and ev[0] is self.sem["pe"]):
                return
            k_ = id(ev[0])
            if k_ not in need or need[k_][1] < ev[1]:
                need[k_] = ev
        for tr in reads:
            add(tr.w)
        for tr in writes:
            add(tr.w)
            for r in tr.rs:
                add(r)
        for ev in need.values():
            self._wait(e, ev)

    def _commit(self, ev, reads, writes):
        for tr in reads:
            tr.rs.append(ev)
            if len(tr.rs) > 24:
                best = {}
                for s, v in tr.rs:
                    if id(s) not in best or best[id(s)][1] < v:
                        best[id(s)] = (s, v)
                tr.rs = list(best.values())
        for tr in writes:
            tr.w = ev
            tr.rs = []

    @staticmethod
    def _trks(lst):
        out = []
        for b in lst:
            out.append(b.trk if isinstance(b, Buf) else b)
        return out

    def op(self, e, fn, reads=(), writes=()):
        reads, writes = self._trks(reads), self._trks(writes)
        writes = writes + [t for t in reads if t.excl]
        reads = [t for t in reads if not t.excl]
        self._deps(e, reads, writes)
        inst = fn(self.eng[e])
        self.cnt[e] += 1
        inst.then_inc(self.sem[e], 1)
        self._commit((self.sem[e], self.cnt[e]), reads, writes)
        return inst

    def dma(self, e, out, in_, sbuf_buf, reads=(), writes=(), **kw):
        reads, writes = self._trks(reads), self._trks(writes)
        self._deps(e, reads, writes)
        kind = "sw" if e == "pool" else "hw"
        sem = self._dsem(sbuf_buf, kind)
        inst = self.eng[e].dma_start(out=out, in_=in_, **kw)
        sbuf_buf.dcnt[kind] += 16
        inst.then_inc(sem, 16)
        ev = (sem, sbuf_buf.dcnt[kind])
        self.dma_pending[id(sem)] = ev
        self._commit(ev, reads, writes)

    def barrier(self):
        sp = "sp"
        for ev in self.dma_pending.values():
            self._wait(sp, ev)
        self.dma_pending = {}
        for e in self.ENGS:
            if e != sp and self.cnt[e] > 0:
                self._wait(sp, (self.sem[e], self.cnt[e]))
        self.eng[sp].sem_inc(self.sem[sp], 1)
        self.cnt[sp] += 1
        for e in self.ENGS:
            if e != sp:
                self._wait(e, (self.sem[sp], self.cnt[sp]))


def col_groups(w, g=512):
    return [(s, min(g, w - s)) for s in range(0, w, g)]


class Prog:
    def __init__(self, cfg):
        self.cfg = cfg
        self.nc = bass.Bass("TRN2", target_bir_lowering=False)
        self.inputs = {}
        self.outputs = {}

    def din(self, name, shape, dt=F32):
        ap = self.nc.dram_tensor(name, list(shape), dt, kind="ExternalInput").ap()
        self.inputs[name] = ap
        return ap

    def dout(self, name, shape, dt=F32):
        ap = self.nc.dram_tensor(name, list(shape), dt, kind="ExternalOutput").ap()
        self.outputs[name] = ap
        return ap

    def dscr(self, name, shape, dt):
        return self.nc.dram_tensor(name, list(shape), dt, kind="Internal").ap()

    def build(self):
        cfg = self.cfg
        nc = self.nc
        L, SEQ, NSS, PAST, TT = cfg.depth, cfg.seq, cfg.nss, cfg.past, cfg.tt
        I = self.din
        self.x_p = I("x_p", [SEQ, D])
        self.x_s = I("x_s", [NSS * 64, D])
        self.consts = I("consts", [128, 512])
        self.pvec = I("pvec", [L, 128, 128])
        self.fin_g = I("fin_g", [128, 8])
        self.w_gate = [I("w_gate%d" % f, [L, D, DFF]) for f in (1, 2)]
        self.w_up = [I("w_up%d" % f, [L, D, DFF]) for f in (1, 2)]
        self.w_down = [I("w_down%d" % f, [L, DFF, D]) for f in (1, 2)]
        self.w_cq = I("w_cq", [L, D, 256])
        self.w_ckv = I("w_ckv", [L, D, 128])
        self.w_kr4 = I("w_kr4", [L, D, 128])
        self.w_pb = I("w_pb", [L, D, D_SHIFT])
        self.w_uq = I("w_uq", [L, 256, 768])
        self.w_ukT = I("w_ukT", [L, 64, 8, 128])
        self.w_uv = I("w_uv", [L, 128, 512])
        self.rope_cs = I("rope_cs", [2, 128, TT])
        self.w_out = I("w_out", [L, D, D])
        self.w_mq = I("w_mq", [L, D, D])
        self.w_mk = I("w_mk", [L, D, D])
        self.w_mv = I("w_mv", [L, D, D])
        self.w_mo = I("w_mo", [L, D, D])
        self.w_wup = I("w_wup", [L, 64, 512])
        self.w_aup = I("w_aup", [L, 64, 512])
        self.w_gup = I("w_gup", [L, 128, 512])
        self.st_wkv = I("st_wkv", [L, NSS, 8, 64, 64])
        self.st_sh = I("st_sh", [L, NSS, 128, 14])
        self.mem_p = I("mem_p", [N_MEM, D])
        self.c_mk = I("c_mk", [L, NSS, N_MEM, D])
        self.c_mv = I("c_mv", [L, NSS, N_MEM, D])
        self.c_ckv = I("c_ckv", [L, NSS, PAST, 128])
        self.c_kr = I("c_kr", [L, NSS, PAST, 32])
        O = self.dout
        self.y_p = O("y_p", [SEQ, D])
        self.y_s = O("y_s", [NSS * 64, D])
        self.o_ckv_p = O("o_ckv_p", [L, SEQ, 128])
        self.o_kr_p = O("o_kr_p", [L, SEQ, 32])
        self.o_mk = O("o_mk", [L, N_MEM, D])
        self.o_mv = O("o_mv", [L, N_MEM, D])
        self.o_wkv_p = O("o_wkv_p", [L, 8, 64, 64])
        self.o_sh_p = O("o_sh_p", [L, 1, D_SHIFT])
        self.o_wkv_s = O("o_wkv_s", [L, NSS, 8, 64, 64])
        self.o_sh_s = O("o_sh_s", [L, NSS, 1, D_SHIFT])
        self.o_ckv_s = O("o_ckv_s", [L, NSS * 64, 128])
        self.o_kr_s = O("o_kr_s", [L, NSS * 64, 32])
        self.xT = self.dscr("xT", [D, TT], F32)
        self.qlatT = self.dscr("qlatT", [8 * 128, TT], BF16)
        self.qropeT = self.dscr("qropeT", [2, 128, TT], BF16)
        self.ckvT = self.dscr("ckvT", [128, TT], BF16)
        self.kropeT = self.dscr("kropeT", [128, TT], BF16)
        self.pbT = self.dscr("pbT", [D_SHIFT, TT], F32)
        if cfg.debug:
            self.ymixT = O("ymixT", [D, TT], BF16)
        else:
            self.ymixT = self.dscr("ymixT", [D, TT], BF16)

        with ExitStack() as es:
            k = self.k = KB(nc, es)
            self.ident = k.sb([128, 128], F32)
            self.onesb = k.sb([128, 128], BF16)
            self.ones1 = k.sb([128, 128], F32)
            k.dma("sp", self.ident[:], self.consts[:, 0:128], self.ident, writes=[self.ident])
            k.op("dve", lambda e: e.memset(self.onesb[:], 1.0 / 1024), writes=[self.onesb])
            k.op("dve", lambda e: e.memset(self.ones1[:], 1.0), writes=[self.ones1])
            self.cmask = k.sb([128, 384], F32)
            k.dma("sp", self.cmask[:], self.consts[:, 128:512], self.cmask, writes=[self.cmask])
            self.fing = k.sb([128, 8], F32)
            k.dma("sp", self.fing[:], self.fin_g, self.fing, writes=[self.fing])
            self.pv = k.sb([128, L, 128], F32)
            k.dma("sp", self.pv[:], self.pvec.rearrange("l p c -> p l c"), self.pv, writes=[self.pv])
            self.ps = [k.psb([128, 512], F32) for _ in range(8)]
            k.barrier()

            self.phase_in()
            if cfg.zero_yb:
                with ExitStack() as es0:
                    z = k.sb([128, 4, TT], BF16, es=es0)
                    k.op("dve", lambda e: e.memset(z[:], 0.0), writes=[z])
                    k.dma("sp", self.ymixT[512:1024, :].rearrange("(c p) t -> p c t", p=128), z[:], z, reads=[z])
                    k.barrier()
                    k.release_dsem(z)
            for l in range(L):
                self.phase_ffn(l, 0)
                self.phase_proj(l)
                self.phase_mla(l)
                if "rwkv" not in cfg.skip:
                    self.phase_rwkv(l)
                if "xattn" not in cfg.skip:
                    self.phase_xattn(l)
                self.phase_ffn(l, 1)
            self.phase_out()
            k.barrier()
        return nc

    def warm(self, buf, lhsT, rhs, n=28, bank=7):
        k = self.k
        pb = self.ps[bank]
        for i in range(n):
            k.op("pe", lambda e: e.matmul(pb[:, :], lhsT, rhs, start=True, stop=True), reads=[buf], writes=[pb])

    def tiles(self, w=1024):
        cfg = self.cfg
        out = []
        for s in range(0, cfg.seq, w):
            out.append((s, min(w, cfg.seq - s)))
        out.append((cfg.seq, cfg.nss * 64))
        return out

    def phase_in(self):
        k, cfg = self.k, self.cfg
        with ExitStack() as es:
            NB = 2
            tin = [k.sb([128, D], F32, es=es) for _ in range(NB)]
            tout = [k.sb([128, 8, 128], F32, es=es) for _ in range(NB)]
            blocks = [(self.x_p, i * 128, i * 128) for i in range(cfg.seq // 128)]
            blocks += [(self.x_s, i * 128, cfg.seq + i * 128) for i in range(cfg.nss * 64 // 128)]
            for bi, (src, r0, c0) in enumerate(blocks):
                a, b = tin[bi % NB], tout[bi % NB]
                k.dma("sp", a[:], src[r0:r0 + 128, :], a, writes=[a])
                for g in range(2):
                    pb = self.ps[(bi * 2 + g) % 8]
                    for c in range(4):
                        cc = g * 4 + c
                        k.op("pe", lambda e: e.transpose(pb[:, c * 128:(c + 1) * 128], a[:, cc * 128:(cc + 1) * 128], self.ident[:]),
                             reads=[a, self.ident], writes=[pb])
                    eng = "act" if g == 0 else "dve"
                    if eng == "act":
                        k.op("act", lambda e: e.copy(out=b[:, g * 4:(g + 1) * 4, :], in_=pb[:].rearrange("p (c t) -> p c t", c=4)), reads=[pb], writes=[b])
                    else:
                        k.op("dve", lambda e: e.tensor_copy(out=b[:, g * 4:(g + 1) * 4, :], in_=pb[:].rearrange("p (c t) -> p c t", c=4)), reads=[pb], writes=[b])
                k.dma("sp", self.xT.rearrange("(c p) t -> p c t", p=128)[:, :, c0:c0 + 128], b[:], b, reads=[b])
            k.barrier()
            for b_ in tin + tout:
                k.release_dsem(b_)

    def rmsnorm_fm(self, xt, ht, W, gcol, sq, rstd, gbuf=None):
        k = self.k
        gbuf = gbuf or self.pv
        for c in range(8):
            k.op("act", lambda e: e.activation(out=sq[:, c, :W], in_=xt[:, c, :W], func=AF.Square), reads=[xt], writes=[sq.subs[c]])
        for gi, (s0, n) in enumerate(col_groups(W)):
            pb = self.ps[gi % 2]
            for c in range(8):
                k.op("pe", lambda e: e.matmul(pb[:, :n], self.onesb[:], sq[:, c, s0:s0 + n], start=(c == 0), stop=(c == 7)),
                     reads=[self.onesb, sq.subs[c]], writes=[pb])
            k.op("act", lambda e: e.activation(out=rstd[:, s0:s0 + n], in_=pb[:, :n], func=AF.Sqrt, bias=self.epsb[:, 0:1], scale=1.0), reads=[pb, self.epsb], writes=[rstd])
        k.op("dve", lambda e: e.reciprocal(out=rstd[:, :W], in_=rstd[:, :W]), reads=[rstd], writes=[rstd])
        for c in range(8):
            k.op("dve", lambda e: e.scalar_tensor_tensor(out=ht[:, c, :W], in0=xt[:, c, :W], scalar=gcol(c), in1=rstd[:, :W], op0=ALU.mult, op1=ALU.mult),
                 reads=[xt, rstd, gbuf], writes=[ht.subs[c]])

    def phase_ffn(self, l, f):
        k, cfg = self.k, self.cfg
        WMAX = 1024
        xTv = self.xT.rearrange("(c p) t -> p c t", p=128)
        with ExitStack() as es:
            self.epsb = k.sb([128, 1], F32, es=es)
            k.op("dve", lambda e: e.memset(self.epsb[:], EPS), writes=[self.epsb])
            wd = k.sb([128, 22, D], BF16, es=es)
            wdv = self.w_down[f][l].rearrange("(c p) n -> p c n", p=128)
            for c0 in range(0, 22, 2):
                k.dma("pool", wd[:, c0:c0 + 2, :], wdv[:, c0:c0 + 2, :], wd, writes=[wd])
            xt = k.sb([128, 8, WMAX], F32, es=es)
            ht = k.sb([128, 8, WMAX], BF16, n=8, es=es)
            sq = k.sb([128, 8, WMAX], BF16, n=8, es=es)
            rstd = k.sb([128, WMAX], F32, es=es)
            at = k.sb([128, 22, WMAX], BF16, n=22, es=es)
            sg = [k.sb([128, WMAX], F32, es=es) for _ in range(2)]
            NWB = 3
            wg = [k.sb([128, 8, 256], BF16, es=es) for _ in range(NWB)]
            wu = [k.sb([128, 8, 256], BF16, es=es) for _ in range(NWB)]
            wgv = self.w_gate[f][l].rearrange("(c p) n -> p c n", p=128)
            wuv = self.w_up[f][l].rearrange("(c p) n -> p c n", p=128)
            nblk = 0
            for (t0, W) in self.tiles(WMAX):
                k.dma("sp", xt[:, :, :W], xTv[:, :, t0:t0 + W], xt, writes=[xt])
                self.rmsnorm_fm(xt, ht, W, lambda c: self.pv[:, l, f * 8 + c:f * 8 + c + 1], sq, rstd)
                cg = col_groups(W)
                for jb in range(11):
                    g_, u_ = wg[nblk % NWB], wu[nblk % NWB]
                    nblk += 1
                    k.dma("pool", g_[:], wgv[:, :, jb * 256:(jb + 1) * 256], g_, writes=[g_])
                    k.dma("pool", u_[:], wuv[:, :, jb * 256:(jb + 1) * 256], u_, writes=[u_])
                    for jj in range(2):
                        j = jb * 2 + jj
                        par = j % 2
                        pg = [self.ps[par * 4 + i] for i in range(2)]
                        pu = [self.ps[par * 4 + 2 + i] for i in range(2)]
                        for gi, (s0, n) in enumerate(cg):
                            for c in range(8):
                                k.op("pe", lambda e: e.matmul(pg[gi][:, :n], g_[:, c, jj * 128:(jj + 1) * 128], ht[:, c, s0:s0 + n], start=(c == 0), stop=(c == 7)),
                                     reads=[g_, ht.subs[c]], writes=[pg[gi]])
                            for c in range(8):
                                k.op("pe", lambda e: e.matmul(pu[gi][:, :n], u_[:, c, jj * 128:(jj + 1) * 128], ht[:, c, s0:s0 + n], start=(c == 0), stop=(c == 7)),
                                     reads=[u_, ht.subs[c]], writes=[pu[gi]])
                        s_ = sg[par]
                        for gi, (s0, n) in enumerate(cg):
                            k.op("act", lambda e: e.activation(out=s_[:, s0:s0 + n], in_=pg[gi][:, :n], func=AF.Silu), reads=[pg[gi]], writes=[s_])
                            k.op("dve", lambda e: e.tensor_tensor(out=at[:, j, s0:s0 + n], in0=s_[:, s0:s0 + n], in1=pu[gi][:, :n], op=ALU.mult),
                                 reads=[s_, pu[gi]], writes=[at.subs[j]])
                for m in range(8):
                    par = m % 2
                    pd = [self.ps[par * 4 + i] for i in range(2)]
                    for gi, (s0, n) in enumerate(cg):
                        for c in range(22):
                            k.op("pe", lambda e: e.matmul(pd[gi][:, :n], wd[:, c, m * 128:(m + 1) * 128], at[:, c, s0:s0 + n], start=(c == 0), stop=(c == 21)),
                                 reads=[wd, at.subs[c]], writes=[pd[gi]])
                        k.op("dve", lambda e: e.scalar_tensor_tensor(out=xt[:, m, s0:s0 + n], in0=pd[gi][:, :n], scalar=0.5, in1=xt[:, m, s0:s0 + n], op0=ALU.mult, op1=ALU.add),
                             reads=[pd[gi], xt], writes=[xt])
                k.dma("sp", xTv[:, :, t0:t0 + W], xt[:, :, :W], xt, reads=[xt])
            k.barrier()
            for b_ in [wd, xt] + wg + wu:
                k.release_dsem(b_)


    def small_norm(self, src, dst, nch, W, ones_ap, gcol, sq, rstd, pbank, gbuf=None):
        k = self.k
        gbuf = gbuf or self.pv
        for c in range(nch):
            k.op("act", lambda e: e.activation(out=sq[:, c, :W], in_=src[:, c, :W], func=AF.Square), reads=[src], writes=[sq])
        for c in range(nch):
            k.op("pe", lambda e: e.matmul(pbank[:, :W], ones_ap, sq[:, c, :W], start=(c == 0), stop=(c == nch - 1)),
                 reads=[self.onesv, sq], writes=[pbank])
        k.op("act", lambda e: e.activation(out=rstd[:, :W], in_=pbank[:, :W], func=AF.Sqrt, bias=self.epsb[:, 0:1], scale=1.0), reads=[pbank, self.epsb], writes=[rstd])
        k.op("dve", lambda e: e.reciprocal(out=rstd[:, :W], in_=rstd[:, :W]), reads=[rstd], writes=[rstd])
        for c in range(nch):
            k.op("dve", lambda e: e.scalar_tensor_tensor(out=dst[:, c, :W], in0=src[:, c, :W], scalar=gcol(c), in1=rstd[:, :W], op0=ALU.mult, op1=ALU.mult),
                 reads=[src, rstd, gbuf], writes=[dst])

    def phase_proj(self, l):
        k, cfg = self.k, self.cfg
        WM = 512
        xTv = self.xT.rearrange("(c p) t -> p c t", p=128)
        with ExitStack() as es:
            self.epsb = k.sb([128, 1], F32, es=es)
            k.op("dve", lambda e: e.memset(self.epsb[:], EPS), writes=[self.epsb])
            self.onesv = k.sb([128, 2, 128], BF16, es=es)
            k.op("dve", lambda e: e.memset(self.onesv[:, 0, :], 1.0 / 256), writes=[self.onesv])
            k.op("dve", lambda e: e.memset(self.onesv[:, 1, :], 1.0 / 128), writes=[self.onesv])
            fm = lambda ap: ap.rearrange("(c p) n -> p c n", p=128)
            wcq = k.sb([128, 8, 256], BF16, es=es)
            wckv = k.sb([128, 8, 128], BF16, es=es)
            wkr = k.sb([128, 8, 128], BF16, es=es)
            wkrr = k.sb([128, 8, 128], BF16, es=es)
            wpb = k.sb([128, 8, D_SHIFT], BF16, es=es)
            wuq = k.sb([128, 2, 768], BF16, es=es)
            wqro = k.sb([128, 2, 256], BF16, es=es)
            wqrr = k.sb([128, 2, 256], BF16, es=es)
            wuk = k.sb([64, 8, 128], BF16, es=es)
            k.dma("pool", wcq[:], fm(self.w_cq[l]), wcq, writes=[wcq])
            k.dma("pool", wckv[:], fm(self.w_ckv[l]), wckv, writes=[wckv])
            k.dma("pool", wkr[:], fm(self.w_kr4[l]), wkr, writes=[wkr])
            for c0 in range(0, 8, 2):
                k.dma("pool", wpb[:, c0:c0 + 2, :], fm(self.w_pb[l])[:, c0:c0 + 2, :], wpb, writes=[wpb])
            k.dma("pool", wuq[:], fm(self.w_uq[l]), wuq, writes=[wuq])
            k.dma("pool", wuk[:], self.w_ukT[l], wuk, writes=[wuk])
            wuq_r = wuq[:].rearrange("p c (h d) -> p c h d", d=96)[:, :, :, 64:96]
            wqro_v = wqro[:].rearrange("p c (h d) -> p c h d", d=32)
            wqrr_v = wqrr[:].rearrange("p c (h d) -> p c h d", d=32)
            for c in range(2):
                k.op("act", lambda e: e.copy(out=wqro_v[:, c], in_=wuq_r[:, c]), reads=[wuq], writes=[wqro])
                k.op("act", lambda e: e.mul(wqrr_v[:, c, :, 0:16], wuq_r[:, c, :, 16:32], -1.0), reads=[wuq], writes=[wqrr])
                k.op("act", lambda e: e.copy(out=wqrr_v[:, c, :, 16:32], in_=wuq_r[:, c, :, 0:16]), reads=[wuq], writes=[wqrr])
            wkr_v = wkr[:].rearrange("p c (h d) -> p c h d", d=32)
            wkrr_v = wkrr[:].rearrange("p c (h d) -> p c h d", d=32)
            for c0 in range(0, 8, 4):
                k.op("act", lambda e: e.mul(wkrr_v[:, c0:c0 + 4, :, 0:16].rearrange("p c h d -> p (c h) d"), wkr_v[:, c0:c0 + 4, :, 16:32].rearrange("p c h d -> p (c h) d"), -1.0), reads=[wkr], writes=[wkrr])
                k.op("act", lambda e: e.copy(out=wkrr_v[:, c0:c0 + 4, :, 16:32].rearrange("p c h d -> p (c h) d"), in_=wkr_v[:, c0:c0 + 4, :, 0:16].rearrange("p c h d -> p (c h) d")), reads=[wkr], writes=[wkrr])
            xt = k.sb([128, 8, WM], F32, es=es)
            ht = k.sb([128, 8, WM], BF16, n=8, es=es)
            sq = k.sb([128, 8, WM], BF16, n=8, es=es)
            rstd = k.sb([128, WM], F32, es=es)
            cq = k.sb([128, 2, WM], F32, es=es)
            cqn = k.sb([128, 2, WM], BF16, es=es)
            sq2 = k.sb([128, 2, WM], BF16, es=es)
            rstd2 = k.sb([128, WM], F32, es=es)
            qn = k.sb([64, 8, WM], BF16, n=8, es=es)
            qlat = k.sb([128, 8, WM], BF16, n=8, es=es)
            qro = k.sb([128, 2, WM], BF16, es=es)
            cs = k.sb([128, 2, WM], F32, es=es)
            t1 = k.sb([128, WM], F32, es=es)
            t2 = k.sb([128, WM], F32, es=es)
            ckv = k.sb([128, 1, WM], F32, es=es)
            ckvn = k.sb([128, 1, WM], F32, es=es)
            ckvb = k.sb([128, WM], BF16, es=es)
            krf = k.sb([128, WM], F32, es=es)
            krb = k.sb([128, WM], BF16, es=es)
            tmo = [k.sb([128, 160], F32, es=es) for _ in range(2)]
            pbs = k.sb([128, 14, WM], F32, n=14, es=es)
            ps = self.ps
            ntm = 0
            for (t0, W) in self.tiles(WM):
                k.dma("sp", xt[:, :, :W], xTv[:, :, t0:t0 + W], xt, writes=[xt])
                k.dma("sp", cs[:, :, :W], self.rope_cs.rearrange("g p t -> p g t")[:, :, t0:t0 + W], cs, writes=[cs])
                self.rmsnorm_fm(xt, ht, W, lambda c: self.pv[:, l, 16 + c:17 + c], sq, rstd)

                def lin(pb, wt, c0, m, rows=128):
                    for c in range(8):
                        k.op("pe", lambda e: e.matmul(pb[0:rows, :W], wt[:, c, c0:c0 + m], ht[:, c, :W], start=(c == 0), stop=(c == 7)),
                             reads=[wt, ht.subs[c]], writes=[pb])
                for c2 in range(2):
                    lin(ps[c2], wcq, c2 * 128, 128)
                    k.op("act", lambda e: e.copy(out=cq[:, c2, :W], in_=ps[c2][:, :W]), reads=[ps[c2]], writes=[cq])
                self.small_norm(cq, cqn, 2, W, self.onesv[:, 0, :], lambda c: self.pv[:, l, 40 + c:41 + c], sq2, rstd2, ps[2])
                for h in range(8):
                    pb = ps[h % 2]
                    for c in range(2):
                        k.op("pe", lambda e: e.matmul(pb[0:64, :W], wuq[:, c, h * 96:h * 96 + 64], cqn[:, c, :W], start=(c == 0), stop=(c == 1)),
                             reads=[wuq, cqn], writes=[pb])
                    k.op("act", lambda e: e.copy(out=qn[:, h, :W], in_=pb[0:64, :W]), reads=[pb], writes=[qn.subs[h]])
                for h in range(8):
                    pl = ps[2 + h % 2]
                    k.op("pe", lambda e: e.matmul(pl[:, :W], wuk[:, h, :], qn[:, h, :W], start=True, stop=True), reads=[wuk, qn.subs[h]], writes=[pl])
                    k.op("dve", lambda e: e.tensor_copy(out=qlat[:, h, :W], in_=pl[:, :W]), reads=[pl], writes=[qlat.subs[h]])
                k.dma("sp", self.qlatT.rearrange("(h c) t -> c h t", c=128)[:, :, t0:t0 + W], qlat[:, :, :W], qlat, reads=qlat.subs)
                for g in range(2):
                    pr, pt = ps[4 + g], ps[6 + g]
                    for c in range(2):
                        k.op("pe", lambda e: e.matmul(pr[:, :W], wqro[:, c, g * 128:(g + 1) * 128], cqn[:, c, :W], start=(c == 0), stop=(c == 1)), reads=[wqro, cqn], writes=[pr])
                    for c in range(2):
                        k.op("pe", lambda e: e.matmul(pt[:, :W], wqrr[:, c, g * 128:(g + 1) * 128], cqn[:, c, :W], start=(c == 0), stop=(c == 1)), reads=[wqrr, cqn], writes=[pt])
                    k.op("dve", lambda e: e.tensor_tensor(out=t1[:, :W], in0=pr[:, :W], in1=cs[:, 0, :W], op=ALU.mult), reads=[pr, cs], writes=[t1])
                    k.op("dve", lambda e: e.tensor_tensor(out=t2[:, :W], in0=pt[:, :W], in1=cs[:, 1, :W], op=ALU.mult), reads=[pt, cs], writes=[t2])
                    k.op("dve", lambda e: e.tensor_tensor(out=qro[:, g, :W], in0=t1[:, :W], in1=t2[:, :W], op=ALU.add), reads=[t1, t2], writes=[qro])
                k.dma("sp", self.qropeT.rearrange("g p t -> p g t")[:, :, t0:t0 + W], qro[:, :, :W], qro, reads=[qro])
                lin(ps[0], wckv, 0, 128)
                k.op("act", lambda e: e.copy(out=ckv[:, 0, :W], in_=ps[0][:, :W]), reads=[ps[0]], writes=[ckv])
                self.small_norm(ckv, ckvn, 1, W, self.onesv[:, 1, :], lambda c: self.pv[:, l, 42:43], sq2, rstd2, ps[1])
                k.op("act", lambda e: e.copy(out=ckvb[:, :W], in_=ckvn[:, 0, :W]), reads=[ckvn], writes=[ckvb])
                k.dma("sp", self.ckvT[:, t0:t0 + W], ckvb[:, :W], ckvb, reads=[ckvb])
                lin(ps[2], wkr, 0, 128)
                lin(ps[3], wkrr, 0, 128)
                k.op("dve", lambda e: e.tensor_tensor(out=t1[:, :W], in0=ps[2][:, :W], in1=cs[:, 0, :W], op=ALU.mult), reads=[ps[2], cs], writes=[t1])
                k.op("dve", lambda e: e.tensor_tensor(out=t2[:, :W], in0=ps[3][:, :W], in1=cs[:, 1, :W], op=ALU.mult), reads=[ps[3], cs], writes=[t2])
                k.op("dve", lambda e: e.tensor_tensor(out=krf[:, :W], in0=t1[:, :W], in1=t2[:, :W], op=ALU.add), reads=[t1, t2], writes=[krf])
                k.op("act", lambda e: e.copy(out=krb[:, :W], in_=krf[:, :W]), reads=[krf], writes=[krb])
                k.dma("sp", self.kropeT[:, t0:t0 + W], krb[:, :W], krb, reads=[krb])
                for tb in range(W // 128):
                    pb = ps[4 + ntm % 4]
                    o_ = tmo[ntm % 2]
                    ntm += 1
                    k.op("pe", lambda e: e.transpose(pb[:, 0:128], ckvn[:, 0, tb * 128:(tb + 1) * 128], self.ident[:]), reads=[ckvn, self.ident], writes=[pb])
                    k.op("pe", lambda e: e.transpose(pb[:, 128:256], krf[:, tb * 128:(tb + 1) * 128], self.ident[:]), reads=[krf, self.ident], writes=[pb])
                    k.op("act", lambda e: e.copy(out=o_[:, :], in_=pb[:, 0:160]), reads=[pb], writes=[o_])
                    tg = t0 + tb * 128
                    if tg < cfg.seq:
                        d1, d2 = self.o_ckv_p[l, tg:tg + 128, :], self.o_kr_p[l, tg:tg + 128, :]
                    else:
                        d1, d2 = self.o_ckv_s[l, tg - cfg.seq:tg - cfg.seq + 128, :], self.o_kr_s[l, tg - cfg.seq:tg - cfg.seq + 128, :]
                    k.dma("sp", d1, o_[:, 0:128], o_, reads=[o_])
                    k.dma("sp", d2, o_[:, 128:160], o_, reads=[o_])
                for j in range(14):
                    pb = ps[j % 4]
                    lin(pb, wpb, j * 128, 128)
                    if j % 2 == 0:
                        k.op("act", lambda e: e.copy(out=pbs[:, j, :W], in_=pb[:, :W]), reads=[pb], writes=[pbs.subs[j]])
                    else:
                        k.op("dve", lambda e: e.tensor_copy(out=pbs[:, j, :W], in_=pb[:, :W]), reads=[pb], writes=[pbs.subs[j]])
                k.dma("sp", self.pbT.rearrange("(c p) t -> p c t", p=128)[:, :, t0:t0 + W], pbs[:, :, :W], pbs, reads=pbs.subs)
            k.barrier()
            for b_ in [wcq, wckv, wkr, wpb, wuq, wuk, xt, cs, qlat, qro, ckvb, krb, pbs] + tmo:
                k.release_dsem(b_)

    def phase_mla(self, l):
        k, cfg = self.k, self.cfg
        ps = self.ps
        SEQ, NSS, PAST = cfg.seq, cfg.nss, cfg.past
        ymv = self.ymixT[0:512, :].rearrange("(h v) t -> v h t", v=64)
        with ExitStack() as es:
            wuv = k.sb([128, 512], BF16, es=es)
            k.dma("pool", wuv[:], self.w_uv[l], wuv, writes=[wuv])
            TKMAX = max(SEQ, PAST + 64)
            NKBMAX = (TKMAX + 127) // 128
            ckv_all = k.sb([128, NKBMAX * 128], BF16, es=es)
            kr_all = k.sb([128, NKBMAX * 128], BF16, es=es)
            V1 = k.sb([128, NKBMAX, 8, 65], BF16, n=NKBMAX, es=es)
            k.op("pool", lambda e: e.memset(V1[:, :, :, 64:65], 1.0), writes=[V1] + V1.subs)
            qlat = k.sb([128, 8, 512], BF16, es=es)
            qro = k.sb([128, 2, 512], BF16, es=es)
            qro_s = k.sb([128, 8, 64], BF16, es=es)
            k.op("dve", lambda e: e.memset(qro_s[:], 0.0), writes=[qro_s])
            k.op("dve", lambda e: e.memset(kr_all[:], 0.0), writes=[kr_all])
            krz = [k.sb([128, NKBMAX * 128], BF16, es=es) for _ in range(4)]
            for j in range(4):
                k.op("pool" if j % 2 else "dve", lambda e: e.memset(krz[j][:], 0.0), writes=[krz[j]])
            E = [k.sb([128, 512], BF16, es=es) for _ in range(4)]
            rl = k.sb([128, 512], F32, es=es)
            k.op("dve", lambda e: e.memset(rl[:], 0.0), writes=[rl])
            sel = k.sb([128, 128], F32, es=es)
            k.op("dve", lambda e: e.memset(sel[:], 0.0), writes=[sel])
            k.op("dve", lambda e: e.memset(sel[64:65, :], 1.0), writes=[sel])
            bcs = k.sb([64, 512], F32, es=es)
            yat = k.sb([64, 8, 512], BF16, n=8, es=es)
            yas = k.sb([64, 512], BF16, es=es)
            cst = k.sb([128, 4, 128], F32, es=es)
            cstk = k.sb([128, 4, 32], F32, es=es)
            nE = [0]
            nS = [0]

            def build_v(nkb, tk):
                for kb in range(nkb):
                    rows = min(128, tk - kb * 128)
                    pb = ps[4 + kb % 2]
                    k.op("pe", lambda e: e.matmul(pb[0:rows, :], ckv_all[:, kb * 128:kb * 128 + rows], wuv[:], start=True, stop=True), reads=[ckv_all, wuv], writes=[pb])
                    src = pb[0:rows, :].rearrange("p (h v) -> p h v", v=64)
                    if kb % 2 == 0:
                        k.op("act", lambda e: e.copy(out=V1[0:rows, kb, :, 0:64], in_=src), reads=[pb], writes=[V1.subs[kb]])
                    else:
                        k.op("dve", lambda e: e.tensor_copy(out=V1[0:rows, kb, :, 0:64], in_=src), reads=[pb], writes=[V1.subs[kb]])

            def finish(oT, ncol, dst_ap, dst_tr):
                k.op("dve", lambda e: e.reciprocal(out=rl[64:65, :ncol], in_=oT[64:65, :ncol]), reads=[oT], writes=[rl])
                bc = ps[6 + nS[0] % 2]
                nS[0] += 1
                k.op("pe", lambda e: e.matmul(bc[:, :ncol], sel[:, :], rl[:, :ncol], start=True, stop=True), reads=[sel, rl], writes=[bc])
                k.op("act", lambda e: e.copy(out=bcs[:, :ncol], in_=bc[0:64, :ncol]), reads=[bc], writes=[bcs])
                k.op("dve", lambda e: e.tensor_tensor(out=dst_ap, in0=oT[0:64, :ncol], in1=bcs[:, :ncol], op=ALU.mult), reads=[oT, bcs], writes=[dst_tr])

            nkb = SEQ // 128
            k.dma("sp", ckv_all[:, 0:SEQ], self.ckvT[:, 0:SEQ], ckv_all, writes=[ckv_all])
            for j in range(4):
                k.dma("sp", krz[j][32 * j:32 * j + 32, 0:SEQ], self.kropeT[32 * j:32 * j + 32, 0:SEQ], krz[j], writes=[krz[j]])
            build_v(nkb, SEQ)
            for qt in range(SEQ // 512):
                q0 = qt * 512
                k.dma("sp", qlat[:], self.qlatT.rearrange("(h c) t -> c h t", c=128)[:, :, q0:q0 + 512], qlat, writes=[qlat])
                k.dma("sp", qro[:], self.qropeT.rearrange("g p t -> p g t")[:, :, q0:q0 + 512], qro, writes=[qro])
                last = qt * 4 + 3
                units = [(h, kb) for h in range(8) for kb in range(last + 1)]
                SKEW = 3
                slots = {}

                def emit_s(h, kb):
                    ca = 2 * kb - 8 * qt
                    c0 = max(ca, 0) * 64
                    st = ps[nE[0] % 4]
                    e_ = E[nE[0] % 4]
                    nE[0] += 1
                    slots[(h, kb)] = (e_, c0)
                    k.op("pe", lambda e: e.matmul(st[:, c0:512], ckv_all[:, kb * 128:(kb + 1) * 128], qlat[:, h, c0:512], start=True, stop=False), reads=[ckv_all, qlat], writes=[st])
                    k.op("pe", lambda e: e.matmul(st[:, c0:512], krz[h % 4][:, kb * 128:(kb + 1) * 128], qro[:, h // 4, c0:512], start=False, stop=True), reads=[krz[h % 4], qro], writes=[st])
                    if ca < 0:
                        k.op("act", lambda e: e.activation(out=e_[:, :], in_=st[:, :], func=AF.Exp, scale=MLA_SCALE), reads=[st], writes=[e_])
                    else:
                        k.op("act", lambda e: e.activation(out=e_[0:64, c0:512], in_=st[0:64, c0:512], func=AF.Exp, scale=MLA_SCALE), reads=[st], writes=[e_])
                        if c0 + 64 < 512:
                            k.op("act", lambda e: e.activation(out=e_[64:128, c0 + 64:512], in_=st[64:128, c0 + 64:512], func=AF.Exp, scale=MLA_SCALE), reads=[st], writes=[e_])
                        k.op("pool", lambda e: e.memset(e_[64:128, c0:c0 + 64], 0.0), writes=[e_])

                def emit_pv(h, kb):
                    e_, c0 = slots.pop((h, kb))
                    oT = ps[4 + h % 2]
                    k.op("pe", lambda e: e.matmul(oT[0:65, c0:512], V1[:, kb, h, :], e_[:, c0:512], start=(kb == 0), stop=(kb == last), skip_group_check=True),
                         reads=[V1.subs[kb], e_], writes=[oT])
                    if kb == last:
                        finish(oT, 512, yat[:, h, :], yat.subs[h])

                for i in range(len(units) + SKEW):
                    if i >= SKEW:
                        emit_pv(*units[i - SKEW])
                    if i < len(units):
                        emit_s(*units[i])
                k.dma("sp", ymv[:, :, q0:q0 + 512], yat[:], yat, reads=yat.subs)

            npb = PAST // 128
            tk = PAST + 64
            nkb = npb + 1
            for s_ in range(NSS):
                t0 = SEQ + s_ * 64
                for b0 in range(0, npb, 4):
                    nb_ = min(4, npb - b0)
                    k.dma("sp", cst[:, 0:nb_, :], self.c_ckv[l, s_, b0 * 128:(b0 + nb_) * 128, :].rearrange("(b p) c -> p b c", p=128), cst, writes=[cst])
                    k.dma("sp", cstk[:, 0:nb_, :], self.c_kr[l, s_, b0 * 128:(b0 + nb_) * 128, :].rearrange("(b p) c -> p b c", p=128), cstk, writes=[cstk])
                    pa, pk = ps[(b0 // 4) % 2], ps[2 + (b0 // 4) % 2]
                    for b in range(nb_):
                        k.op("pe", lambda e: e.transpose(pa[:, b * 128:(b + 1) * 128], cst[:, b, :], self.ident[:]), reads=[cst, self.ident], writes=[pa])
                    for b in range(nb_):
                        k.op("pe", lambda e: e.transpose(pk[0:32, b * 128:(b + 1) * 128], cstk[:, b, :], self.ident[:]), reads=[cstk, self.ident], writes=[pk])
                    k.op("act", lambda e: e.copy(out=ckv_all[:, b0 * 128:(b0 + nb_) * 128], in_=pa[:, 0:nb_ * 128]), reads=[pa], writes=[ckv_all])
                    k.op("dve", lambda e: e.tensor_copy(out=kr_all[0:32, b0 * 128:(b0 + nb_) * 128], in_=pk[0:32, 0:nb_ * 128]), reads=[pk], writes=[kr_all])
                k.dma("sp", ckv_all[:, PAST:PAST + 64], self.ckvT[:, t0:t0 + 64], ckv_all, writes=[ckv_all])
                k.dma("sp", kr_all[0:32, PAST:PAST + 64], self.kropeT[0:32, t0:t0 + 64], kr_all, writes=[kr_all])
                build_v(nkb, tk)
                k.dma("sp", qlat[:, :, 0:64], self.qlatT.rearrange("(h c) t -> c h t", c=128)[:, :, t0:t0 + 64], qlat, writes=[qlat])
                k.dma("sp", qro_s[0:32, :, :], self.qropeT.rearrange("g (hh j) t -> j (g hh) t", j=32)[:, :, t0:t0 + 64], qro_s, writes=[qro_s])
                oT = ps[4 + s_ % 2]
                slots = {}

                def emit_s2(kb):
                    rows = min(128, tk - kb * 128)
                    st = ps[nE[0] % 4]
                    e_ = E[nE[0] % 4]
                    nE[0] += 1
                    slots[kb] = (e_, rows)
                    stv = st[0:rows, :].rearrange("p (h t) -> p h t", t=64)
                    k.op("pe", lambda e: e.matmul(stv, ckv_all[:, kb * 128:kb * 128 + rows], qlat[:, :, 0:64], start=True, stop=False), reads=[ckv_all, qlat], writes=[st])
                    k.op("pe", lambda e: e.matmul(stv, kr_all[:, kb * 128:kb * 128 + rows], qro_s[:, :, :], start=False, stop=True), reads=[kr_all, qro_s], writes=[st])
                    k.op("act", lambda e: e.activation(out=e_[0:rows, :], in_=st[0:rows, :], func=AF.Exp, scale=MLA_SCALE), reads=[st], writes=[e_])

                def emit_pv2(kb):
                    e_, rows = slots.pop(kb)
                    for h in range(8):
                        k.op("pe", lambda e: e.matmul(oT[0:65, h * 64:(h + 1) * 64], V1[0:rows, kb, h, :], e_[0:rows, h * 64:(h + 1) * 64], start=(kb == 0 and h == 0), stop=(kb == nkb - 1), skip_group_check=True),
                             reads=[V1.subs[kb], e_], writes=[oT])

                for i in range(nkb + 3):
                    if i >= 3:
                        emit_pv2(i - 3)
                    if i < nkb:
                        emit_s2(i)
                finish(oT, 512, yas[:, :], yas.trk)
                k.dma("sp", ymv[:, :, t0:t0 + 64], yas[:, :].rearrange("p (h t) -> p h t", t=64), yas, reads=[yas])
            k.barrier()
            for b_ in [wuv, ckv_all, kr_all, qlat, qro, qro_s, yat, yas, cst, cstk] + krz:
                k.release_dsem(b_)


    def phase_rwkv(self, l):
        k, cfg = self.k, self.cfg
        nc = self.nc
        ps = self.ps
        SEQ, NSS = cfg.seq, cfg.nss
        WM = 256
        pbv = self.pbT.rearrange("(c p) t -> p c t", p=128)
        ybv = self.ymixT[512:1024, :].rearrange("(c p) t -> p c t", p=128)
        M1 = self.cmask[:, 0:128]
        M2 = self.cmask[0:64, 128:192]
        BLK = self.cmask[:, 256:384]
        ID = self.ident
        PVC = lambda a, b: self.pv[:, l, a:b]
        MU, W0, A0, KK_, KA, RK, GG, GB = PVC(48, 62), PVC(62, 66), PVC(66, 70), PVC(70, 74), PVC(74, 78), PVC(78, 82), PVC(82, 86), PVC(86, 90)
        bc3 = lambda ap, W: ap.unsqueeze(2).to_broadcast([128, ap.shape[1], W])
        with ExitStack() as es:
            S4 = lambda n=0: k.sb([128, 4, WM], F32, n=n, es=es)
            wup = k.sb([64, 512], BF16, es=es)
            aup = k.sb([128, 512], BF16, es=es)
            gup = k.sb([128, 512], BF16, es=es)
            k.dma("pool", wup[:], self.w_wup[l], wup, writes=[wup])
            k.dma("pool", aup[64:128, :], self.w_aup[l], aup, writes=[aup])
            k.dma("pool", gup[:], self.w_gup[l], gup, writes=[gup])
            cst = k.sb([128, 24], F32, es=es)
            k.op("dve", lambda e: e.memset(cst[:, 0:1], 1e-12), writes=[cst])
            k.op("dve", lambda e: e.memset(cst[:, 1:2], GN_EPS), writes=[cst])
            k.op("dve", lambda e: e.tensor_scalar(out=cst[:, 2:6], in0=KA, scalar1=-1.0, scalar2=1.0, op0=ALU.mult, op1=ALU.add), reads=[self.pv], writes=[cst])
            OMK = cst[:, 2:6]
            k.op("dve", lambda e: e.tensor_scalar(out=cst[:, 8:22], in0=MU, scalar1=-1.0, scalar2=1.0, op0=ALU.mult, op1=ALU.add), reads=[self.pv], writes=[cst])
            OMM = cst[:, 8:22]
            mask01 = k.sb([128, WM], F32, es=es)
            k.op("dve", lambda e: e.memset(mask01[:], 1.0), writes=[mask01])
            k.op("dve", lambda e: e.memset(mask01[:, 0:WM:64], 0.0), writes=[mask01])
            PB = k.sb([128, 14, WM], F32, es=es)
            PV_ = k.sb([128, 14, WM], F32, es=es)
            lw, cum, E1, E2, E3, av, kk, tm1, tm2, bon, gT, Ys, yc, tq, km = [S4() for _ in range(15)]
            AR = k.sb([128, 4, WM // 64, 2, 64], F32, es=es)
            BK = k.sb([128, 4, WM // 64, 2, 64], F32, es=es)
            vT = k.sb([128, 4, 64 + WM], F32, es=es)
            k.op("dve", lambda e: e.memset(vT[:, :, 0:64], 0.0), writes=[vT])
            txw = k.sb([64, WM], BF16, es=es)
            xab = k.sb([128, WM], BF16, es=es)
            sxg = k.sb([128, WM], BF16, es=es)
            ybT = k.sb([128, 4, WM], BF16, es=es)
            NCH = WM // 64
            G1m = [k.sb([128, 8, 128], F32, es=es) for _ in range(NCH)]
            Pm = [k.sb([128, 4, 64], F32, es=es) for _ in range(NCH)]
            BKtm = [k.sb([128, 8, 64], F32, es=es) for _ in range(NCH)]
            UV = [k.sb([128, 8, 64], F32, es=es) for _ in range(NCH)]
            RVs = [k.sb([128, 4, 64], F32, es=es) for _ in range(NCH)]
            XsC = [[k.sb([128, 4, 64], F32, es=es) for _ in range(2)] for _ in range(2)]
            XtsC = [[k.sb([128, 4, 64], F32, es=es) for _ in range(2)] for _ in range(2)]
            Rs = k.sb([128, 4, 64], F32, es=es)
            Zt = k.sb([128, 4, 64], F32, es=es)
            tmz = k.sb([128, 4, 64], F32, es=es)
            Sio = k.sb([64, 8, 64], F32, es=es)
            sh0 = k.sb([128, 14], F32, es=es)
            shl = k.sb([128, 14], F32, es=es)

            def tt(e, out, a, b, op, reads, writes):
                k.op(e, lambda en: en.tensor_tensor(out=out, in0=a, in1=b, op=op), reads=reads, writes=writes)

            def actf(out, a, func, reads, writes, bias=None, scale=1.0):
                if bias is None:
                    k.op("act", lambda en: en.activation(out=out, in_=a, func=func, scale=scale), reads=reads, writes=writes)
                else:
                    k.op("act", lambda en: en.activation(out=out, in_=a, func=func, bias=bias, scale=scale), reads=reads, writes=writes)

            c4 = lambda ap: ap.rearrange("p h (c t) -> p h c t", t=64)
            f2 = lambda ap: ap.rearrange("p a t -> p (a t)")

            def do_tile(t0, W, seq_start, shift_src):
                nch = W // 64
                k.dma("sp", PB[:, :, :W], pbv[:, :, t0:t0 + W], PB, writes=[PB])
                if seq_start:
                    if W > 1:
                        k.dma("sp", PV_[:, :, 1:W], pbv[:, :, t0:t0 + W - 1], PV_, writes=[PV_])
                    if shift_src is None:
                        k.op("pool", lambda e: e.memset(PV_[:, :, 0:1], 0.0), writes=[PV_])
                    else:
                        k.dma("sp", sh0[:], shift_src, sh0, writes=[sh0])
                        k.op("pool", lambda e: e.tensor_copy(out=PV_[:, :, 0:1], in_=sh0[:].unsqueeze(2)), reads=[sh0], writes=[PV_])
                else:
                    k.dma("sp", PV_[:, :, :W], pbv[:, :, t0 - 1:t0 + W - 1], PV_, writes=[PV_])
                for cch in range(14):
                    k.op("act", lambda e: e.activation(out=PV_[:, cch, :W], in_=PV_[:, cch, :W], func=AF.Identity, scale=MU[:, cch:cch + 1]), reads=[PV_, self.pv], writes=[PV_])
                    k.op("dve", lambda e: e.scalar_tensor_tensor(out=PB[:, cch, :W], in0=PB[:, cch, :W], scalar=OMM[:, cch:cch + 1], in1=PV_[:, cch, :W], op0=ALU.mult, op1=ALU.add), reads=[PB, PV_, cst], writes=[PB])
                r_, kx, v_ = PB[:, 0:4, :W], PB[:, 4:8, :W], PB[:, 8:12, :W]
                actf(txw[0:64, :W], PB[0:64, 12, :W], AF.Tanh, [PB], [txw])
                tt("dve", kk[:, :, :W], kx, bc3(KK_, W), ALU.mult, [PB, self.pv], [kk])
                actf(xab[64:128, :W], PB[64:128, 12, :W], AF.Copy, [PB], [xab])
                actf(sxg[:, :W], PB[:, 13, :W], AF.Sigmoid, [PB], [sxg])
                actf(tq[:, :, :W], kk[:, :, :W], AF.Square, [kk], [tq])
                for hp in range(4):
                    pz, pa = ps[hp % 2], ps[2 + hp % 2]
                    k.op("pe", lambda e: e.matmul(pz[:, :W], wup[0:64, hp * 128:(hp + 1) * 128], txw[0:64, :W], start=True, stop=True), reads=[wup, txw], writes=[pz])
                    k.op("pe", lambda e: e.matmul(pa[:, :W], aup[64:128, hp * 128:(hp + 1) * 128], xab[64:128, :W], start=True, stop=True), reads=[aup, xab], writes=[pa])
                    actf(lw[:, hp, :W], pz[:, :W], AF.Sigmoid, [pz, self.pv], [lw], bias=W0[:, hp:hp + 1])
                    actf(av[:, hp, :W], pa[:, :W], AF.Sigmoid, [pa, self.pv], [av], bias=A0[:, hp:hp + 1])
                    k.op("dve", lambda e: e.tensor_tensor_scan(out=cum[:, hp, :W], data0=mask01[:, :W], data1=lw[:, hp, :W], initial=0.0, op0=ALU.mult, op1=ALU.add),
                         reads=[mask01, lw], writes=[cum])
                for hp in range(4):
                    pg, pn = ps[4 + hp % 2], ps[6 + hp % 2]
                    k.op("pe", lambda e: e.matmul(pn[:, :W], BLK, tq[:, hp, :W], start=True, stop=True), reads=[self.cmask, tq], writes=[pn])
                    k.op("pe", lambda e: e.matmul(pg[:, :W], gup[:, hp * 128:(hp + 1) * 128], sxg[:, :W], start=True, stop=True), reads=[gup, sxg], writes=[pg])
                    actf(tm2[:, hp, :W], pn[:, :W], AF.Sqrt, [pn, cst], [tm2], bias=cst[:, 0:1])
                    k.op("dve", lambda e: e.tensor_copy(out=gT[:, hp, :W], in_=pg[:, :W]), reads=[pg], writes=[gT])
                actf(E1[:, :, :W], cum[:, :, :W], AF.Exp, [cum], [E1], scale=-DECAY_C)
                tt("dve", tm1[:, :, :W], cum[:, :, :W], lw[:, :, :W], ALU.subtract, [cum, lw], [tm1])
                actf(E2[:, :, :W], cum[:, :, :W], AF.Exp, [cum], [E2], scale=DECAY_C)
                k.op("dve", lambda e: e.reciprocal(out=tm2[:, :, :W], in_=tm2[:, :, :W]), reads=[tm2], writes=[tm2])
                actf(E3[:, :, :W], tm1[:, :, :W], AF.Exp, [tm1], [E3], scale=-DECAY_C)
                tt("dve", kk[:, :, :W], kk[:, :, :W], tm2[:, :, :W], ALU.mult, [kk, tm2], [kk])
                for hp in range(4):
                    k.op("act", lambda e: e.activation(out=km[:, hp, :W], in_=av[:, hp, :W], func=AF.Identity, bias=OMK[:, hp:hp + 1], scale=KA[:, hp:hp + 1]), reads=[av, self.pv, cst], writes=[km])
                k.op("act", lambda e: e.copy(out=vT[:, :, 64:64 + W], in_=v_), reads=[PB], writes=[vT])
                tt("dve", km[:, :, :W], km[:, :, :W], kx, ALU.mult, [km, PB], [km])
                k.op("dve", lambda e: e.scalar_tensor_tensor(out=AR[:, :, 0:nch, 0, :], in0=c4(kk[:, :, :W]), scalar=-1.0, in1=c4(E3[:, :, :W]), op0=ALU.mult, op1=ALU.mult), reads=[kk, E3], writes=[AR])
                tt("dve", AR[:, :, 0:nch, 1, :], c4(r_), c4(E1[:, :, :W]), ALU.mult, [PB, E1], [AR])
                tt("dve", tm2[:, :, :W], kk[:, :, :W], av[:, :, :W], ALU.mult, [kk, av], [tm2])
                tt("dve", BK[:, :, 0:nch, 1, :], c4(km[:, :, :W]), c4(E2[:, :, :W]), ALU.mult, [km, E2], [BK])
                tt("dve", BK[:, :, 0:nch, 0, :], c4(tm2[:, :, :W]), c4(E2[:, :, :W]), ALU.mult, [tm2, E2], [BK])
                tt("pool", tq[:, :, :W], r_, km[:, :, :W], ALU.mult, [PB, km], [tq])
                tt("pool", tq[:, :, :W], tq[:, :, :W], bc3(RK, W), ALU.mult, [tq, self.pv], [tq])

                hv = lambda ap: ap.rearrange("p (a b) t -> p a b t", b=2)
                p3 = lambda ap, t: ap.rearrange("p (h t) -> p h t", t=t)
                HEADS = [(h, h // 2, h % 2, (h % 2) * 64) for h in range(8)]

                def gen_B(c, q, Xs, Xts):
                    g1, pm, bkt, uv, rv = G1m[c], Pm[c], BKtm[c], UV[c], RVs[c]
                    for (h, hp, hh, b0) in HEADS:
                        k.op("pe", lambda e: e.matmul(q[hh][:, hp * 128:(hp + 1) * 128], f2(BK[b0:b0 + 64, hp, c, :, :]), f2(AR[b0:b0 + 64, hp, c, :, :]), start=True, stop=True),
                             reads=[BK, AR], writes=[q[hh]])
                    yield
                    x_prev, xt_prev = Xs[0], Xts[0]
                    for hh in range(2):
                        b0 = hh * 64
                        tt("dve", hv(g1[:, :, :])[:, :, hh, :], p3(q[hh][:, :], 128), M1.unsqueeze(1).to_broadcast([128, 4, 128]), ALU.mult, [q[hh], self.cmask], [g1])
                        tt("dve", x_prev[b0:b0 + 64, :, :], p3(q[hh][0:64, :], 128)[:, :, 0:64], M1[0:64, 0:64].unsqueeze(1).to_broadcast([64, 4, 64]), ALU.mult, [q[hh], self.cmask], [x_prev])
                    for (h, hp, hh, b0) in HEADS:
                        k.op("pe", lambda e: e.matmul(q[hh][b0:b0 + 64, hp * 64:(hp + 1) * 64], AR[b0:b0 + 64, hp, c, 0, :], BK[b0:b0 + 64, hp, c, 0, :], start=True, stop=True), reads=[AR, BK], writes=[q[hh]])
                    yield
                    for hh in range(2):
                        b0 = hh * 64
                        tt("dve", xt_prev[b0:b0 + 64, :, :], p3(q[hh][b0:b0 + 64, 0:256], 64), self.cmask[b0:b0 + 64, 128:192].unsqueeze(1).to_broadcast([64, 4, 64]), ALU.mult, [q[hh], self.cmask], [xt_prev])
                        tt("pool", pm[b0:b0 + 64, :, :], x_prev[b0:b0 + 64, :, :], ID[b0:b0 + 64, b0:b0 + 64].unsqueeze(1).to_broadcast([64, 4, 64]), ALU.add, [x_prev, ID], [pm])
                    yield
                    for lev in range(1, 6):
                        xn, xtn = Xs[lev % 2], Xts[lev % 2]
                        for (h, hp, hh, b0) in HEADS:
                            k.op("pe", lambda e: e.matmul(q[hh][b0:b0 + 64, 256 + hp * 64:256 + (hp + 1) * 64], x_prev[b0:b0 + 64, hp, :], xt_prev[b0:b0 + 64, hp, :], start=True, stop=True), reads=[xt_prev, x_prev], writes=[q[hh]])
                        if lev < 5:
                            for (h, hp, hh, b0) in HEADS:
                                k.op("pe", lambda e: e.matmul(q[hh][b0:b0 + 64, hp * 64:(hp + 1) * 64], xt_prev[b0:b0 + 64, hp, :], x_prev[b0:b0 + 64, hp, :], start=True, stop=True), reads=[xt_prev, x_prev], writes=[q[hh]])
                        for hh in range(2):
                            b0 = hh * 64
                            k.op("dve", lambda e: e.tensor_copy(out=xtn[b0:b0 + 64, :, :], in_=p3(q[hh][b0:b0 + 64, 256:512], 64)), reads=[q[hh]], writes=[xtn])
                            if lev < 5:
                                k.op("act", lambda e: e.copy(out=xn[b0:b0 + 64, :, :], in_=p3(q[hh][b0:b0 + 64, 0:256], 64)), reads=[q[hh]], writes=[xn])
                        yield
                        for (h, hp, hh, b0) in HEADS:
                            k.op("pe", lambda e: e.matmul(q[hh][b0:b0 + 64, hp * 64:(hp + 1) * 64], xtn[b0:b0 + 64, hp, :], pm[b0:b0 + 64, hp, :], start=True, stop=True), reads=[xtn, pm], writes=[q[hh]])
                        for hh in range(2):
                            b0 = hh * 64
                            tt("dve", pm[b0:b0 + 64, :, :], p3(q[hh][b0:b0 + 64, 0:256], 64), pm[b0:b0 + 64, :, :], ALU.add, [q[hh], pm], [pm])
                        yield
                        x_prev, xt_prev = xn, xtn
                    for (h, hp, hh, b0) in HEADS:
                        k.op("pe", lambda e: e.transpose(q[hh][:, hp * 64:(hp + 1) * 64], f2(BK[b0:b0 + 64, hp, c, :, :]), ID[b0:b0 + 64, b0:b0 + 64]), reads=[BK, ID], writes=[q[hh]])
                    for (h, hp, hh, b0) in HEADS:
                        k.op("pe", lambda e: e.transpose(q[hh][:, 256 + hp * 64:256 + (hp + 1) * 64], vT[b0:b0 + 64, hp, c * 64:c * 64 + 128], ID[b0:b0 + 64, b0:b0 + 64]), reads=[vT, ID], writes=[q[hh]])
                    for hh in range(2):
                        k.op("act", lambda e: e.copy(out=hv(bkt[:, :, :])[:, :, hh, :], in_=p3(q[hh][:, 0:256], 64)), reads=[q[hh]], writes=[bkt])
                        k.op("dve", lambda e: e.tensor_copy(out=hv(uv[64:128, :, :])[:, :, hh, :], in_=p3(q[hh][64:128, 256:512], 64)), reads=[q[hh]], writes=[uv])
                    yield
                    for (h, hp, hh, b0) in HEADS:
                        k.op("pe", lambda e: e.matmul(q[hh][b0:b0 + 64, hp * 64:(hp + 1) * 64], g1[64:128, h, 0:64], uv[64:128, h, :], start=True, stop=True), reads=[g1, uv], writes=[q[hh]])
                    for hh in range(2):
                        b0 = hh * 64
                        k.op("act", lambda e: e.copy(out=rv[b0:b0 + 64, :, :], in_=p3(q[hh][b0:b0 + 64, 0:256], 64)), reads=[q[hh]], writes=[rv])
                    yield

                def gen_C(c):
                    cols = slice(c * 64, (c + 1) * 64)
                    g1, pm, bkt, uv, rv = G1m[c], Pm[c], BKtm[c], UV[c], RVs[c]
                    pRU = pYZ = (ps[0], ps[1])
                    for (h, hp, hh, b0) in HEADS:
                        k.op("pe", lambda e: e.matmul(pRU[hh][b0:b0 + 64, hp * 64:(hp + 1) * 64], AR[b0:b0 + 64, hp, c, 0, :], Zt[b0:b0 + 64, hp, :], start=True, stop=True), reads=[AR, Zt], writes=[pRU[hh]])
                    for hh in range(2):
                        b0 = hh * 64
                        tt("dve", Rs[b0:b0 + 64, :, :], p3(pRU[hh][b0:b0 + 64, 0:256], 64), rv[b0:b0 + 64, :, :], ALU.add, [pRU[hh], rv], [Rs])
                    yield
                    for (h, hp, hh, b0) in HEADS:
                        k.op("pe", lambda e: e.matmul(pRU[hh][0:64, 256 + hp * 64:256 + (hp + 1) * 64], pm[b0:b0 + 64, hp, :], Rs[b0:b0 + 64, hp, :], start=True, stop=True), reads=[pm, Rs], writes=[pRU[hh]])
                    k.op("act", lambda e: e.copy(out=hv(uv[0:64, :, :])[:, :, 0, :], in_=p3(pRU[0][0:64, 256:512], 64)), reads=[pRU[0]], writes=[uv])
                    k.op("dve", lambda e: e.tensor_copy(out=hv(uv[0:64, :, :])[:, :, 1, :], in_=p3(pRU[1][0:64, 256:512], 64)), reads=[pRU[1]], writes=[uv])
                    yield
                    for (h, hp, hh, b0) in HEADS:
                        k.op("pe", lambda e: e.matmul(pYZ[hh][b0:b0 + 64, hp * 64:(hp + 1) * 64], Zt[b0:b0 + 64, hp, :], AR[b0:b0 + 64, hp, c, 1, :], start=(hp == 0), stop=False, skip_group_check=True), reads=[Zt, AR], writes=[pYZ[hh]])
                    for (h, hp, hh, b0) in HEADS:
                        k.op("pe", lambda e: e.matmul(pYZ[hh][b0:b0 + 64, 256 + hp * 64:256 + (hp + 1) * 64], bkt[:, h, :], uv[:, h, :], start=False, stop=True, skip_group_check=True), reads=[bkt, uv], writes=[pYZ[hh]])
                    for (h, hp, hh, b0) in HEADS:
                        k.op("pe", lambda e: e.matmul(pYZ[hh][b0:b0 + 64, hp * 64:(hp + 1) * 64], uv[:, h, :], g1[:, h, 64:128], start=False, stop=True, skip_group_check=True), reads=[uv, g1], writes=[pYZ[hh]])
                    for hh in range(2):
                        b0 = hh * 64
                        tt("dve", tmz[b0:b0 + 64, :, :], p3(pYZ[hh][b0:b0 + 64, 256:512], 64), Zt[b0:b0 + 64, :, :], ALU.add, [pYZ[hh], Zt], [tmz])
                    tt("dve", Zt[:, :, :], tmz[:, :, :], E1[:, :, c * 64 + 63:c * 64 + 64].to_broadcast([128, 4, 64]), ALU.mult, [tmz, E1], [Zt])
                    for hh in range(2):
                        b0 = hh * 64
                        k.op("act", lambda e: e.copy(out=Ys[b0:b0 + 64, :, cols], in_=p3(pYZ[hh][b0:b0 + 64, 0:256], 64)), reads=[pYZ[hh]], writes=[Ys])
                    yield

                def run_interleaved(*gens):
                    gens = [g for g in gens if g is not None]
                    while gens:
                        for g in list(gens):
                            try:
                                next(g)
                            except StopIteration:
                                gens.remove(g)

                def chain(*gs):
                    for g in gs:
                        yield from g

                QA, QB = (ps[2], ps[3]), (ps[4], ps[5])
                gB = lambda c: gen_B(c, QA if c % 2 == 0 else QB, XsC[c % 2], XtsC[c % 2]) if c < nch else None
                run_interleaved(gB(0), gB(1))
                for c in range(0, nch, 2):
                    run_interleaved(chain(*[gen_C(cc) for cc in range(c, min(c + 2, nch))]), gB(c + 2), gB(c + 3))

                for hp in range(4):
                    pn = ps[6 + hp % 2]
                    k.op("pe", lambda e: e.matmul(pn[:, :W], BLK, tq[:, hp, :W], start=True, stop=True), reads=[self.cmask, tq], writes=[pn])
                    tt("dve", bon[:, hp, :W], pn[:, :W], PB[:, 8 + hp, :W], ALU.mult, [pn, PB], [bon])
                for hp in range(4):
                    pn = ps[6 + hp % 2]
                    k.op("pe", lambda e: e.matmul(pn[:, :W], BLK, Ys[:, hp, :W], start=True, stop=True), reads=[self.cmask, Ys], writes=[pn])
                    k.op("dve", lambda e: e.scalar_tensor_tensor(out=yc[:, hp, :W], in0=pn[:, :W], scalar=-1.0 / 64, in1=Ys[:, hp, :W], op0=ALU.mult, op1=ALU.add), reads=[pn, Ys], writes=[yc])
                actf(tm1[:, :, :W], yc[:, :, :W], AF.Square, [yc], [tm1])
                for hp in range(4):
                    pn = ps[6 + hp % 2]
                    k.op("pe", lambda e: e.matmul(pn[:, :W], BLK, tm1[:, hp, :W], start=True, stop=True), reads=[self.cmask, tm1], writes=[pn])
                    actf(tm2[:, hp, :W], pn[:, :W], AF.Sqrt, [pn, cst], [tm2], bias=cst[:, 1:2], scale=1.0 / 64)
                k.op("dve", lambda e: e.reciprocal(out=tm2[:, :, :W], in_=tm2[:, :, :W]), reads=[tm2], writes=[tm2])
                tt("dve", yc[:, :, :W], yc[:, :, :W], tm2[:, :, :W], ALU.mult, [yc, tm2], [yc])
                for hp in range(4):
                    k.op("act", lambda e: e.activation(out=yc[:, hp, :W], in_=yc[:, hp, :W], func=AF.Identity, bias=GB[:, hp:hp + 1], scale=GG[:, hp:hp + 1]), reads=[yc, self.pv], writes=[yc])
                tt("dve", yc[:, :, :W], yc[:, :, :W], bon[:, :, :W], ALU.add, [yc, bon], [yc])
                tt("dve", ybT[:, :, :W], yc[:, :, :W], gT[:, :, :W], ALU.mult, [yc, gT], [ybT])
                k.dma("sp", ybv[:, :, t0:t0 + W], ybT[:, :, :W], ybT, reads=[ybT])

            def finish_seq(wkv_dst, sh_dst, t_last):
                if cfg.rstop < 5:
                    return
                pT = ps[6]
                for hp in range(4):
                    k.op("pe", lambda e: e.transpose(pT[0:64, hp * 128:(hp + 1) * 128], Zt[:, hp, :], ID[:, :]), reads=[Zt, ID], writes=[pT])
                k.op("act", lambda e: e.copy(out=Sio[:, :, :], in_=pT[0:64, :].rearrange("p (h t) -> p h t", t=64)), reads=[pT], writes=[Sio])
                k.dma("sp", wkv_dst.rearrange("h v k -> v h k"), Sio[:, :, :], Sio, reads=[Sio])
                with nc.allow_non_contiguous_dma(reason="single token-shift row, 7 KiB"):
                    k.dma("sp", shl[:], pbv[:, :, t_last:t_last + 1].rearrange("p c o -> p (c o)"), shl, writes=[shl])
                    k.dma("sp", sh_dst.rearrange("(c p) -> p c", p=128), shl[:], shl, reads=[shl])

            k.op("dve", lambda e: e.memset(Zt[:], 0.0), writes=[Zt])
            for t0 in range(0, SEQ, WM):
                do_tile(t0, min(WM, SEQ - t0), t0 == 0, None)
            finish_seq(self.o_wkv_p[l], self.o_sh_p[l, 0], SEQ - 1)
            for s_ in range(NSS):
                t0 = SEQ + s_ * 64
                k.dma("sp", Sio[:, :, :], self.st_wkv[l, s_].rearrange("h v k -> v h k"), Sio, writes=[Sio])
                pT = ps[7]
                for hp in range(4):
                    k.op("pe", lambda e: e.transpose(pT[:, hp * 64:(hp + 1) * 64], Sio[:, hp * 2:hp * 2 + 2, :].rearrange("p h t -> p (h t)"), ID[0:64, 0:64]), reads=[Sio, ID], writes=[pT])
                k.op("act", lambda e: e.copy(out=Zt[:, :, :], in_=pT[:, 0:256].rearrange("p (h t) -> p h t", t=64)), reads=[pT], writes=[Zt])
                do_tile(t0, 64, True, self.st_sh[l, s_])
                finish_seq(self.o_wkv_s[l, s_], self.o_sh_s[l, s_, 0], t0 + 63)
            k.barrier()
            for b_ in [wup, aup, gup, PB, PV_, ybT, Sio, sh0, shl]:
                k.release_dsem(b_)

    def phase_xattn(self, l):
        k, cfg = self.k, self.cfg
        ps = self.ps
        SEQ, NSS = cfg.seq, cfg.nss
        WM = 512
        xTv = self.xT.rearrange("(c p) t -> p c t", p=128)
        ymT = self.ymixT.rearrange("(c p) t -> p c t", p=128)
        fm = lambda ap: ap.rearrange("(c p) n -> p c n", p=128)
        with ExitStack() as es:
            self.epsb = k.sb([128, 1], F32, es=es)
            k.op("dve", lambda e: e.memset(self.epsb[:], EPS), writes=[self.epsb])
            oneb = k.sb([128, 128], BF16, es=es)
            k.op("dve", lambda e: e.memset(oneb[:], 1.0), writes=[oneb])
            wout = k.sb([128, 8, D], BF16, es=es)
            wmq = k.sb([128, 8, D], BF16, es=es)
            wmo = k.sb([128, 8, D], BF16, es=es)
            for wt, src in ((wout, self.w_out), (wmq, self.w_mq), (wmo, self.w_mo)):
                for c0 in range(0, 8, 2):
                    k.dma("pool", wt[:, c0:c0 + 2, :], fm(src[l])[:, c0:c0 + 2, :], wt, writes=[wt])
            mkT = [k.sb([128, 8, N_MEM], BF16, es=es) for _ in range(1 + NSS)]
            mvs = [k.sb([128, 2, D], BF16, es=es) for _ in range(1 + NSS)]
            with ExitStack() as es2:
                wmk = k.sb([128, 8, D], BF16, es=es2)
                wmv = k.sb([128, 8, D], BF16, es=es2)
                for wt, src in ((wmk, self.w_mk), (wmv, self.w_mv)):
                    for c0 in range(0, 8, 2):
                        k.dma("pool", wt[:, c0:c0 + 2, :], fm(src[l])[:, c0:c0 + 2, :], wt, writes=[wt])
                mtm = k.sb([128, 2, D], F32, es=es2)
                memT = k.sb([128, 8, N_MEM], F32, es=es2)
                mn = k.sb([128, 8, N_MEM], BF16, n=8, es=es2)
                sq = k.sb([128, 8, N_MEM], BF16, n=8, es=es2)
                rstd = k.sb([128, N_MEM], F32, es=es2)
                otm = [k.sb([128, D], F32, es=es2) for _ in range(2)]
                k.dma("sp", mtm[:], self.mem_p.rearrange("(b p) d -> p b d", p=128), mtm, writes=[mtm])
                for c in range(8):
                    pb = ps[c % 4]
                    for mb in range(2):
                        k.op("pe", lambda e: e.transpose(pb[:, mb * 128:(mb + 1) * 128], mtm[:, mb, c * 128:(c + 1) * 128], self.ident[:]), reads=[mtm, self.ident], writes=[pb])
                    k.op("act", lambda e: e.copy(out=memT[:, c, :], in_=pb[:, 0:256]), reads=[pb], writes=[memT])
                XS = cfg.xstop
                if XS >= 2:
                    self.rmsnorm_fm(memT, mn, N_MEM, lambda c: self.pv[:, l, 32 + c:33 + c], sq, rstd)
                for j in range(8 if XS >= 3 else 0):
                    pb = ps[4 + j % 4]
                    for c in range(8):
                        k.op("pe", lambda e: e.matmul(pb[:, 0:256], wmk[:, c, j * 128:(j + 1) * 128], mn[:, c, :], start=(c == 0), stop=(c == 7)), reads=[wmk, mn.subs[c]], writes=[pb])
                    k.op("act", lambda e: e.copy(out=mkT[0][:, j, :], in_=pb[:, 0:256]), reads=[pb], writes=[mkT[0]])
                ntm = 0
                for wt, dst, keep in (((wmk, self.o_mk, False), (wmv, self.o_mv, True)) if XS >= 4 else ()):
                    for mb in range(2):
                        o_ = otm[ntm % 2]
                        ntm += 1
                        for hh in range(2):
                            pb = ps[(mb * 2 + hh) % 4]
                            for c in range(8):
                                k.op("pe", lambda e: e.matmul(pb[:, :], mn[:, c, mb * 128:(mb + 1) * 128], wt[:, c, hh * 512:(hh + 1) * 512], start=(c == 0), stop=(c == 7)), reads=[wt, mn.subs[c]], writes=[pb])
                            k.op("act", lambda e: e.copy(out=o_[:, hh * 512:(hh + 1) * 512], in_=pb[:, :]), reads=[pb], writes=[o_])
                            if keep:
                                k.op("dve", lambda e: e.tensor_copy(out=mvs[0][:, mb, hh * 512:(hh + 1) * 512], in_=o_[:, hh * 512:(hh + 1) * 512]), reads=[o_], writes=[mvs[0]])
                        k.dma("sp", dst[l, mb * 128:(mb + 1) * 128, :], o_[:], o_, reads=[o_])
                for s_ in range(NSS if "xsamp" not in cfg.skip else 0):
                    k.dma("sp", mtm[:], self.c_mk[l, s_].rearrange("(b p) d -> p b d", p=128), mtm, writes=[mtm])
                    k.dma("pool", mvs[1 + s_][:], self.c_mv[l, s_].rearrange("(b p) d -> p b d", p=128), mvs[1 + s_], writes=[mvs[1 + s_]])
                    for j in range(8):
                        pb = ps[j % 4]
                        for mb in range(2):
                            k.op("pe", lambda e: e.transpose(pb[:, mb * 128:(mb + 1) * 128], mtm[:, mb, j * 128:(j + 1) * 128], self.ident[:]), reads=[mtm, self.ident], writes=[pb])
                        if j % 2 == 0:
                            k.op("act", lambda e: e.copy(out=mkT[1 + s_][:, j, :], in_=pb[:, 0:256]), reads=[pb], writes=[mkT[1 + s_]])
                        else:
                            k.op("dve", lambda e: e.tensor_copy(out=mkT[1 + s_][:, j, :], in_=pb[:, 0:256]), reads=[pb], writes=[mkT[1 + s_]])
                k.barrier()
                for b_ in [wmk, wmv, mtm] + otm:
                    k.release_dsem(b_)
            xt = k.sb([128, 8, WM], F32, es=es)
            ym = k.sb([128, 8, WM], BF16, es=es)
            ht = k.sb([128, 8, WM], BF16, n=8, es=es)
            sq = k.sb([128, 8, WM], BF16, n=8, es=es)
            rstd = k.sb([128, WM], F32, es=es)
            qx = k.sb([128, 8, WM], BF16, n=8, es=es)
            ox = k.sb([128, 8, WM], BF16, n=8, es=es)
            E = [k.sb([128, WM], BF16, es=es) for _ in range(4)]
            rl = k.sb([128, WM], F32, es=es)
            for (t0, W) in (self.tiles(WM) if "xtok" not in cfg.skip else []):
                if t0 < SEQ:
                    segs = [(0, W, mkT[0], mvs[0])]
                else:
                    segs = [(s_ * 64, 64, mkT[1 + s_], mvs[1 + s_]) for s_ in range(NSS)]
                k.dma("sp", xt[:, :, :W], xTv[:, :, t0:t0 + W], xt, writes=[xt])
                k.dma("sp", ym[:, :, :W], ymT[:, :, t0:t0 + W], ym, writes=[ym])
                for m in range(8):
                    pb = ps[m % 2]
                    for c in range(8):
                        k.op("pe", lambda e: e.matmul(pb[:, :W], wout[:, c, m * 128:(m + 1) * 128], ym[:, c, :W], start=(c == 0), stop=(c == 7)), reads=[wout, ym], writes=[pb])
                    k.op("dve", lambda e: e.tensor_tensor(out=xt[:, m, :W], in0=pb[:, :W], in1=xt[:, m, :W], op=ALU.add), reads=[pb, xt], writes=[xt])
                self.rmsnorm_fm(xt, ht, W, lambda c: self.pv[:, l, 24 + c:25 + c], sq, rstd)
                for j in range(8):
                    pb = ps[2 + j % 2]
                    for c in range(8):
                        k.op("pe", lambda e: e.matmul(pb[:, :W], wmq[:, c, j * 128:(j + 1) * 128], ht[:, c, :W], start=(c == 0), stop=(c == 7)), reads=[wmq, ht.subs[c]], writes=[pb])
                    if j % 2 == 0:
                        k.op("act", lambda e: e.copy(out=qx[:, j, :W], in_=pb[:, :W]), reads=[pb], writes=[qx.subs[j]])
                    else:
                        k.op("dve", lambda e: e.tensor_copy(out=qx[:, j, :W], in_=pb[:, :W]), reads=[pb], writes=[qx.subs[j]])
                for (c0, n, mk_, mv_) in segs:
                    for h in range(4):
                        es_ = []
                        for mb in range(2):
                            st = ps[(h % 2) * 2 + mb]
                            e_ = E[(h % 2) * 2 + mb]
                            for dc in range(2):
                                k.op("pe", lambda e: e.matmul(st[:, :n], mk_[:, h * 2 + dc, mb * 128:(mb + 1) * 128], qx[:, h * 2 + dc, c0:c0 + n], start=(dc == 0), stop=(dc == 1)),
                                     reads=[mk_, qx.subs[h * 2 + dc]], writes=[st])
                            k.op("act", lambda e: e.activation(out=e_[:, :n], in_=st[:, :n], func=AF.Exp, scale=1.0 / 16.0), reads=[st], writes=[e_])
                            es_.append(e_)
                        pl = ps[4]
                        for mb in range(2):
                            k.op("pe", lambda e: e.matmul(pl[:, :n], oneb[:], es_[mb][:, :n], start=(mb == 0), stop=(mb == 1)), reads=[oneb, es_[mb]], writes=[pl])
                        k.op("dve", lambda e: e.reciprocal(out=rl[:, :n], in_=pl[:, :n]), reads=[pl], writes=[rl])
                        for dc in range(2):
                            po = ps[5 + dc]
                            for mb in range(2):
                                k.op("pe", lambda e: e.matmul(po[:, :n], mv_[:, mb, h * 256 + dc * 128:h * 256 + (dc + 1) * 128], es_[mb][:, :n], start=(mb == 0), stop=(mb == 1)), reads=[mv_, es_[mb]], writes=[po])
                            k.op("dve", lambda e: e.tensor_tensor(out=ox[:, h * 2 + dc, c0:c0 + n], in0=po[:, :n], in1=rl[:, :n], op=ALU.mult), reads=[po, rl], writes=[ox.subs[h * 2 + dc]])
                for m in range(8):
                    pb = ps[m % 2]
                    for j in range(8):
                        k.op("pe", lambda e: e.matmul(pb[:, :W], wmo[:, j, m * 128:(m + 1) * 128], ox[:, j, :W], start=(j == 0), stop=(j == 7)), reads=[wmo, ox.subs[j]], writes=[pb])
                    k.op("dve", lambda e: e.tensor_tensor(out=xt[:, m, :W], in0=pb[:, :W], in1=xt[:, m, :W], op=ALU.add), reads=[pb, xt], writes=[xt])
                k.dma("sp", xTv[:, :, t0:t0 + W], xt[:, :, :W], xt, reads=[xt])
            k.barrier()
            for b_ in [wout, wmq, wmo, xt, ym] + mvs:
                k.release_dsem(b_)

    def phase_out(self):
        k, cfg = self.k, self.cfg
        xTv = self.xT.rearrange("(c p) t -> p c t", p=128)
        W = 512
        with ExitStack() as es:
            self.epsb = k.sb([128, 1], F32, es=es)
            k.op("dve", lambda e: e.memset(self.epsb[:], EPS), writes=[self.epsb])
            xt = k.sb([128, 8, W], F32, es=es)
            sq = k.sb([128, 8, W], BF16, n=8, es=es)
            rstd = k.sb([128, W], F32, es=es)
            yt = k.sb([128, 8, W], F32, n=8, es=es)
            ob = [k.sb([128, D], F32, es=es) for _ in range(2)]
            nb = 0
            for (t0, Wt) in self.tiles(W):
                k.dma("sp", xt[:, :, :Wt], xTv[:, :, t0:t0 + Wt], xt, writes=[xt])
                self.rmsnorm_fm(xt, yt, Wt, lambda c: self.fing[:, c:c + 1], sq, rstd, gbuf=self.fing)
                for tb in range(Wt // 128):
                    o_ = ob[nb % 2]
                    for g in range(2):
                        pb = self.ps[4 + (nb * 2 + g) % 4]
                        for c in range(4):
                            cc = g * 4 + c
                            k.op("pe", lambda e: e.transpose(pb[:, c * 128:(c + 1) * 128], yt[:, cc, tb * 128:(tb + 1) * 128], self.ident[:]),
                                 reads=[yt.subs[cc], self.ident], writes=[pb])
                        if g == 0:
                            k.op("act", lambda e: e.copy(out=o_[:, 0:512], in_=pb[:]), reads=[pb], writes=[o_])
                        else:
                            k.op("dve", lambda e: e.tensor_copy(out=o_[:, 512:1024], in_=pb[:]), reads=[pb], writes=[o_])
                    tg = t0 + tb * 128
                    if tg < cfg.seq:
                        dst = self.y_p[tg:tg + 128, :]
                    else:
                        dst = self.y_s[tg - cfg.seq:tg - cfg.seq + 128, :]
                    k.dma("sp", dst, o_[:], o_, reads=[o_])
                    nb += 1
            k.barrier()
            for b_ in [xt] + ob:
                k.release_dsem(b_)


def host_consts():
    c = np.zeros((128, 512), np.float32)
    c[:, 0:128] = np.eye(128, dtype=np.float32)
    s_ = np.arange(128)[:, None] % 64
    t_ = np.arange(64)[None, :]
    c[:, 128:192] = (s_ < t_)
    c[:, 192:256] = (s_ <= t_)
    c[:, 256:320] = (s_ > t_)
    blk = (np.arange(128)[:, None] // 64) == (np.arange(128)[None, :] // 64)
    c[:, 384:512] = blk
    return c


def fm_vec(v):
    return np.ascontiguousarray(v.reshape(-1, 128).T)


def make_in_maps(cfg, inp, n_cores=8):
    L = cfg.depth
    pvec = np.zeros((L, 128, 128), np.float32)
    for l in range(L):
        pvec[l, :, 0:8] = fm_vec(inp["ffn1_norm"][l])
        pvec[l, :, 8:16] = fm_vec(inp["ffn2_norm"][l])
        pvec[l, :, 16:24] = fm_vec(inp["mix_norm"][l])
        pvec[l, :, 24:32] = fm_vec(inp["xattn_norm"][l])
        pvec[l, :, 32:40] = fm_vec(inp["mem_kv_norm"][l])
        pvec[l, :, 40:42] = fm_vec(inp["q_norm"][l])
        pvec[l, :, 42:43] = fm_vec(inp["kv_norm"][l])
        pvec[l, :, 48:62] = fm_vec(inp["shift_mu"][l])
        for j, n_ in enumerate(["w0", "a0", "k_k", "k_a", "r_k", "gn_gain", "gn_bias"]):
            pvec[l, :, 62 + 4 * j:66 + 4 * j] = fm_vec(inp[n_][l].reshape(-1))
    w_in = inp["w_in"][:L]
    pos = np.concatenate([np.arange(cfg.seq)] + [cfg.past + np.arange(64)] * cfg.nss).astype(np.float32)
    inv = (np.float32(10000.0) ** (-np.arange(16, dtype=np.float32) / np.float32(16))).astype(np.float32)
    ang = (pos[None, :] * inv[:, None]).astype(np.float32)
    cosf = np.tile(np.cos(ang).astype(np.float32), (8, 1))
    sinf = np.tile(np.sin(ang).astype(np.float32), (8, 1))
    shared = {
        "consts": host_consts(), "pvec": pvec, "fin_g": fm_vec(inp["final_norm"]),
        "w_gate1": inp["ffn1_w_gate"][:L], "w_up1": inp["ffn1_w_up"][:L], "w_down1": inp["ffn1_w_down"][:L],
        "w_gate2": inp["ffn2_w_gate"][:L], "w_up2": inp["ffn2_w_up"][:L], "w_down2": inp["ffn2_w_down"][:L],
        "w_cq": w_in[:, :, 0:256], "w_ckv": w_in[:, :, 256:384], "w_kr4": np.tile(w_in[:, :, 384:416], (1, 1, 4)),
        "w_pb": w_in[:, :, 416:], "w_uq": inp["w_uq"][:L],
        "w_ukT": inp["w_uk"][:L].transpose(0, 3, 1, 2), "w_uv": inp["w_uv"][:L].transpose(0, 2, 1, 3).reshape(L, 128, 512),
        "rope_cs": np.stack([cosf, sinf]),
        "w_wup": inp["w_up"][:L], "w_aup": inp["a_up"][:L], "w_gup": inp["g_up"][:L],
        "w_out": inp["w_out"][:L], "w_mq": inp["w_mq"][:L], "w_mk": inp["w_mk"][:L], "w_mv": inp["w_mv"][:L], "w_mo": inp["w_mo"][:L],
    }
    shared = {k_: np.ascontiguousarray(v, dtype=np.float32) for k_, v in shared.items()}
    maps = []
    nb = inp["x_prompt"].shape[0]
    for c in range(n_cores):
        m = dict(shared)
        b = c % nb
        m["x_p"] = np.ascontiguousarray(inp["x_prompt"][b, :cfg.seq])
        sl = slice(c * cfg.nss, (c + 1) * cfg.nss)
        m["x_s"] = np.ascontiguousarray(inp["x_sample"][sl].reshape(cfg.nss * 64, D))
        m["st_wkv"] = np.ascontiguousarray(inp["state_wkv"][:L, sl])
        m["st_sh"] = np.ascontiguousarray(inp["state_shift"][:L, sl, 0].reshape(L, cfg.nss, 14, 128).transpose(0, 1, 3, 2))
        m["mem_p"] = np.ascontiguousarray(inp["mem_prompt"][b])
        m["c_mk"] = np.ascontiguousarray(inp["cache_mem_k"][:L, sl].reshape(L, cfg.nss, N_MEM, D))
        m["c_mv"] = np.ascontiguousarray(inp["cache_mem_v"][:L, sl].reshape(L, cfg.nss, N_MEM, D))
        m["c_ckv"] = np.ascontiguousarray(inp["cache_ckv"][:L, sl])
        m["c_kr"] = np.ascontiguousarray(inp["cache_krope"][:L, sl])
        maps.append(m)
    return maps


def run(cfg, inp, n_cores=8):
    prog = Prog(cfg)
    nc = prog.build()
    maps = make_in_maps(cfg, inp, n_cores)
    res = run_bass_kernel_spmd(nc, maps, core_ids=list(range(n_cores)))
    return res.results


def assemble(cfg, res, n_cores=8, nb=4):
    L, nss = cfg.depth, cfg.nss
    g = lambda c, n, shp: res[c][n] if n in res[c] else np.zeros(shp, np.float32)
    P = range(nb)
    C = range(n_cores)
    y_p = np.stack([res[b]["y_p"] for b in P])
    y_s = np.concatenate([res[c]["y_s"].reshape(nss, 64, D) for c in C])
    ckv_p = np.stack([res[b]["o_ckv_p"] for b in P], axis=1)
    kr_p = np.stack([res[b]["o_kr_p"] for b in P], axis=1)
    mk_p = np.stack([res[b]["o_mk"].reshape(L, N_MEM, MEM_HEADS, MEM_HD) for b in P], axis=1)
    mv_p = np.stack([res[b]["o_mv"].reshape(L, N_MEM, MEM_HEADS, MEM_HD) for b in P], axis=1)
    wkv_p = np.stack([g(b, "o_wkv_p", (L, 8, 64, 64)) for b in P], axis=1)
    sh_p = np.stack([g(b, "o_sh_p", (L, 1, D_SHIFT)) for b in P], axis=1)
    ckv_s = np.concatenate([res[c]["o_ckv_s"].reshape(L, nss, 64, 128) for c in C], axis=1)
    kr_s = np.concatenate([res[c]["o_kr_s"].reshape(L, nss, 64, 32) for c in C], axis=1)
    wkv_s = np.concatenate([g(c, "o_wkv_s", (L, nss, 8, 64, 64)) for c in C], axis=1)
    sh_s = np.concatenate([g(c, "o_sh_s", (L, nss, 1, D_SHIFT)) for c in C], axis=1)
    outs = (y_p, y_s, ckv_p, kr_p, mk_p, mv_p, wkv_p, sh_p, ckv_s, kr_s, wkv_s, sh_s)
    return tuple(np.ascontiguousarray(o, dtype=np.float32) for o in outs)


def kernel(**inputs):
    cfg = Cfg()
    inp = {k_: np.asarray(v) for k_, v in inputs.items()}
    res = run(cfg, inp)
    return assemble(cfg, res)
```
